# Optimizing a Trainium2 kernel written in Bass

```python
import math
import jax, jax.numpy as jnp
from jax import lax
import numpy as np

D_MODEL = 1024
BATCH = 4
SEQ = 4096
DEPTH = 1
DEC_BATCH = 128
DEC_SEQ = 1
PAST_LEN = 8192
PAGE_SIZE = 128

N_Q_HEADS = 8
N_KV_HEADS = 2
HEAD_DIM = 64
Q_PER_KV = N_Q_HEADS // N_KV_HEADS
ATTN_WIDTH = N_Q_HEADS * HEAD_DIM
KV_WIDTH = N_KV_HEADS * HEAD_DIM
WINDOW = 128
ROPE_DIM = HEAD_DIM // 4
ROPE_THETA = 500000.0
SSM_WIDTH = D_MODEL // 2
SSM_GROUP = 16
N_SSM_GROUPS = SSM_WIDTH // SSM_GROUP
SSM_STATE = 64
GATE_WIDTH = 2 * D_MODEL
IN_WIDTH = ATTN_WIDTH + 2 * KV_WIDTH + SSM_WIDTH + GATE_WIDTH
D_FF = -(-8 * D_MODEL // (3 * 256)) * 256
NORM_EPS = 1e-5

kernel_name = "swa_sink_s5_gated_hybrid_step"


def rms_norm(x, g):
    xf = x.astype(jnp.float32)
    y = xf * lax.rsqrt(jnp.mean(xf * xf, axis=-1, keepdims=True) + NORM_EPS)
    return (y * g.astype(jnp.float32)).astype(x.dtype)


def rope_partial(x, pos):
    half = ROPE_DIM // 2
    inv_freq = ROPE_THETA ** (-(jnp.arange(half, dtype=jnp.float32) * 2.0 / ROPE_DIM))
    ang = pos.astype(jnp.float32)[:, None] * inv_freq[None, :]
    cos = jnp.cos(ang)[:, None, :]
    sin = jnp.sin(ang)[:, None, :]
    xr = x[..., :ROPE_DIM].astype(jnp.float32)
    x1, x2 = xr[..., :half], xr[..., half:]
    rot = jnp.concatenate([x1 * cos - x2 * sin, x2 * cos + x1 * sin], axis=-1)
    return jnp.concatenate([rot.astype(x.dtype), x[..., ROPE_DIM:]], axis=-1)


def sink_softmax(scores, mask, sinks):
    s = jnp.where(mask, scores, -jnp.inf)
    sk = jnp.broadcast_to(sinks.astype(jnp.float32)[:, :, None, None], s.shape[:-1] + (1,))
    p = jax.nn.softmax(jnp.concatenate([s, sk], axis=-1), axis=-1)
    return p[..., :-1]


def swa_prompt(q, k, v, sinks):
    B, T = q.shape[0], q.shape[1]
    nb = T // WINDOW
    qb = q.reshape(B, nb, WINDOW, N_KV_HEADS, Q_PER_KV, HEAD_DIM)
    kb = k.reshape(B, nb, WINDOW, N_KV_HEADS, HEAD_DIM)
    vb = v.reshape(B, nb, WINDOW, N_KV_HEADS, HEAD_DIM)
    kk = jnp.concatenate([jnp.concatenate([jnp.zeros_like(kb[:, :1]), kb[:, :-1]], axis=1), kb], axis=2)
    vv = jnp.concatenate([jnp.concatenate([jnp.zeros_like(vb[:, :1]), vb[:, :-1]], axis=1), vb], axis=2)
    scale = HEAD_DIM ** -0.5
    scores = jnp.einsum('bnqkgd,bnskd->bnkgqs', qb, kk, preferred_element_type=jnp.float32) * scale
    n = jnp.arange(nb)[:, None, None]
    i = jnp.arange(WINDOW)[None, :, None]
    j = jnp.arange(2 * WINDOW)[None, None, :]
    qpos = n * WINDOW + i
    kpos = (n - 1) * WINDOW + j
    mask = (kpos >= 0) & (kpos <= qpos) & (qpos - kpos < WINDOW)
    p = sink_softmax(scores, mask[None, :, None, None], sinks.reshape(N_KV_HEADS, Q_PER_KV))
    out = jnp.einsum('bnkgqs,bnskd->bnqkgd', p.astype(v.dtype), vv)
    wb = min(WINDOW, T)
    return out.reshape(B, T, ATTN_WIDTH), k[:, T - wb:], v[:, T - wb:]


def swa_sample(q, k, v, sinks, k_buf, v_buf):
    DB, S = q.shape[0], q.shape[1]
    wb = k_buf.shape[1]
    kk = jnp.concatenate([k_buf.astype(k.dtype), k], axis=1)
    vv = jnp.concatenate([v_buf.astype(v.dtype), v], axis=1)
    qg = q.reshape(DB, S, N_KV_HEADS, Q_PER_KV, HEAD_DIM)
    scale = HEAD_DIM ** -0.5
    scores = jnp.einsum('bqkgd,bskd->bkgqs', qg, kk, preferred_element_type=jnp.float32) * scale
    qpos = PAST_LEN + jnp.arange(S)[:, None]
    kpos = PAST_LEN - wb + jnp.arange(wb + S)[None, :]
    mask = (kpos <= qpos) & (qpos - kpos < WINDOW)
    p = sink_softmax(scores, mask, sinks.reshape(N_KV_HEADS, Q_PER_KV))
    out = jnp.einsum('bkgqs,bskd->bqkgd', p.astype(v.dtype), vv)
    return out.reshape(DB, S, ATTN_WIDTH), kk[:, S:], vv[:, S:]


def s5_discretize(lam_re, lam_im, log_dt, b_re, b_im):
    lam_re = lam_re.astype(jnp.float32)
    lam_im = lam_im.astype(jnp.float32)
    dt = jnp.exp(log_dt.astype(jnp.float32))[:, None]
    mag = jnp.exp(lam_re * dt)
    lb_re = mag * jnp.cos(lam_im * dt)
    lb_im = mag * jnp.sin(lam_im * dt)
    den = lam_re * lam_re + lam_im * lam_im
    nr = lb_re - 1.0
    c_re = ((nr * lam_re + lb_im * lam_im) / den)[..., None]
    c_im = ((lb_im * lam_re - nr * lam_im) / den)[..., None]
    b_re = b_re.astype(jnp.float32)
    b_im = b_im.astype(jnp.float32)
    bb_re = c_re * b_re - c_im * b_im
    bb_im = c_re * b_im + c_im * b_re
    return lb_re, lb_im, bb_re, bb_im


def _affine_combine(e1, e2):
    ar1, ai1, br1, bi1 = e1
    ar2, ai2, br2, bi2 = e2
    return (ar1 * ar2 - ai1 * ai2,
            ar1 * ai2 + ai1 * ar2,
            ar2 * br1 - ai2 * bi1 + br2,
            ar2 * bi1 + ai2 * br1 + bi2)


def s5_branch(u, h0_re, h0_im, lam_re, lam_im, log_dt, b_re, b_im, c_re, c_im, d, w_glu, b_glu):
    B, T = u.shape[0], u.shape[1]
    lb_re, lb_im, bb_re, bb_im = s5_discretize(lam_re, lam_im, log_dt, b_re, b_im)
    ug = u.reshape(B, T, N_SSM_GROUPS, SSM_GROUP).astype(jnp.float32)
    bu_re = jnp.einsum('gph,btgh->btgp', bb_re, ug)
    bu_im = jnp.einsum('gph,btgh->btgp', bb_im, ug)
    h0r = h0_re.astype(jnp.float32)
    h0i = h0_im.astype(jnp.float32)
    bu_re = bu_re.at[:, 0].add(lb_re * h0r - lb_im * h0i)
    bu_im = bu_im.at[:, 0].add(lb_re * h0i + lb_im * h0r)
    a_re = jnp.broadcast_to(lb_re, bu_re.shape)
    a_im = jnp.broadcast_to(lb_im, bu_im.shape)
    _, _, h_re, h_im = lax.associative_scan(_affine_combine, (a_re, a_im, bu_re, bu_im), axis=1)
    y = (jnp.einsum('ghp,btgp->btgh', c_re.astype(jnp.float32), h_re)
         - jnp.einsum('ghp,btgp->btgh', c_im.astype(jnp.float32), h_im))
    y = y.reshape(B, T, SSM_WIDTH) + d.astype(jnp.float32) * u.astype(jnp.float32)
    z = jax.nn.gelu(y).astype(u.dtype)
    out = z * jax.nn.sigmoid(z @ w_glu + b_glu)
    return out, h_re[:, -1].astype(u.dtype), h_im[:, -1].astype(u.dtype)


def hybrid_layer(x, pos, attend, h0_re, h0_im, norm1_g, w_in, b_gate, attn_sinks,
                 ssm_lam_re, ssm_lam_im, ssm_log_dt, ssm_b_re, ssm_b_im, ssm_c_re, ssm_c_im,
                 ssm_d, w_glu, b_glu, w_branch_attn, w_branch_ssm, w_out, norm2_g,
                 w_ffn_gate, w_ffn_up, w_ffn_down):
    B, T = x.shape[0], x.shape[1]
    h = rms_norm(x, norm1_g)
    proj = h @ w_in
    o1 = ATTN_WIDTH
    o2 = o1 + KV_WIDTH
    o3 = o2 + KV_WIDTH
    o4 = o3 + SSM_WIDTH
    q = rope_partial(proj[..., :o1].reshape(B, T, N_Q_HEADS, HEAD_DIM), pos)
    k = rope_partial(proj[..., o1:o2].reshape(B, T, N_KV_HEADS, HEAD_DIM), pos)
    v = proj[..., o2:o3].reshape(B, T, N_KV_HEADS, HEAD_DIM)
    u = proj[..., o3:o4]
    gates = proj[..., o4:] + b_gate
    attn_out, k_state, v_state = attend(q, k, v, attn_sinks)
    ssm_out, hr, hi = s5_branch(u, h0_re, h0_im, ssm_lam_re, ssm_lam_im, ssm_log_dt, ssm_b_re,
                                ssm_b_im, ssm_c_re, ssm_c_im, ssm_d, w_glu, b_glu)
    merged = (jax.nn.sigmoid(gates[..., :D_MODEL]) * (attn_out @ w_branch_attn)
              + jax.nn.sigmoid(gates[..., D_MODEL:]) * (ssm_out @ w_branch_ssm))
    x = x + merged @ w_out
    h2 = rms_norm(x, norm2_g)
    x = x + (jax.nn.silu(h2 @ w_ffn_gate) * (h2 @ w_ffn_up)) @ w_ffn_down
    return x, k_state, v_state, hr, hi


def setup_inputs(seed: int = 0) -> dict:
    key = jax.random.key(seed)
    ks = jax.random.split(key, 32)
    f32 = jnp.float32

    def nrm(k, shape, scale):
        return jax.random.normal(k, shape, f32) * scale

    wb = min(WINDOW, PAST_LEN)
    G, P, H = N_SSM_GROUPS, SSM_STATE, SSM_GROUP
    lam_im0 = math.pi * jnp.arange(P, dtype=f32)
    return {
        "x_prompt": nrm(ks[0], (BATCH, SEQ, D_MODEL), 1.0),
        "x_sample": nrm(ks[1], (DEC_BATCH, DEC_SEQ, D_MODEL), 1.0),
        "state_k_win": nrm(ks[2], (DEPTH, DEC_BATCH, wb, N_KV_HEADS, HEAD_DIM), 1.0),
        "state_v_win": nrm(ks[3], (DEPTH, DEC_BATCH, wb, N_KV_HEADS, HEAD_DIM), 1.0),
        "state_ssm_re": nrm(ks[4], (DEPTH, DEC_BATCH, G, P), 0.3),
        "state_ssm_im": nrm(ks[5], (DEPTH, DEC_BATCH, G, P), 0.3),
        "norm1_g": 1.0 + nrm(ks[6], (DEPTH, D_MODEL), 0.02),
        "w_in": nrm(ks[7], (DEPTH, D_MODEL, IN_WIDTH), D_MODEL ** -0.5),
        "b_gate": nrm(ks[8], (DEPTH, GATE_WIDTH), 0.02),
        "attn_sinks": nrm(ks[9], (DEPTH, N_Q_HEADS), 1.0),
        "ssm_lam_re": -0.5 + nrm(ks[10], (DEPTH, G, P), 0.01),
        "ssm_lam_im": lam_im0 + nrm(ks[11], (DEPTH, G, P), 0.01),
        "ssm_log_dt": jax.random.uniform(ks[12], (DEPTH, G), f32, math.log(1e-3), math.log(1e-1)),
        "ssm_b_re": nrm(ks[13], (DEPTH, G, P, H), (2 * H) ** -0.5),
        "ssm_b_im": nrm(ks[14], (DEPTH, G, P, H), (2 * H) ** -0.5),
        "ssm_c_re": nrm(ks[15], (DEPTH, G, H, P), P ** -0.5),
        "ssm_c_im": nrm(ks[16], (DEPTH, G, H, P), P ** -0.5),
        "ssm_d": nrm(ks[17], (DEPTH, SSM_WIDTH), 1.0),
        "w_glu": nrm(ks[18], (DEPTH, SSM_WIDTH, SSM_WIDTH), SSM_WIDTH ** -0.5),
        "b_glu": nrm(ks[19], (DEPTH, SSM_WIDTH), 0.02),
        "w_branch_attn": nrm(ks[20], (DEPTH, ATTN_WIDTH, D_MODEL), ATTN_WIDTH ** -0.5),
        "w_branch_ssm": nrm(ks[21], (DEPTH, SSM_WIDTH, D_MODEL), SSM_WIDTH ** -0.5),
        "w_out": nrm(ks[22], (DEPTH, D_MODEL, D_MODEL), D_MODEL ** -0.5),
        "norm2_g": 1.0 + nrm(ks[23], (DEPTH, D_MODEL), 0.02),
        "w_ffn_gate": nrm(ks[24], (DEPTH, D_MODEL, D_FF), D_MODEL ** -0.5),
        "w_ffn_up": nrm(ks[25], (DEPTH, D_MODEL, D_FF), D_MODEL ** -0.5),
        "w_ffn_down": nrm(ks[26], (DEPTH, D_FF, D_MODEL), D_FF ** -0.5),
        "norm_f_g": 1.0 + nrm(ks[27], (D_MODEL,), 0.02),
    }


def reference(x_prompt, x_sample, state_k_win, state_v_win, state_ssm_re, state_ssm_im,
              norm1_g, w_in, b_gate, attn_sinks, ssm_lam_re, ssm_lam_im, ssm_log_dt,
              ssm_b_re, ssm_b_im, ssm_c_re, ssm_c_im, ssm_d, w_glu, b_glu,
              w_branch_attn, w_branch_ssm, w_out, norm2_g, w_ffn_gate, w_ffn_up, w_ffn_down,
              norm_f_g):
    pos_p = jnp.arange(x_prompt.shape[1], dtype=jnp.int32)
    pos_s = PAST_LEN + jnp.arange(x_sample.shape[1], dtype=jnp.int32)
    zeros_h = jnp.zeros((x_prompt.shape[0], N_SSM_GROUPS, SSM_STATE), x_prompt.dtype)
    xp, xs = x_prompt, x_sample
    kp_l, vp_l, hrp_l, hip_l = [], [], [], []
    ks_l, vs_l, hrs_l, his_l = [], [], [], []
    for l in range(DEPTH):
        weights = (norm1_g[l], w_in[l], b_gate[l], attn_sinks[l], ssm_lam_re[l], ssm_lam_im[l],
                   ssm_log_dt[l], ssm_b_re[l], ssm_b_im[l], ssm_c_re[l], ssm_c_im[l], ssm_d[l],
                   w_glu[l], b_glu[l], w_branch_attn[l], w_branch_ssm[l], w_out[l], norm2_g[l],
                   w_ffn_gate[l], w_ffn_up[l], w_ffn_down[l])
        xp, kp, vp, hrp, hip = hybrid_layer(xp, pos_p, swa_prompt, zeros_h, zeros_h, *weights)
        kbuf, vbuf = state_k_win[l], state_v_win[l]
        attend_s = lambda q, k, v, s, kb=kbuf, vb=vbuf: swa_sample(q, k, v, s, kb, vb)
        xs, ksn, vsn, hrs, his = hybrid_layer(xs, pos_s, attend_s, state_ssm_re[l], state_ssm_im[l], *weights)
        kp_l.append(kp); vp_l.append(vp); hrp_l.append(hrp); hip_l.append(hip)
        ks_l.append(ksn); vs_l.append(vsn); hrs_l.append(hrs); his_l.append(his)
    y_prompt = rms_norm(xp, norm_f_g)
    y_sample = rms_norm(xs, norm_f_g)
    return (y_prompt, y_sample,
            jnp.stack(kp_l), jnp.stack(vp_l), jnp.stack(hrp_l), jnp.stack(hip_l),
            jnp.stack(ks_l), jnp.stack(vs_l), jnp.stack(hrs_l), jnp.stack(his_l))
```

```python
import contextlib
import math
import numpy as np
import ml_dtypes
import concourse.bass as bass
import concourse.mybir as mybir
from concourse.bass_utils import run_bass_kernel_spmd

F32 = mybir.dt.float32
BF16 = mybir.dt.bfloat16
AF = mybir.ActivationFunctionType
ALU = mybir.AluOpType

D = 1024
TOK = 2048
NS = 16
NCOL = TOK + NS
DFF = 2816
NFT = 22
EPS = 1e-5


class Buf:
    __slots__ = ("w", "r")

    def __init__(self):
        self.w = None
        self.r = []


class _Op:
    __slots__ = ("fn", "deps", "chan", "val", "ms", "alld", "seg", "cost", "lat", "idx", "eng")

    def __init__(self, fn, deps, chan):
        self.fn = fn
        self.deps = deps
        self.chan = chan
        self.val = 0
        self.ms = 0


LOOKAHEAD = 600


class _ProbeIns:
    def then_inc(self, *a, **k):
        return self


class _ProbeEng:
    def __init__(self):
        self.kind = None
        self.kw = None
        self.args = None

    def __getattr__(self, name):
        def f(*args, **kw):
            self.kind = name
            self.kw = kw
            self.args = args
            return _ProbeIns()
        return f


def _free(ap):
    n = 1
    for d in ap.shape[1:]:
        n *= int(d)
    return n


def _estimate(eng, fn):
    p = _ProbeEng()
    try:
        fn(p)
        kw, args, kind = p.kw, p.args, p.kind
        if kind == "dma_start":
            out = kw.get("out")
            nbytes = _free(out) * int(out.shape[0]) * (2 if "bfloat16" in str(out.dtype) else 4)
            if eng == "pool":
                nbytes *= 2
            return 0.06, 2.0 + nbytes / 200e3
        if kind == "matmul":
            n = _free(kw["rhs"])
            c = 0.045 + max(n, 64) / 1950.0
            return c, c + 0.1
        if kind == "transpose":
            return 0.1, 0.2
        out = kw.get("out", args[0] if args else None)
        n = _free(out)
        if kind == "tensor_tensor_scan":
            n *= 2
        if eng == "act":
            c = 0.19 + n / 1200.0
        else:
            c = 0.16 + n / 960.0
        return c, c + 0.05
    except Exception:
        return None


class Prog:
    ENG = ("pe", "act", "dve", "pool", "sp")

    def __init__(self):
        self.ops = {e: [] for e in self.ENG}
        self.chan_cnt = {}
        self.chan_last = {}
        self.seg = 0

    def op(self, eng, fn, reads=(), writes=(), chan=None, n=512, lat=None):
        idx = len(self.ops[eng])
        me = (eng, idx)
        alld = set()
        for b in reads:
            if b.w is not None:
                alld.add(b.w)
        for b in writes:
            if b.w is not None:
                alld.add(b.w)
            for r in b.r:
                alld.add(r)
        if chan is not None and chan in self.chan_last:
            alld.add(self.chan_last[chan])
        alld.discard(me)
        raw = {b.w for b in reads if b.w is not None}
        deps = set()
        for d in alld:
            dop = self.ops[d[0]][d[1]]
            same_compute = (d[0] == eng and dop.chan is None and chan is None)
            if same_compute and eng == "pe":
                continue
            deps.add(d)
        o = _Op(fn, deps, chan)
        o.alld = alld
        o.seg = self.seg
        o.eng = eng
        o.idx = idx
        est = _estimate(eng, fn) if fn is not None else (0.0, 0.0)
        if est is None:
            est = (0.06, 2.5) if chan is not None else (0.3, 0.3)
        o.cost, o.lat = est
        if chan is not None:
            self.chan_cnt[chan] = self.chan_cnt.get(chan, 0) + 16
            o.val = self.chan_cnt[chan]
            self.chan_last[chan] = me
        self.ops[eng].append(o)
        for b in reads:
            b.r.append(me)
        for b in writes:
            b.w = me
            b.r = []
        return me

    def fence(self):
        last = []
        for e in self.ENG:
            for i in range(len(self.ops[e]) - 1, -1, -1):
                if self.ops[e][i].chan is None and self.ops[e][i].fn is not None:
                    last.append((e, i))
                    break
        last += list(self.chan_last.values())
        self.seg += 1
        for e in self.ENG:
            idx = len(self.ops[e])
            o = _Op(None, {d for d in last if d != (e, idx)}, None)
            o.alld = set(); o.seg = self.seg; o.eng = e; o.idx = idx; o.cost = 0.0; o.lat = 0.0
            self.ops[e].append(o)
        self.seg += 1

    def schedule(self):
        order = {e: [] for e in self.ENG}
        fin = {}
        nseg = self.seg + 1
        segops = [{e: [] for e in self.ENG} for _ in range(nseg)]
        for e in self.ENG:
            for o in self.ops[e]:
                segops[o.seg][e].append(o)
        tbase = 0.0
        for sg in range(nseg):
            for e in self.ENG:
                for o in segops[sg][e]:
                    if o.fn is None:
                        nd = {d for d in o.deps if self.ops[d[0]][d[1]].chan is not None}
                        for e2 in self.ENG:
                            for i2 in reversed(order[e2]):
                                o2 = self.ops[e2][i2]
                                if o2.chan is None and o2.fn is not None:
                                    if (e2, i2) != (e, o.idx):
                                        nd.add((e2, i2))
                                    break
                        o.deps = nd
            pend = {e: list(segops[sg][e]) for e in self.ENG}
            efree = {e: tbase for e in self.ENG}
            total = sum(len(v) for v in pend.values())
            done = 0
            while done < total:
                best = None
                for e in self.ENG:
                    pl = pend[e]
                    for o in pl[:LOOKAHEAD]:
                        ok = True
                        rdy = efree[e]
                        for d in o.alld:
                            f = fin.get(d)
                            if f is None:
                                ok = False
                                break
                            if f > rdy:
                                rdy = f
                        if not ok:
                            continue
                        if best is None or rdy < best[0] - 1e-9:
                            best = (rdy, e, o)
                        break_early = (rdy <= efree[e] + 1e-9)
                        if break_early:
                            break
                assert best is not None, "scheduler deadlock"
                rdy, e, o = best
                pend[e].remove(o)
                cp = None; cpt = -1.0
                for d in o.alld:
                    if fin[d] > cpt:
                        cpt = fin[d]; cp = d
                if efree[e] >= cpt and order[e]:
                    cp = (e, order[e][-1])
                o.ms = 0
                self.crit = getattr(self, "crit", {})
                self.crit[(e, o.idx)] = (cp, rdy, rdy + o.lat)
                efree[e] = rdy + o.cost
                fin[(e, o.idx)] = rdy + o.lat
                order[e].append(o.idx)
                done += 1
            tbase = max([tbase] + [fin[(e, o.idx)] for e in self.ENG for o in segops[sg][e]])
            self.seg_end = getattr(self, 'seg_end', []) + [round(tbase, 1)]
        self.est_us = tbase
        return order

    def emit(self, nc):
        order = self.schedule()
        needed = {e: set() for e in self.ENG}
        for e in self.ENG:
            for o in self.ops[e]:
                for (de, di) in o.deps:
                    if self.ops[de][di].chan is None:
                        needed[de].add(di)
        for e in self.ENG:
            c = 0
            for i in order[e]:
                o = self.ops[e][i]
                if o.chan is None and i in needed[e]:
                    c += 1
                    o.ms = c
                    o.val = c
        with contextlib.ExitStack() as st:
            esem = {e: st.enter_context(nc.semaphore("s_" + e)) for e in self.ENG}
            csem = {c: st.enter_context(nc.semaphore("c_" + str(c))) for c in self.chan_cnt}
            block = st.enter_context(nc.Block())
            prog = self

            def run(engname, eng):
                waited = {}
                for i in order[engname]:
                    o = prog.ops[engname][i]
                    for (de, di) in sorted(o.deps):
                        d = prog.ops[de][di]
                        if d.chan is not None:
                            sem, key = csem[d.chan], ("c", d.chan)
                        else:
                            sem, key = esem[de], ("e", de)
                        if waited.get(key, 0) >= d.val:
                            continue
                        eng.wait_ge(sem, d.val)
                        waited[key] = d.val
                    if o.fn is None:
                        continue
                    ins = o.fn(eng)
                    if o.chan is not None:
                        ins.then_inc(csem[o.chan], 16)
                    elif o.ms:
                        ins.then_inc(esem[engname], 1)
                if engname == "sp":
                    for c, v in prog.chan_cnt.items():
                        if waited.get(("c", c), 0) < v:
                            eng.wait_ge(csem[c], v)

            block.tensor(lambda e: run("pe", e))
            block.scalar(lambda e: run("act", e))
            block.vector(lambda e: run("dve", e))
            block.gpsimd(lambda e: run("pool", e))
            block.sync(lambda e: run("sp", e))


class _Stop(Exception):
    pass


STOP = [None]
DUMPS = []


class TB:
    def __init__(self, t, b=None):
        self.t = t
        self.b = b if b is not None else Buf()

    def __getitem__(self, k):
        return self.t[k]


def build():
    nc = bass.Bass("TRN2", target_bir_lowering=False)
    P = Prog()

    def din(name, shape, dt=F32):
        return nc.dram_tensor(name, list(shape), dt, kind="ExternalInput").ap()

    def dout(name, shape, dt=F32):
        return nc.dram_tensor(name, list(shape), dt, kind="ExternalOutput").ap()

    xo = din("xo", [TOK, D]); xp = din("xp", [TOK, D]); xs = din("xs", [NS, D])
    kwin = din("kwin", [NS, 128, 128]); vwin = din("vwin", [NS, 128, 128])
    sre_l = din("sre_l", [128, 16, NS]); sim_l = din("sim_l", [128, 16, NS])
    mk_own = din("mk_own", [128, 512], BF16); mk_prev = din("mk_prev", [128, 512], BF16)
    mk_prev0 = din("mk_prev0", [128, 512], BF16)
    ropec = din("ropec", [128, 18, 8]); ropes = din("ropes", [128, 18, 8])
    g1b = din("g1b", [128, D]); g2b = din("g2b", [128, D]); gfb = din("gfb", [128, D])
    w_in = din("w_in", [D, 3328]); bgate_l = din("bgate_l", [128, 16]); sink_l = din("sink_l", [128, 4])
    lam_re_l = din("lam_re_l", [128, 16]); lam_im_l = din("lam_im_l", [128, 16]); logdt_l = din("logdt_l", [128, 16])
    bre_l = din("bre_l", [128, 16, 16]); bim_l = din("bim_l", [128, 16, 16])
    cre_l = din("cre_l", [128, 16, 16]); cim_l = din("cim_l", [128, 16, 16])
    d_l = din("d_l", [128, 4]); w_glu = din("w_glu", [512, 512]); bglu_l = din("bglu_l", [128, 4])
    w_ba = din("w_ba", [512, D]); w_bs = din("w_bs", [512, D]); w_out = din("w_out", [D, D])
    w_fg = din("w_fg", [D, DFF]); w_fu = din("w_fu", [D, DFF]); w_fd = din("w_fd", [DFF, D])
    ident_bf = din("ident_bf", [128, 128], BF16); ident_f = din("ident_f", [128, 128])
    mask_g2 = din("mask_g2", [128, 2]); mask_bd = din("mask_bd", [128, 128])

    y_o = dout("y", [TOK, D]); ys_o = dout("ys", [NS, D])
    kwp_o = dout("kwp", [128, 128]); vwp_o = dout("vwp", [128, 128])
    hre_o = dout("hre", [128, 16]); him_o = dout("him", [128, 16])
    kws_o = dout("kws", [NS, 128, 128]); vws_o = dout("vws", [NS, 128, 128])
    sres_o = dout("sres", [128, 16, NS]); sims_o = dout("sims", [128, 16, NS])

    with contextlib.ExitStack() as st:
        def sb(name, shape, dt=F32):
            return TB(st.enter_context(nc.sbuf_tensor(name, list(shape), dt)))

        def op(eng, fn, r=(), w=(), chan=None):
            return P.op(eng, fn, [x.b for x in r], [x.b for x in w], chan)

        def chk(k):
            if STOP[0] == k:
                raise _Stop()

        def dump(name, ap, shape, dt=F32):
            if STOP[0] is None:
                return
            d = nc.dram_tensor("dbg_" + name, list(shape), dt, kind="ExternalOutput").ap()
            DUMPS.append("dbg_" + name)
            P.fence()
            P.op("sp", lambda e: e.dma_start(out=d, in_=ap), [], [], chan="dbg_" + name)

        def body():
            PS = [TB(st.enter_context(nc.psum_tensor("ps%d" % i, [128, 512], F32))) for i in range(6)]
            PTB = [TB(st.enter_context(nc.psum_tensor("pt%d" % i, [128, 1024], BF16))) for i in range(2)]
            rr = {"ps": 0, "pt": 0, "ld": 0}

            def nps():
                rr["ps"] = (rr["ps"] + 1) % len(PS)
                return PS[rr["ps"]]

            def npt():
                rr["pt"] = (rr["pt"] + 1) % 2
                return PTB[rr["pt"]]

            def load(dst, src, r=(), q="sp"):
                rr["ld"] += 1
                ch = "ld%d" % (rr["ld"] % 8)
                op(q, lambda e: e.dma_start(out=dst[:], in_=src), r, [dst], chan=ch)

            identb = sb("identb", [128, 128], BF16); identf = sb("identf", [128, 128])
            mg2 = sb("mg2", [128, 2]); mbd = sb("mbd", [128, 128])
            mko = sb("mko", [128, 512], BF16); mkp = sb("mkp", [128, 512], BF16); mkp0 = sb("mkp0", [128, 512], BF16)
            rc_t = sb("rc_t", [128, 18, 8]); rs_t = sb("rs_t", [128, 18, 8])
            g1t = sb("g1t", [128, D]); g2t = sb("g2t", [128, D]); gft = sb("gft", [128, D])
            bgt = sb("bgt", [128, 16]); esk = sb("esk", [128, 4]); dlt = sb("dlt", [128, 4]); bglt = sb("bglt", [128, 4])
            ones_bf = sb("ones_bf", [128, 64], BF16)
            for dst, src in ((identb, ident_bf), (identf, ident_f), (mg2, mask_g2), (mbd, mask_bd), (mko, mk_own),
                             (mkp, mk_prev), (mkp0, mk_prev0), (rc_t, ropec), (rs_t, ropes), (g1t, g1b), (g2t, g2b),
                             (gft, gfb), (bgt, bgate_l), (esk, sink_l), (dlt, d_l), (bglt, bglu_l)):
                load(dst, src)
            op("dve", lambda e: e.memset(ones_bf[:], 1.0), [], [ones_bf])
            op("act", lambda e: e.activation(out=esk[:], in_=esk[:], func=AF.Exp), [esk], [esk])
            chk(-1)

            ssmT = sb("ssmT", [128, 4, NCOL], BF16)
            kT_all = sb("kT_all", [128, TOK + 128], BF16)
            v_all = sb("v_all", [128, 17, 128], BF16)
            wqk = sb("wqk", [128, 8, 768], BF16)
            op("pool", lambda e: e.dma_start(out=wqk[:, :, 0:512], in_=w_in[:, 0:512].rearrange("(k p) c -> p k c", p=128)), [], [wqk], chan="wq0")
            op("pool", lambda e: e.dma_start(out=wqk[:, :, 512:768], in_=w_in[:, 512:768].rearrange("(k p) c -> p k c", p=128)), [wqk], [wqk], chan="wq1")

            ARW = 38800
            arena = st.enter_context(nc.sbuf_tensor("arena", [128, ARW], F32))
            apos = {"o": 0}

            def carve(shape, dt=F32):
                n = int(np.prod(shape[1:]))
                words = n if dt == F32 else (n + 1) // 2
                o = apos["o"]
                apos["o"] = o + words
                assert apos["o"] <= ARW, apos["o"]
                v = arena[:, o:o + words]
                if dt != F32:
                    v = v.bitcast(dt)
                if len(shape) == 3:
                    v = v.rearrange("p (a b) -> p a b", b=shape[2])
                elif len(shape) == 4:
                    v = v.rearrange("p (a b c) -> p a b c", b=shape[2], c=shape[3])
                elif len(shape) == 5:
                    v = v.rearrange("p (a b c d) -> p a b c d", b=shape[2], c=shape[3], d=shape[4])
                return TB(v)

            WB = carve([128, 4, 8, 2, 128], BF16)
            WD = carve([128, 8, 16, 2, 32], BF16)
            KT = carve([128, 4, 8, 128], BF16)
            UTR = carve([128, 16, 64]); UTI = carve([128, 16, 64]); RT = carve([128, 16, 64])
            LBR = carve([128, 16]); LBI = carve([128, 16]); R8 = carve([128, 16])
            CARR = carve([128, 16]); CARI = carve([128, 16])
            wglu_t = carve([128, 4, 512], BF16)
            wu = carve([128, 8, 512], BF16)
            op("pool", lambda e: e.dma_start(out=wu[:], in_=w_in[:, 768:1280].rearrange("(k p) c -> p k c", p=128)), [], [wu], chan="wq2")
            op("pool", lambda e: e.dma_start(out=wglu_t[:], in_=w_glu.rearrange("(k p) c -> p k c", p=128)), [wu], [wglu_t], chan="wq2")
            apos_keep = apos["o"]

            C0 = Buf()

            def cvec(shape=(128, 16)):
                t = carve(list(shape)); t.b = C0
                return t

            def dv(fn):
                P.op("dve", fn, [C0], [C0])

            def av(fn):
                P.op("act", fn, [C0], [C0])

            def TT(o, a, b, o_):
                dv(lambda e: e.tensor_tensor(out=o, in0=a, in1=b, op=o_))

            lamr = cvec(); lami = cvec(); ldt = cvec()
            brel = cvec((128, 16, 16)); biml = cvec((128, 16, 16)); crel = cvec((128, 16, 16)); ciml = cvec((128, 16, 16))
            for dst, src in ((lamr, lam_re_l), (lami, lam_im_l), (ldt, logdt_l), (brel, bre_l), (biml, bim_l), (crel, cre_l), (ciml, cim_l)):
                load(dst, src)
            chk(-0.9)
            dtv = cvec(); are = cvec(); aim = cvec(); mag = cvec(); cr = cvec(); ci = cvec(); t1 = cvec(); t2 = cvec(); t3 = cvec()
            av(lambda e: e.activation(out=dtv[:], in_=ldt[:], func=AF.Exp))
            TT(are[:], lamr[:], dtv[:], ALU.mult)
            TT(aim[:], lami[:], dtv[:], ALU.mult)
            av(lambda e: e.activation(out=mag[:], in_=are[:], func=AF.Exp))
            av(lambda e: e.activation(out=R8[:], in_=are[:], func=AF.Exp, scale=8.0))
            av(lambda e: e.activation(out=ci[:], in_=aim[:], func=AF.Sin, scale=1.0 / 16))
            av(lambda e: e.activation(out=t1[:], in_=aim[:], func=AF.Sin, scale=1.0 / 32))
            TT(t1[:], t1[:], t1[:], ALU.mult)
            dv(lambda e: e.tensor_scalar(out=cr[:], in0=t1[:], scalar1=-2.0, scalar2=1.0, op0=ALU.mult, op1=ALU.add))
            for _ in range(4):
                TT(t1[:], cr[:], cr[:], ALU.mult)
                TT(t2[:], ci[:], ci[:], ALU.mult)
                TT(t3[:], cr[:], ci[:], ALU.mult)
                TT(cr[:], t1[:], t2[:], ALU.subtract)
                dv(lambda e: e.tensor_scalar(out=ci[:], in0=t3[:], scalar1=2.0, scalar2=None, op0=ALU.mult))
            TT(LBR[:], mag[:], cr[:], ALU.mult)
            TT(LBI[:], mag[:], ci[:], ALU.mult)
            chk(-0.8)
            den = cvec(); nr = cvec(); cfr = cvec(); cfi = cvec()
            TT(t1[:], lamr[:], lamr[:], ALU.mult)
            TT(t2[:], lami[:], lami[:], ALU.mult)
            TT(den[:], t1[:], t2[:], ALU.add)
            dv(lambda e: e.reciprocal(out=den[:], in_=den[:]))
            dv(lambda e: e.tensor_scalar(out=nr[:], in0=LBR[:], scalar1=-1.0, scalar2=None, op0=ALU.add))
            TT(t1[:], nr[:], lamr[:], ALU.mult)
            TT(t2[:], LBI[:], lami[:], ALU.mult)
            TT(t1[:], t1[:], t2[:], ALU.add)
            TT(cfr[:], t1[:], den[:], ALU.mult)
            TT(t1[:], LBI[:], lamr[:], ALU.mult)
            TT(t2[:], nr[:], lami[:], ALU.mult)
            TT(t1[:], t1[:], t2[:], ALU.subtract)
            TT(cfi[:], t1[:], den[:], ALU.mult)
            bbR = cvec((128, 16, 16)); bbI = cvec((128, 16, 16)); u1 = cvec((128, 16, 16)); u2 = cvec((128, 16, 16))

            def bc_h(v):
                return v[:].unsqueeze(2).broadcast_to([128, 16, 16])

            TT(u1[:], brel[:], bc_h(cfr), ALU.mult); TT(u2[:], biml[:], bc_h(cfi), ALU.mult); TT(bbR[:], u1[:], u2[:], ALU.subtract)
            TT(u1[:], biml[:], bc_h(cfr), ALU.mult); TT(u2[:], brel[:], bc_h(cfi), ALU.mult); TT(bbI[:], u1[:], u2[:], ALU.add)
            LPR = cvec((128, 9, 16)); LPI = cvec((128, 9, 16))
            dv(lambda e: e.memset(LPR[:, 0, :], 1.0)); dv(lambda e: e.memset(LPI[:, 0, :], 0.0))
            for k in range(8):
                TT(t1[:], LPR[:, k, :], LBR[:], ALU.mult); TT(t2[:], LPI[:, k, :], LBI[:], ALU.mult)
                TT(LPR[:, k + 1, :], t1[:], t2[:], ALU.subtract)
                TT(t1[:], LPR[:, k, :], LBI[:], ALU.mult); TT(t2[:], LPI[:, k, :], LBR[:], ALU.mult)
                TT(LPI[:, k + 1, :], t1[:], t2[:], ALU.add)
            dv(lambda e: e.reciprocal(out=t3[:], in_=R8[:]))
            def mq(v):
                return v.rearrange("p (q m) -> p m q", m=4)
            TT(UTR[:, :, 0].rearrange("p (m q) -> p m q", q=4), mq(LPR[:, 8, :]), mq(t3[:]), ALU.mult)
            TT(UTI[:, :, 0].rearrange("p (m q) -> p m q", q=4), mq(LPI[:, 8, :]), mq(t3[:]), ALU.mult)
            w1 = cvec((128, 16, 32)); w2 = cvec((128, 16, 32))
            n = 1
            while n < 64:
                def bc_n(T_, n=n):
                    return T_[:, :, n - 1:n].broadcast_to([128, 16, n])
                TT(w1[:, :, 0:n], UTR[:, :, 0:n], bc_n(UTR), ALU.mult); TT(w2[:, :, 0:n], UTI[:, :, 0:n], bc_n(UTI), ALU.mult)
                TT(UTR[:, :, n:2 * n], w1[:, :, 0:n], w2[:, :, 0:n], ALU.subtract)
                TT(w1[:, :, 0:n], UTR[:, :, 0:n], bc_n(UTI), ALU.mult); TT(w2[:, :, 0:n], UTI[:, :, 0:n], bc_n(UTR), ALU.mult)
                TT(UTI[:, :, n:2 * n], w1[:, :, 0:n], w2[:, :, 0:n], ALU.add)
                n *= 2
            dv(lambda e: e.tensor_copy(out=RT[:].rearrange("p (m q) c -> p m q c", q=4), in_=mq(R8[:]).unsqueeze(3).broadcast_to([128, 4, 4, 64])))
            dv(lambda e: e.memset(RT[:, :, 0:1], 0.0))
            dv(lambda e: e.memset(CARR[:], 0.0)); dv(lambda e: e.memset(CARI[:], 0.0))
            chk(-0.7)
            BPR = cvec((128, 8, 16, 16)); BPI = cvec((128, 8, 16, 16)); X1 = cvec((128, 8, 16, 16)); X2 = cvec((128, 8, 16, 16))

            def bc_k(v):
                return v[:].unsqueeze(1).broadcast_to([128, 8, 16, 16])

            def bc_p(v, lo):
                return v[:, lo:lo + 8, :].unsqueeze(3).broadcast_to([128, 8, 16, 16])

            TT(X1[:], bc_k(bbR), bc_p(LPR, 0), ALU.mult); TT(X2[:], bc_k(bbI), bc_p(LPI, 0), ALU.mult); TT(BPR[:], X1[:], X2[:], ALU.subtract)
            TT(X1[:], bc_k(bbI), bc_p(LPR, 0), ALU.mult); TT(X2[:], bc_k(bbR), bc_p(LPI, 0), ALU.mult); TT(BPI[:], X1[:], X2[:], ALU.add)
            chk(-0.6)
            EBR = cvec((128, 128, 2, 16)); EBI = cvec((128, 128, 2, 16))
            mg2b = mg2[:].unsqueeze(1).unsqueeze(3).broadcast_to([128, 128, 2, 16])
            TT(EBR[:], BPR[:].rearrange("p k s h -> p (k s) h").unsqueeze(2).broadcast_to([128, 128, 2, 16]), mg2b, ALU.mult)
            TT(EBI[:], BPI[:].rearrange("p k s h -> p (k s) h").unsqueeze(2).broadcast_to([128, 128, 2, 16]), mg2b, ALU.mult)
            EBRv = EBR[:].rearrange("p (k q m) g h -> p k q (m g h)", k=8, q=4)
            EBIv = EBI[:].rearrange("p (k q m) g h -> p k q (m g h)", k=8, q=4)
            CER = cvec((128, 16, 2, 16)); CEIN = cvec((128, 16, 2, 16))
            mg2c = mg2[:].unsqueeze(1).unsqueeze(3).broadcast_to([128, 16, 2, 16])
            TT(CER[:], crel[:].unsqueeze(2).broadcast_to([128, 16, 2, 16]), mg2c, ALU.mult)
            TT(CEIN[:], ciml[:].unsqueeze(2).broadcast_to([128, 16, 2, 16]), mg2c, ALU.mult)
            dv(lambda e: e.tensor_scalar(out=CEIN[:], in0=CEIN[:], scalar1=-1.0, scalar2=None, op0=ALU.mult))
            CERv = CER[:].rearrange("p (q m) g h -> p q (m g h)", q=4)
            CEINv = CEIN[:].rearrange("p (q m) g h -> p q (m g h)", q=4)
            CB = TB(None, C0)
            chk(-0.5)
            for q in range(4):
                for s in range(0, 8, 2):
                    ps = nps()
                    for j in range(2):
                        for ri, EV in enumerate((EBRv, EBIv)):
                            src = EV[:, 7 - (s + j), q, :]
                            dstp = ps[:, (j * 2 + ri) * 128:(j * 2 + ri + 1) * 128]
                            op("pe", lambda e, src=src, dstp=dstp: e.transpose(out=dstp, in_=src, identity=identf[:]), [CB, identf], [ps])
                    op("act", lambda e, ps=ps, q=q, s=s: e.activation(out=WB[:, q, s:s + 2, :, :].rearrange("p a b c -> p (a b c)"), in_=ps[:], func=AF.Copy), [ps], [WB])
            chk(-0.4)
            tmpk = cvec((128, 128))
            for q in range(4):
                for dl in range(8):
                    ps = nps()
                    op("pe", lambda e, ps=ps, q=q, dl=dl: e.matmul(ps[:, 0:128], lhsT=EBRv[:, dl, q, :], rhs=CERv[:, q, :], start=True, stop=False), [CB], [ps])
                    op("pe", lambda e, ps=ps, q=q, dl=dl: e.matmul(ps[:, 0:128], lhsT=EBIv[:, dl, q, :], rhs=CEINv[:, q, :], start=False, stop=True), [CB], [ps])
                    if dl == 0:
                        op("dve", lambda e, ps=ps, tmpk=tmpk: e.tensor_tensor(out=tmpk[:], in0=ps[:, 0:128], in1=mbd[:], op=ALU.mult), [ps, mbd, CB], [CB])
                        op("dve", lambda e, q=q, tmpk=tmpk: e.scalar_tensor_tensor(out=KT[:, q, 0, :], in0=identf[:], scalar=dlt[:, q:q + 1], in1=tmpk[:], op0=ALU.mult, op1=ALU.add), [CB, identf, dlt], [KT])
                    else:
                        op("dve", lambda e, ps=ps, q=q, dl=dl: e.tensor_tensor(out=KT[:, q, dl, :], in0=ps[:, 0:128], in1=mbd[:], op=ALU.mult), [ps, mbd], [KT])
            chk(-0.3)
            TT(X1[:], bc_k(crel), bc_p(LPR, 1), ALU.mult); TT(X2[:], bc_k(ciml), bc_p(LPI, 1), ALU.mult); TT(BPR[:], X1[:], X2[:], ALU.subtract)
            TT(X1[:], bc_k(crel), bc_p(LPI, 1), ALU.mult); TT(X2[:], bc_k(ciml), bc_p(LPR, 1), ALU.mult); TT(BPI[:], X1[:], X2[:], ALU.add)
            dv(lambda e: e.tensor_scalar(out=BPI[:], in0=BPI[:], scalar1=-1.0, scalar2=None, op0=ALU.mult))
            for ri, BPx in enumerate((BPR, BPI)):
                P.op("dve", lambda e, ri=ri, BPx=BPx: e.tensor_tensor(
                    out=WD[:, :, :, ri, :].rearrange("p t s (g h) -> p (t s) g h", g=2),
                    in0=BPx[:].rearrange("p k s h -> p (k s) h").unsqueeze(2).broadcast_to([128, 128, 2, 16]),
                    in1=mg2b, op=ALU.mult), [C0, mg2.b], [WD.b])

            P.fence()
            chk(1)
            apos["o"] = apos_keep

            def rmsnorm_to_T(xt, rows, gt, dstT, col0, tmp_bf, ss, junk):
                op("dve", lambda e: e.memset(ss[0:rows, :], 0.0), [], [ss])
                op("act", lambda e: e.activation(out=junk[0:rows, :], in_=xt[0:rows, :], func=AF.Square, accum_out=ss[0:rows, :]), [xt, ss], [junk, ss])
                op("act", lambda e: e.activation(out=ss[0:rows, :], in_=ss[0:rows, :], func=AF.Sqrt, scale=1.0 / D, bias=EPS), [ss], [ss])
                op("dve", lambda e: e.reciprocal(out=ss[0:rows, :], in_=ss[0:rows, :]), [ss], [ss])
                op("dve", lambda e: e.scalar_tensor_tensor(out=tmp_bf[0:rows, :], in0=xt[0:rows, :], scalar=ss[0:rows, 0:1], in1=gt[0:rows, :], op0=ALU.mult, op1=ALU.mult), [xt, ss, gt], [tmp_bf])
                pt = npt()
                for kc in range(8):
                    op("pe", lambda e, kc=kc: e.transpose(out=pt[:, kc * 128:kc * 128 + rows], in_=tmp_bf[0:rows, kc * 128:(kc + 1) * 128], identity=identb[0:rows, 0:rows]), [tmp_bf, identb], [pt])
                op("act", lambda e: e.activation(out=dstT[:, 0:8, col0:col0 + rows], in_=pt[:].rearrange("p (k t) -> p k t", t=128)[:, :, 0:rows], func=AF.Copy), [pt], [dstT])
                return ss

            WRING = [None] * 4
            wr_i = {"i": 0}

            SCR = {}

            def scratch(key, n):
                if key not in SCR:
                    SCR[key] = (nc.dram_tensor("wscr_" + key, [128, n], BF16).ap(), TB(None))
                return SCR[key]

            def wload(src_ap, nk, ncols, key=None, ti=0):
                wr_i["i"] = (wr_i["i"] + 1) % 4
                slot = WRING[wr_i["i"]]
                flat = slot[:, 0:nk * ncols]
                view = flat.rearrange("p (k c) -> p k c", c=ncols)
                i = wr_i["i"]
                if key is None:
                    op("pool", lambda e: e.dma_start(out=view, in_=src_ap.rearrange("(k p) c -> p k c", p=128)), [], [slot], chan="wr%d" % i)
                    return TB(view, slot.b)
                scr, sb_ = scratch(key, nk * ncols)
                op("pool", lambda e: e.dma_start(out=flat, in_=scr), [sb_], [slot], chan="wr%d" % i)
                return TB(view, slot.b)

            def precast(src_ap, nk, ncols, key, off=0, total=None):
                scr, sb_ = scratch(key, total if total is not None else nk * ncols)
                dst = scr[:, off:off + nk * ncols].rearrange("p (k c) -> p k c", c=ncols)
                op("pool", lambda e: e.dma_start(out=dst, in_=src_ap.rearrange("(k p) c -> p k c", p=128)), [sb_], [sb_], chan="wcast")

            xst = [carve([128, D]) for _ in range(2)]
            xnb = [carve([128, D], BF16) for _ in range(2)]
            junk = carve([128, D], BF16); ssv = [carve([128, 1]) for _ in range(2)]
            xnT = [carve([128, 8, 528], BF16) for _ in range(2)]
            uT = [carve([128, 4, 528], BF16) for _ in range(2)]
            Hp = [carve([128, 2, 16, 65], BF16)] * 2
            mt = [carve([128, 256]) for _ in range(4)]
            bRm = [carve([128, 256]) for _ in range(4)]; bIm = [carve([128, 256]) for _ in range(4)]
            GR = carve([128, 256]); GI = carve([128, 256])
            yT = carve([128, 4, 528]); zT = yT; zTb = carve([128, 4, 528], BF16); sg = carve([128, 528])
            small = [carve([128, 16]) for _ in range(8)]
            h0r = carve([128, 16, NS]); h0i = carve([128, 16, NS]); h0b = carve([128, 2, 16, NS], BF16)
            hnr = carve([128, 16, NS]); hni = carve([128, 16, NS]); s1 = carve([128, 16, NS]); s2 = carve([128, 16, NS])
            kvf = [carve([128, 256]) for _ in range(2)]
            qk_tok = [carve([128, 640], BF16) for _ in range(2)]
            rtmp = [carve([128, 10, 8]) for _ in range(4)]
            chk(0.4)
            load(h0r, sre_l); load(h0i, sim_l)
            chk(0.5)
            op("dve", lambda e: e.tensor_copy(out=h0b[:, 0, :, :], in_=h0r[:]), [h0r], [h0b])
            op("dve", lambda e: e.tensor_copy(out=h0b[:, 1, :, :], in_=h0i[:]), [h0i], [h0b])
            cnt = {"x": 0, "t": 0}
            hpB = [Buf() for _ in range(4)]

            def rope_kv(ps_kv, rows, ridx, kv_f):
                op("act", lambda e: e.activation(out=kv_f[0:rows, :], in_=ps_kv[0:rows, 0:256], func=AF.Copy), [ps_kv], [kv_f])
                oview = kv_f[0:rows, 0:128].rearrange("p (h d) -> p h d", d=64)
                rope_apply(oview, oview, rows, ridx, 2, [kv_f], kv_f)

            def rope_apply(src, dst, rows, ridx, nh, rbufs, dstb):
                cosb = rc_t[0:rows, ridx:ridx + 1, :].broadcast_to([rows, nh, 8])
                sinb = rs_t[0:rows, ridx:ridx + 1, :].broadcast_to([rows, nh, 8])
                a, b, c, d = rtmp
                x1 = src[:, :, 0:8]; x2 = src[:, :, 8:16]
                op("dve", lambda e: e.tensor_tensor(out=a[0:rows, 0:nh, :], in0=x1, in1=cosb, op=ALU.mult), rbufs + [rc_t], [a])
                op("dve", lambda e: e.tensor_tensor(out=b[0:rows, 0:nh, :], in0=x2, in1=sinb, op=ALU.mult), rbufs + [rs_t], [b])
                op("dve", lambda e: e.tensor_tensor(out=c[0:rows, 0:nh, :], in0=x2, in1=cosb, op=ALU.mult), rbufs + [rc_t], [c])
                op("dve", lambda e: e.tensor_tensor(out=d[0:rows, 0:nh, :], in0=x1, in1=sinb, op=ALU.mult), rbufs + [rs_t], [d])
                op("dve", lambda e: e.tensor_tensor(out=dst[:, :, 0:8], in0=a[0:rows, 0:nh, :], in1=b[0:rows, 0:nh, :], op=ALU.subtract), [a, b], [dstb])
                op("dve", lambda e: e.tensor_tensor(out=dst[:, :, 8:16], in0=c[0:rows, 0:nh, :], in1=d[0:rows, 0:nh, :], op=ALU.add), [c, d], [dstb])

            def ssm_tile(src_dram, ntok_sub, is_own, tile_i, sample=False):
                cnt["t"] += 1
                xT = xnT[cnt["t"] % 2]; u_t = uT[cnt["t"] % 2]; hp = Hp[cnt["t"] % 2]
                ncols = 16 if sample else 512
                subs = [(0, 16)] if sample else [(j * 128, 128) for j in range(4)]
                for (c0, rows) in subs:
                    cnt["x"] += 1
                    xt = xst[cnt["x"] % 2]; tb = xnb[cnt["x"] % 2]; ss = ssv[cnt["x"] % 2]
                    load(TB(xt[0:rows, :], xt.b), src_dram[c0:c0 + rows, :])
                    rmsnorm_to_T(xt, rows, g1t, xT, c0, tb, ss, junk)
                chk(2.01)
                for ct in range(4):
                    ps = nps()
                    for kc in range(8):
                        op("pe", lambda e, ps=ps, kc=kc, ct=ct: e.matmul(ps[:, 0:ncols], lhsT=wu[:, kc, ct * 128:(ct + 1) * 128], rhs=xT[:, kc, 0:ncols], start=(kc == 0), stop=(kc == 7)), [wu, xT], [ps])
                    op("act", lambda e, ps=ps, ct=ct: e.activation(out=u_t[:, ct, 0:ncols], in_=ps[:, 0:ncols], func=AF.Copy), [ps], [u_t])
                chk(2.02)
                if sample:
                    bps = []
                    for m in range(4):
                        ps = nps()
                        bps.append(ps)
                        for q in range(4):
                            for ri in range(2):
                                reg = q * 2 + ri
                                op("pe", lambda e, ps=ps, q=q, m=m, ri=ri, reg=reg: e.matmul(
                                    ps[:, reg * 16:(reg + 1) * 16], lhsT=WB[m * 32:(m + 1) * 32, q, 7, ri, :],
                                    rhs=u_t[m * 32:(m + 1) * 32, q, 0:16], start=True, stop=True, tile_position=(m * 32, 0)), [WB, u_t], [ps])
                    lbr_b = LBR[:].unsqueeze(2).broadcast_to([128, 16, 16]); lbi_b = LBI[:].unsqueeze(2).broadcast_to([128, 16, 16])
                    op("dve", lambda e: e.tensor_tensor(out=s1[:], in0=h0r[:], in1=lbr_b, op=ALU.mult), [h0r, LBR], [s1])
                    op("dve", lambda e: e.tensor_tensor(out=s2[:], in0=h0i[:], in1=lbi_b, op=ALU.mult), [h0i, LBI], [s2])
                    op("dve", lambda e: e.tensor_tensor(out=s1[:], in0=s1[:], in1=s2[:], op=ALU.subtract), [s1, s2], [s1])
                    for m in range(4):
                        bv = bps[m][:, 0:128].rearrange("p (q r b) -> p q r b", r=2, b=16)
                        op("dve", lambda e, m=m, bv=bv: e.tensor_tensor(out=hnr[:].rearrange("p (q m) b -> p m q b", m=4)[:, m], in0=s1[:].rearrange("p (q m) b -> p m q b", m=4)[:, m], in1=bv[:, :, 0, :], op=ALU.add), [s1, bps[m]], [hnr])
                    op("dve", lambda e: e.tensor_tensor(out=s1[:], in0=h0i[:], in1=lbr_b, op=ALU.mult), [h0i, LBR, hnr], [s1])
                    op("dve", lambda e: e.tensor_tensor(out=s2[:], in0=h0r[:], in1=lbi_b, op=ALU.mult), [h0r, LBI], [s2])
                    op("dve", lambda e: e.tensor_tensor(out=s1[:], in0=s1[:], in1=s2[:], op=ALU.add), [s1, s2], [s1])
                    for m in range(4):
                        bv = bps[m][:, 0:128].rearrange("p (q r b) -> p q r b", r=2, b=16)
                        op("dve", lambda e, m=m, bv=bv: e.tensor_tensor(out=hni[:].rearrange("p (q m) b -> p m q b", m=4)[:, m], in0=s1[:].rearrange("p (q m) b -> p m q b", m=4)[:, m], in1=bv[:, :, 1, :], op=ALU.add), [s1, bps[m]], [hni])
                    op("sp", lambda e: e.dma_start(out=sres_o, in_=hnr[:]), [hnr], [], chan="o_sr")
                    op("sp", lambda e: e.dma_start(out=sims_o, in_=hni[:]), [hni], [], chan="o_si")
                    for q in range(4):
                        ps = nps()
                        op("pe", lambda e, ps=ps, q=q: e.matmul(ps[:, 0:16], lhsT=KT[:, q, 0, :], rhs=u_t[:, q, 0:16], start=True, stop=False), [KT, u_t], [ps])
                        for m in range(4):
                            for ri in range(2):
                                last = (ri == 1)
                                op("pe", lambda e, ps=ps, q=q, m=m, ri=ri, last=last: e.matmul(
                                    ps[m * 32:(m + 1) * 32, 0:16], lhsT=WD[:, 0, q * 4 + m, ri, :], rhs=h0b[:, ri, q * 4 + m, :],
                                    start=False, stop=last, tile_position=(0, m * 32)), [WD, h0b], [ps])
                        op("act", lambda e, ps=ps, q=q: e.activation(out=yT[:, q, 0:16], in_=ps[:, 0:16], func=AF.Copy), [ps], [yT])
                    glu(16, TOK)
                    return
                yield xT
                vps = [nps() for _ in range(4)]
                for q in range(4):
                    for ri in range(2):
                        reg = q * 2 + ri
                        for s in range(8):
                            for m in range(4):
                                ps = vps[m]
                                op("pe", lambda e, ps=ps, reg=reg, q=q, m=m, s=s, ri=ri: e.matmul(
                                    ps[:, reg * 64:(reg + 1) * 64], lhsT=WB[m * 32:(m + 1) * 32, q, s, ri, :],
                                    rhs=u_t[m * 32:(m + 1) * 32, q, 0:512].rearrange("p (c s) -> p s c", s=8)[:, s, :], start=(s == 0), stop=(s == 7), tile_position=(m * 32, 0)), [WB, u_t], [ps])
                chk(2.03)
                op("dve", lambda e: e.tensor_copy(out=hp[:, 0, :, 0:1], in_=CARR[:].unsqueeze(2)), [CARR], [hp] + [TB(None, b_) for b_ in hpB])
                op("dve", lambda e: e.tensor_copy(out=hp[:, 1, :, 0:1], in_=CARI[:].unsqueeze(2)), [CARI], [hp] + [TB(None, b_) for b_ in hpB])

                def a3(t):
                    return t[:, 0:256].rearrange("p (q c) -> p q c", c=64)

                def stm(X, m):
                    return X[:].rearrange("p (q m) -> p m q", m=4)[:, m, :]

                a, b, c, d = mt[0], mt[1], mt[2], mt[3]
                k1, k2, k3, k4 = small[0], small[1], small[2], small[3]
                for m in range(4):
                    V = vps[m]
                    Vv = V[:].rearrange("p (q r c) -> p q r c", r=2, c=64)
                    VR = Vv[:, :, 0, :]; VI = Vv[:, :, 1, :]
                    ur = UTR[:, m * 4:(m + 1) * 4, :]; ui = UTI[:, m * 4:(m + 1) * 4, :]
                    bR = bRm[m]; bI = bIm[m]
                    op("dve", lambda e, VR=VR, ur=ur: e.tensor_tensor(out=a3(a), in0=VR, in1=ur, op=ALU.mult), [V, UTR], [a])
                    op("dve", lambda e, VI=VI, ui=ui: e.tensor_tensor(out=a3(b), in0=VI, in1=ui, op=ALU.mult), [V, UTI], [b])
                    op("dve", lambda e, VI=VI, ur=ur: e.tensor_tensor(out=a3(c), in0=VI, in1=ur, op=ALU.mult), [V, UTR], [c])
                    op("dve", lambda e, VR=VR, ui=ui: e.tensor_tensor(out=a3(d), in0=VR, in1=ui, op=ALU.mult), [V, UTI], [d])
                    op("dve", lambda e, bR=bR: e.tensor_tensor(out=bR[:, 0:256], in0=a[:, 0:256], in1=b[:, 0:256], op=ALU.add), [a, b], [bR])
                    op("dve", lambda e, bI=bI: e.tensor_tensor(out=bI[:, 0:256], in0=c[:, 0:256], in1=d[:, 0:256], op=ALU.subtract), [c, d], [bI])
                for m in range(4):
                    ur = UTR[:, m * 4:(m + 1) * 4, :]; ui = UTI[:, m * 4:(m + 1) * 4, :]
                    rt = RT[:, m * 4:(m + 1) * 4, :].rearrange("p q c -> p (q c)")
                    bR = bRm[m]; bI = bIm[m]
                    op("dve", lambda e, m=m: e.tensor_tensor(out=k1[:, 0:4], in0=stm(CARR, m), in1=stm(R8, m), op=ALU.mult), [CARR, R8], [k1])
                    op("dve", lambda e, m=m: e.tensor_tensor(out=k2[:, 0:4], in0=stm(CARI, m), in1=stm(R8, m), op=ALU.mult), [CARI, R8], [k2])
                    op("dve", lambda e, bR=bR: e.tensor_tensor(out=a3(bR)[:, :, 0:1], in0=a3(bR)[:, :, 0:1], in1=k1[:, 0:4].unsqueeze(2), op=ALU.add), [bR, k1], [bR])
                    op("dve", lambda e, bI=bI: e.tensor_tensor(out=a3(bI)[:, :, 0:1], in0=a3(bI)[:, :, 0:1], in1=k2[:, 0:4].unsqueeze(2), op=ALU.add), [bI, k2], [bI])
                    op("dve", lambda e, rt=rt, bR=bR: e.tensor_tensor_scan(out=GR[:, 0:256], data0=rt, data1=bR[:, 0:256], initial=0.0, op0=ALU.mult, op1=ALU.add), [bR, RT], [GR])
                    op("dve", lambda e, rt=rt, bI=bI: e.tensor_tensor_scan(out=GI[:, 0:256], data0=rt, data1=bI[:, 0:256], initial=0.0, op0=ALU.mult, op1=ALU.add), [bI, RT], [GI])
                    if is_own:
                        hpr = hp[:, 0, :, :].rearrange("p (q m) c -> p m q c", m=4)[:, m, :, 1:65]
                        hpi = hp[:, 1, :, :].rearrange("p (q m) c -> p m q c", m=4)[:, m, :, 1:65]
                        op("dve", lambda e, ur=ur: e.tensor_tensor(out=a3(a), in0=a3(GR), in1=ur, op=ALU.mult), [GR, UTR], [a])
                        op("dve", lambda e, ui=ui: e.tensor_tensor(out=a3(b), in0=a3(GI), in1=ui, op=ALU.mult), [GI, UTI], [b])
                        op("dve", lambda e, hpr=hpr: e.tensor_tensor(out=hpr, in0=a3(a), in1=a3(b), op=ALU.subtract), [a, b], [TB(None, hpB[m])])
                        op("dve", lambda e, ur=ur: e.tensor_tensor(out=a3(c), in0=a3(GI), in1=ur, op=ALU.mult), [GI, UTR], [c])
                        op("dve", lambda e, ui=ui: e.tensor_tensor(out=a3(d), in0=a3(GR), in1=ui, op=ALU.mult), [GR, UTI], [d])
                        op("dve", lambda e, hpi=hpi: e.tensor_tensor(out=hpi, in0=a3(c), in1=a3(d), op=ALU.add), [c, d], [TB(None, hpB[m])])
                    g63r = a3(GR)[:, :, 63]; g63i = a3(GI)[:, :, 63]
                    u63r = UTR[:, m * 4:(m + 1) * 4, 63]; u63i = UTI[:, m * 4:(m + 1) * 4, 63]
                    op("dve", lambda e, g63r=g63r, u63r=u63r: e.tensor_tensor(out=k1[:, 0:4], in0=g63r, in1=u63r, op=ALU.mult), [GR, UTR], [k1])
                    op("dve", lambda e, g63i=g63i, u63i=u63i: e.tensor_tensor(out=k2[:, 0:4], in0=g63i, in1=u63i, op=ALU.mult), [GI, UTI], [k2])
                    op("dve", lambda e, g63i=g63i, u63r=u63r: e.tensor_tensor(out=k3[:, 0:4], in0=g63i, in1=u63r, op=ALU.mult), [GI, UTR], [k3])
                    op("dve", lambda e, g63r=g63r, u63i=u63i: e.tensor_tensor(out=k4[:, 0:4], in0=g63r, in1=u63i, op=ALU.mult), [GR, UTI], [k4])
                    op("dve", lambda e, m=m: e.tensor_tensor(out=stm(CARR, m), in0=k1[:, 0:4], in1=k2[:, 0:4], op=ALU.subtract), [k1, k2, hp], [CARR])
                    op("dve", lambda e, m=m: e.tensor_tensor(out=stm(CARI, m), in0=k3[:, 0:4], in1=k4[:, 0:4], op=ALU.add), [k3, k4, hp], [CARI])
                chk(2.04)
                if not is_own:
                    return xT
                for q in range(4):
                    ps = nps()
                    for tau in range(8):
                        for dl in range(tau + 1):
                            op("pe", lambda e, ps=ps, q=q, tau=tau, dl=dl: e.matmul(
                                ps[:, tau * 64:(tau + 1) * 64], lhsT=KT[:, q, dl, :], rhs=u_t[:, q, 0:512].rearrange("p (c s) -> p s c", s=8)[:, tau - dl, :], start=(dl == 0), stop=(dl == tau)), [KT, u_t], [ps])
                    op("act", lambda e, ps=ps, q=q: e.activation(out=yT[:, q, 0:512].rearrange("p (c t) -> p t c", t=8), in_=ps[:].rearrange("p (t c) -> p t c", c=64), func=AF.Copy), [ps], [yT])
                pds = [nps() for _ in range(4)]
                for m in range(4):
                    for q in range(4):
                        ps = pds[q]
                        for tau in range(8):
                            for ri in range(2):
                                op("pe", lambda e, ps=ps, q=q, tau=tau, m=m, ri=ri: e.matmul(
                                    ps[m * 32:(m + 1) * 32, tau * 64:(tau + 1) * 64], lhsT=WD[:, tau, q * 4 + m, ri, :], rhs=hp[:, ri, q * 4 + m, 0:64],
                                    start=(ri == 0), stop=(ri == 1), tile_position=(0, m * 32)), [WD, TB(None, hpB[m])], [ps])
                for q in range(4):
                    ps = pds[q]
                    op("dve", lambda e, ps=ps, q=q: e.tensor_tensor(out=yT[:, q, 0:512].rearrange("p (c t) -> p t c", t=8), in0=ps[:].rearrange("p (t c) -> p t c", c=64), in1=yT[:, q, 0:512].rearrange("p (c t) -> p t c", t=8), op=ALU.add), [ps, yT], [yT])
                glu(512, tile_i * 512)
                return xT

            def glu(ncols, col0):
                op("act", lambda e: e.activation(out=zT[:, :, 0:ncols], in_=yT[:, :, 0:ncols], func=AF.Gelu), [yT], [zT])
                op("dve", lambda e: e.tensor_copy(out=zTb[:, :, 0:ncols], in_=zT[:, :, 0:ncols]), [zT], [zTb])
                for ct in range(4):
                    ps = nps()
                    for kc in range(4):
                        op("pe", lambda e, ps=ps, kc=kc, ct=ct: e.matmul(ps[:, 0:ncols], lhsT=wglu_t[:, kc, ct * 128:(ct + 1) * 128], rhs=zTb[:, kc, 0:ncols], start=(kc == 0), stop=(kc == 3)), [wglu_t, zTb], [ps])
                    op("act", lambda e, ps=ps, ct=ct: e.activation(out=sg[:, 0:ncols], in_=ps[:, 0:ncols], func=AF.Sigmoid, bias=bglt[:, ct:ct + 1]), [ps, bglt], [sg])
                    op("dve", lambda e, ct=ct: e.tensor_tensor(out=ssmT[:, ct, col0:col0 + ncols], in0=zT[:, ct, 0:ncols], in1=sg[:, 0:ncols], op=ALU.mult), [zT, sg], [ssmT])

            chk(0)
            for dg in range(2):
                precast(w_in[:, 1280 + dg * 512:1280 + (dg + 1) * 512], 8, 512, "g1_%d" % dg)
                precast(w_in[:, 2304 + dg * 512:2304 + (dg + 1) * 512], 8, 512, "g2_%d" % dg)
                precast(w_ba[:, dg * 512:(dg + 1) * 512], 4, 512, "br_%d" % dg, 0, 4096)
                precast(w_bs[:, dg * 512:(dg + 1) * 512], 4, 512, "br_%d" % dg, 2048, 4096)
            for hf_ in range(2):
                precast(w_out[:, hf_ * 512:(hf_ + 1) * 512], 8, 512, "wo_%d" % hf_)
            for fg in range(6):
                nf = 4 if fg < 5 else 2
                precast(w_fg[:, fg * 512:fg * 512 + nf * 128], 8, nf * 128, "fg_%d" % fg)
                precast(w_fu[:, fg * 512:fg * 512 + nf * 128], 8, nf * 128, "fu_%d" % fg)
            for hf_ in range(2):
                for fgp in range(3):
                    nk = 8 if fgp < 2 else 6
                    precast(w_fd[fgp * 1024:fgp * 1024 + nk * 128, hf_ * 512:(hf_ + 1) * 512], nk, 512, "fd_%d_%d" % (hf_, fgp))
            def finish(g):
                for _ in g:
                    pass

            tiles_ = [(xp[ti * 512:(ti + 1) * 512, :], False, ti) for ti in range(4)] + [(xo[ti * 512:(ti + 1) * 512, :], True, ti) for ti in range(4)]
            gens = []
            for k_ in range(4):
                g_ = ssm_tile(tiles_[k_][0], 4, tiles_[k_][1], tiles_[k_][2])
                xT_last = next(g_)
                if gens:
                    finish(gens[-1])
                gens.append(g_)
            def kv_block(xT, c0, rows, ridx, blk, kout=None, vout=None, kcol=None):
                psk = nps()
                for kc in range(8):
                    op("pe", lambda e, kc=kc: e.matmul(psk[0:rows, 0:256], lhsT=xT[:, kc, c0:c0 + rows], rhs=wqk[:, kc, 512:768], start=(kc == 0), stop=(kc == 7)), [xT, wqk], [psk])
                kf = kvf[blk % 2]
                rope_kv(psk, rows, ridx, kf)
                return kf

            kf = kv_block(xT_last, 384, 128, 16, 0)
            qkt = qk_tok[0]
            op("act", lambda e: e.activation(out=qkt[:, 512:640], in_=kf[:, 0:128], func=AF.Copy), [kf], [qkt])
            op("act", lambda e: e.activation(out=v_all[:, 0, :], in_=kf[:, 128:256], func=AF.Copy), [kf], [v_all])
            pt = npt()
            op("pe", lambda e: e.transpose(out=pt[:, 0:128], in_=qkt[:, 512:640], identity=identb[:]), [qkt, identb], [pt])
            op("act", lambda e: e.activation(out=kT_all[:, 0:128], in_=pt[:, 0:128], func=AF.Copy), [pt], [kT_all])
            chk(3)
            for k_ in range(4, 8):
                g_ = ssm_tile(tiles_[k_][0], 4, tiles_[k_][1], tiles_[k_][2])
                next(g_)
                finish(gens[-1])
                gens.append(g_)
            finish(gens[-1])
            op("sp", lambda e: e.dma_start(out=hre_o, in_=CARR[:]), [CARR], [], chan="o_hr")
            op("sp", lambda e: e.dma_start(out=him_o, in_=CARI[:]), [CARI], [], chan="o_hi")
            finish(ssm_tile(xs, 1, True, 0, sample=True))

            P.fence()
            apos["o"] = 0

            for i_ in range(4):
                WRING[i_] = carve([128, 4096], BF16)
            x1 = carve([128, 5, D])
            xn_tok = [carve([128, D], BF16) for _ in range(2)]
            junk = carve([128, D], BF16); ssv = [carve([128, 1]) for _ in range(4)]
            actT = carve([128, 8, 528], BF16)
            h2T = carve([128, 8, 528], BF16)
            xs_f = [carve([128, D]) for _ in range(2)]
            hB = [Buf() for _ in range(5)]
            qT = carve([128, 4, 528], BF16)
            attnT = carve([128, 4, 528], BF16)
            mergedT = carve([128, 8, 528], BF16)
            alias_o = apos["o"]
            hT = carve([128, NFT, 528], BF16)
            alias_end = apos["o"]
            qk_tok = [carve([128, 640], BF16) for _ in range(2)]
            kvf = [carve([128, 256]) for _ in range(2)]
            rtmp = [carve([128, 10, 8]) for _ in range(4)]
            pexp = [carve([128, 512], BF16) for _ in range(4)]
            qf = carve([128, 512])
            denr = carve([128, 512])
            sg1 = carve([128, 528]); sg2 = carve([128, 528]); mtmp = carve([128, 528]); mtmp2 = carve([128, 528])
            silu_t = [carve([128, 528])] * 2
            yout = [carve([128, D])] * 2
            keep_o = apos["o"]
            apos["o"] = alias_o
            KSb = carve([128, NS, 128], BF16); VSb = carve([128, NS, 128], BF16); KST = carve([128, NS, 128], BF16)
            assert apos["o"] <= alias_end
            apos["o"] = keep_o
            pes = carve([128, 2, NS, 4], BF16)
            cnt = {"x": 0}
            x1B = [Buf() for _ in range(5)]; aTB = [Buf() for _ in range(5)]; qTB = [Buf() for _ in range(5)]
            atB = [Buf() for _ in range(5)]; mgB = [Buf() for _ in range(8)]; hTB = [Buf() for _ in range(NFT)]
            kTB = [Buf() for _ in range(18)]; vB = [Buf() for _ in range(18)]
            bkw = TB(None); bvw = TB(None)

            def Dp(bufs):
                return [TB(None, b) for b in bufs]

            def cgB(lst, lc, n):
                return Dp([lst[k] for k in range(5) if k * 128 < lc + n and k * 128 + (128 if k < 4 else 16) > lc])

            HTALL = Dp(hTB)

            def tile_main(ti):
                has_s = (ti == 3)
                subs = [(ti * 512 + j * 128, 128, j, ti * 4 + j) for j in range(4)]
                if has_s:
                    subs.append((TOK, 16, 4, 17))
                cgs = [(ti * 512, 512, 0)] + ([(TOK, 16, 512)] if has_s else [])
                for (g0, rows, j, ridx) in subs:
                    cnt["x"] += 1
                    src = xo[g0:g0 + rows, :] if g0 < TOK else xs
                    xf = xs_f[cnt["x"] % 2]
                    load(TB(xf[0:rows, :], xf.b), src)
                    rmsnorm_to_T(xf, rows, g1t, TB(actT.t, aTB[j]), j * 128, xn_tok[cnt["x"] % 2], ssv[cnt["x"] % 4], junk)
                chk(5.1)
                for (g0, rows, j, ridx) in subs:
                    cnt["x"] += 1
                    lc = j * 128
                    psq = nps(); psk = nps()
                    for kc in range(8):
                        op("pe", lambda e, kc=kc, psq=psq, lc=lc, rows=rows: e.matmul(psq[0:rows, :], lhsT=actT[:, kc, lc:lc + rows], rhs=wqk[:, kc, 0:512], start=(kc == 0), stop=(kc == 7)), [TB(None, aTB[j]), wqk], [psq])
                    for kc in range(8):
                        op("pe", lambda e, kc=kc, psk=psk, lc=lc, rows=rows: e.matmul(psk[0:rows, 0:256], lhsT=actT[:, kc, lc:lc + rows], rhs=wqk[:, kc, 512:768], start=(kc == 0), stop=(kc == 7)), [TB(None, aTB[j]), wqk], [psk])
                    chk(5.11)
                    qkt = qk_tok[cnt["x"] % 2]; kf = kvf[cnt["x"] % 2]
                    op("act", lambda e, psq=psq, rows=rows: e.activation(out=qf[0:rows, :], in_=psq[0:rows, :], func=AF.Copy), [psq], [qf])
                    op("dve", lambda e, qkt=qkt, rows=rows: e.tensor_copy(out=qkt[0:rows, 0:512], in_=qf[0:rows, :]), [qf], [qkt])
                    chk(5.115)
                    rope_apply(qf[0:rows, :].rearrange("p (h d) -> p h d", d=64), qkt[0:rows, 0:512].rearrange("p (h d) -> p h d", d=64), rows, ridx, 8, [qf], qkt)
                    chk(5.12)
                    rope_kv(psk, rows, ridx, kf)
                    chk(5.13)
                    op("act", lambda e, qkt=qkt, kf=kf, rows=rows: e.activation(out=qkt[0:rows, 512:640], in_=kf[0:rows, 0:128], func=AF.Copy), [kf], [qkt])
                    if g0 < TOK:
                        blk = g0 // 128 + 1
                        op("act", lambda e, kf=kf, blk=blk: e.activation(out=v_all[:, blk, :], in_=kf[:, 128:256], func=AF.Copy), [kf], [TB(None, vB[blk])])
                        if g0 == TOK - 128:
                            op("sp", lambda e, kf=kf: e.dma_start(out=kwp_o, in_=kf[:, 0:128]), [kf], [], chan="o_kw")
                            op("sp", lambda e, kf=kf: e.dma_start(out=vwp_o, in_=kf[:, 128:256]), [kf], [], chan="o_vw")
                    else:
                        op("sp", lambda e, kf=kf: e.dma_start(out=kws_o[:, 127, :], in_=kf[0:NS, 0:128]), [kf], [bkw], chan="o_ks")
                        op("sp", lambda e, kf=kf: e.dma_start(out=vws_o[:, 127, :], in_=kf[0:NS, 128:256]), [kf], [bvw], chan="o_vs")
                        op("sp", lambda e: e.dma_start(out=kws_o[:, 0:127, :], in_=kwin[:, 1:128, :]), [], [], chan="o_ks2")
                        op("sp", lambda e: e.dma_start(out=vws_o[:, 0:127, :], in_=vwin[:, 1:128, :]), [], [], chan="o_vs2")
                        op("pool", lambda e: e.dma_start(out=KSb[0:112, :, :], in_=kwin[:, 1:113, :].rearrange("b j e -> j b e")), [], [KSb] + HTALL, chan="l_ks")
                        op("pool", lambda e: e.dma_start(out=VSb[0:112, :, :], in_=vwin[:, 1:113, :].rearrange("b j e -> j b e")), [], [VSb] + HTALL, chan="l_vs")
                        op("pool", lambda e: e.dma_start(out=KSb[112:127, :, :], in_=kwin[:, 113:128, :].rearrange("b j e -> j b e")), [KSb], [KSb], chan="l_ks")
                        op("pool", lambda e: e.dma_start(out=VSb[112:127, :, :], in_=vwin[:, 113:128, :].rearrange("b j e -> j b e")), [VSb], [VSb], chan="l_vs")
                        op("pool", lambda e: e.dma_start(out=KSb[127:128, :, :], in_=kws_o[:, 127:128, :].rearrange("b j e -> j b e")), [KSb, bkw], [KSb], chan="l_ks")
                        op("pool", lambda e: e.dma_start(out=VSb[127:128, :, :], in_=vws_o[:, 127:128, :].rearrange("b j e -> j b e")), [VSb, bvw], [VSb], chan="l_vs")
                    chk(5.14)
                    pt = npt()
                    for t in range(5):
                        op("pe", lambda e, t=t, pt=pt, qkt=qkt, rows=rows: e.transpose(out=pt[:, t * 128:t * 128 + rows], in_=qkt[0:rows, t * 128:(t + 1) * 128], identity=identb[0:rows, 0:rows]), [qkt, identb], [pt])
                    op("act", lambda e, pt=pt, lc=lc, rows=rows: e.activation(out=qT[:, 0:4, lc:lc + rows], in_=pt[:, 0:512].rearrange("p (k t) -> p k t", t=128)[:, :, 0:rows], func=AF.Copy), [pt], [TB(None, qTB[j])])
                    if g0 < TOK:
                        op("act", lambda e, pt=pt, g0=g0: e.activation(out=kT_all[:, 128 + g0:256 + g0], in_=pt[:, 512:640], func=AF.Copy), [pt], [TB(None, kTB[g0 // 128 + 1])])
                chk(5.2)
                for (g0, rows, j, ridx) in subs:
                    if g0 >= TOK:
                        continue
                    lc = j * 128
                    blk = g0 // 128 + 1
                    pe_t = []
                    for kv in range(2):
                        for which in range(2):
                            kb = blk - which
                            ps = nps()
                            op("pe", lambda e, ps=ps, kv=kv, kb=kb, lc=lc: e.matmul(
                                ps[:].rearrange("p (h q) -> p h q", q=128), lhsT=kT_all[kv * 64:(kv + 1) * 64, kb * 128:(kb + 1) * 128],
                                rhs=qT[kv * 64:(kv + 1) * 64, 0:4, lc:lc + 128], start=True, stop=True, tile_position=(kv * 64, 0)), [TB(None, kTB[kb]), TB(None, qTB[j])], [ps])
                            pe_ = pexp[kv * 2 + which]
                            op("act", lambda e, ps=ps, pe_=pe_: e.activation(out=pe_[:], in_=ps[:], func=AF.Exp, scale=0.125), [ps], [pe_])
                            mk = mko if which == 0 else (mkp0 if blk == 1 else mkp)
                            op("dve", lambda e, pe_=pe_, mk=mk: e.tensor_tensor(out=pe_[:], in0=pe_[:], in1=mk[:], op=ALU.mult), [pe_, mk], [pe_])
                            pe_t.append((kv, kb, pe_))
                    psn = nps(); psd = nps()
                    for idx, (kv, kb, pe_) in enumerate(pe_t):
                        first = (idx % 2 == 0); lastk = (idx % 2 == 1)
                        op("pe", lambda e, kv=kv, kb=kb, pe_=pe_, first=first, lastk=lastk, psn=psn: e.matmul(
                            psn[kv * 64:(kv + 1) * 64, :], lhsT=v_all[:, kb, kv * 64:(kv + 1) * 64], rhs=pe_[:], start=first, stop=lastk, tile_position=(0, kv * 64)), [TB(None, vB[kb]), pe_], [psn])
                        op("pe", lambda e, kv=kv, pe_=pe_, first=first, lastk=lastk, psd=psd: e.matmul(
                            psd[kv * 64:(kv + 1) * 64, :], lhsT=ones_bf[:, 0:64], rhs=pe_[:], start=first, stop=lastk, tile_position=(0, kv * 64)), [ones_bf, pe_], [psd])
                    op("dve", lambda e, psd=psd: e.tensor_tensor(out=denr[:].rearrange("p (h q) -> p h q", q=128), in0=psd[:].rearrange("p (h q) -> p h q", q=128), in1=esk[:].unsqueeze(2).broadcast_to([128, 4, 128]), op=ALU.add), [psd, esk], [denr])
                    op("dve", lambda e: e.reciprocal(out=denr[:], in_=denr[:]), [denr], [denr])
                    op("dve", lambda e, psn=psn, lc=lc: e.tensor_tensor(out=attnT[:, :, lc:lc + 128], in0=psn[:].rearrange("p (h q) -> p h q", q=128), in1=denr[:].rearrange("p (h q) -> p h q", q=128), op=ALU.mult), [psn, denr], [TB(None, atB[j])])
                chk(5.3)
                if has_s:
                    for hb in range(2):
                        pt = npt()
                        for bb in range(8):
                            b_ = hb * 8 + bb
                            op("pe", lambda e, pt=pt, bb=bb, b_=b_: e.transpose(out=pt[:, bb * 128:(bb + 1) * 128], in_=KSb[:, b_, :], identity=identb[:]), [KSb, identb], [pt])
                        op("act", lambda e, pt=pt, hb=hb: e.activation(out=KST[:, hb * 8:(hb + 1) * 8, :].rearrange("p b k -> p (b k)"), in_=pt[:], func=AF.Copy), [pt], [KST] + HTALL)
                    for kv in range(2):
                        ps = nps()
                        for b_ in range(NS):
                            op("pe", lambda e, ps=ps, kv=kv, b_=b_: e.matmul(
                                ps[:, b_ * 4:(b_ + 1) * 4], lhsT=KST[kv * 64:(kv + 1) * 64, b_, :],
                                rhs=qT[kv * 64:(kv + 1) * 64, 0:4, 512 + b_], start=True, stop=True, tile_position=(kv * 64, 0)), [KST, TB(None, qTB[4])] + HTALL, [ps])
                        op("act", lambda e, ps=ps, kv=kv: e.activation(out=pes[:, kv, :, :].rearrange("p b c -> p (b c)"), in_=ps[:, 0:64], func=AF.Exp, scale=0.125), [ps], [pes])
                    psn = nps(); psd = nps()
                    for kv in range(2):
                        for b_ in range(NS):
                            op("pe", lambda e, kv=kv, b_=b_, psn=psn: e.matmul(psn[kv * 64:(kv + 1) * 64, b_ * 4:(b_ + 1) * 4], lhsT=VSb[:, b_, kv * 64:(kv + 1) * 64], rhs=pes[:, kv, b_, :], start=True, stop=True, tile_position=(0, kv * 64)), [VSb, pes] + HTALL, [psn])
                            op("pe", lambda e, kv=kv, b_=b_, psd=psd: e.matmul(psd[kv * 64:(kv + 1) * 64, b_ * 4:(b_ + 1) * 4], lhsT=ones_bf[:, 0:64], rhs=pes[:, kv, b_, :], start=True, stop=True, tile_position=(0, kv * 64)), [ones_bf, pes], [psd])
                    dv_ = denr[:, 0:64].rearrange("p (b h) -> p b h", h=4)
                    op("dve", lambda e, psd=psd: e.tensor_tensor(out=dv_, in0=psd[:, 0:64].rearrange("p (b h) -> p b h", h=4), in1=esk[:].unsqueeze(1).broadcast_to([128, NS, 4]), op=ALU.add), [psd, esk], [denr])
                    op("dve", lambda e: e.reciprocal(out=denr[:, 0:64], in_=denr[:, 0:64]), [denr], [denr])
                    op("dve", lambda e, psn=psn: e.tensor_tensor(out=attnT[:, :, 512:528].rearrange("p h b -> p b h"), in0=psn[:, 0:64].rearrange("p (b h) -> p b h", h=4), in1=dv_, op=ALU.mult), [psn, denr], [TB(None, atB[4])])
                yield "F1"
                for dg in range(2):
                    wg1 = wload(w_in[:, 1280 + dg * 512:1280 + (dg + 1) * 512], 8, 512, "g1_%d" % dg, ti)
                    wg2 = wload(w_in[:, 2304 + dg * 512:2304 + (dg + 1) * 512], 8, 512, "g2_%d" % dg, ti)
                    wr_i["i"] = (wr_i["i"] + 1) % 4
                    slot = WRING[wr_i["i"]]
                    vba = slot[:, 0:2048].rearrange("p (k c) -> p k c", c=512); vbs = slot[:, 2048:4096].rearrange("p (k c) -> p k c", c=512)
                    i_ = wr_i["i"]
                    scr_b, sbb_ = scratch("br_%d" % dg, 4096)
                    op("pool", lambda e, slot=slot, scr_b=scr_b: e.dma_start(out=slot[:, 0:4096], in_=scr_b), [sbb_], [slot], chan="wr%d" % i_)
                    wbr = TB(None, slot.b)
                    for dd in range(4):
                        dt_ = dg * 4 + dd
                        for (gc, n, lc) in cgs:
                            p1 = nps(); p2 = nps(); pa = nps(); pb = nps()
                            for kc in range(8):
                                op("pe", lambda e, kc=kc, p1=p1, dd=dd, lc=lc, n=n, wg1=wg1: e.matmul(p1[:, 0:n], lhsT=wg1[:, kc, dd * 128:(dd + 1) * 128], rhs=actT[:, kc, lc:lc + n], start=(kc == 0), stop=(kc == 7)), [wg1] + cgB(aTB, lc, n), [p1])
                            for kc in range(8):
                                op("pe", lambda e, kc=kc, p2=p2, dd=dd, lc=lc, n=n, wg2=wg2: e.matmul(p2[:, 0:n], lhsT=wg2[:, kc, dd * 128:(dd + 1) * 128], rhs=actT[:, kc, lc:lc + n], start=(kc == 0), stop=(kc == 7)), [wg2] + cgB(aTB, lc, n), [p2])
                            for kc in range(4):
                                op("pe", lambda e, kc=kc, pa=pa, dd=dd, lc=lc, n=n, vba=vba: e.matmul(pa[:, 0:n], lhsT=vba[:, kc, dd * 128:(dd + 1) * 128], rhs=attnT[:, kc, lc:lc + n], start=(kc == 0), stop=(kc == 3)), [wbr] + cgB(atB, lc, n), [pa])
                            for kc in range(4):
                                op("pe", lambda e, kc=kc, pb=pb, dd=dd, gc=gc, n=n, vbs=vbs: e.matmul(pb[:, 0:n], lhsT=vbs[:, kc, dd * 128:(dd + 1) * 128], rhs=ssmT[:, kc, gc:gc + n], start=(kc == 0), stop=(kc == 3)), [wbr, ssmT], [pb])
                            op("act", lambda e, p1=p1, dt_=dt_, n=n: e.activation(out=sg1[:, 0:n], in_=p1[:, 0:n], func=AF.Sigmoid, bias=bgt[:, dt_:dt_ + 1]), [p1, bgt], [sg1])
                            op("act", lambda e, p2=p2, dt_=dt_, n=n: e.activation(out=sg2[:, 0:n], in_=p2[:, 0:n], func=AF.Sigmoid, bias=bgt[:, 8 + dt_:9 + dt_]), [p2, bgt], [sg2])
                            op("dve", lambda e, pa=pa, n=n: e.tensor_tensor(out=mtmp[:, 0:n], in0=pa[:, 0:n], in1=sg1[:, 0:n], op=ALU.mult), [pa, sg1], [mtmp])
                            op("dve", lambda e, pb=pb, n=n: e.tensor_tensor(out=mtmp2[:, 0:n], in0=pb[:, 0:n], in1=sg2[:, 0:n], op=ALU.mult), [pb, sg2], [mtmp2])
                            op("dve", lambda e, dt_=dt_, lc=lc, n=n: e.tensor_tensor(out=mergedT[:, dt_, lc:lc + n], in0=mtmp[:, 0:n], in1=mtmp2[:, 0:n], op=ALU.add), [mtmp, mtmp2], [TB(None, mgB[dt_])])
                chk(5.4)
                yield "F2"
                wo = [wload(w_out[:, hf_ * 512:(hf_ + 1) * 512], 8, 512, "wo_%d" % hf_, ti) for hf_ in range(2)]
                for (g0, rows, j, ridx) in subs:
                    lc = j * 128
                    for hf_ in range(2):
                        ps = nps()
                        for kc in range(8):
                            op("pe", lambda e, kc=kc, ps=ps, lc=lc, rows=rows, hf_=hf_: e.matmul(ps[0:rows, :], lhsT=mergedT[:, kc, lc:lc + rows], rhs=wo[hf_][:, kc, :], start=(kc == 0), stop=(kc == 7)), [TB(None, mgB[kc]), wo[hf_]], [ps])
                        op("dve", lambda e, ps=ps, j=j, rows=rows, hf_=hf_: e.tensor_tensor(out=x1[0:rows, j, hf_ * 512:(hf_ + 1) * 512], in0=ps[0:rows, :], in1=x1[0:rows, j, hf_ * 512:(hf_ + 1) * 512], op=ALU.add), [ps, TB(None, x1B[j])], [TB(None, x1B[j])])
                for (g0, rows, j, ridx) in subs:
                    cnt["x"] += 1
                    rmsnorm_to_T(TB(x1[:, j, :], x1B[j]), rows, g2t, TB(h2T.t, hB[j]), j * 128, xn_tok[cnt["x"] % 2], ssv[cnt["x"] % 4], junk)
                chk(5.5)
                yield "B1"
                for fg in range(6):
                    nf = 4 if fg < 5 else 2
                    wg = wload(w_fg[:, fg * 512:fg * 512 + nf * 128], 8, nf * 128, "fg_%d" % fg, ti)
                    wu = wload(w_fu[:, fg * 512:fg * 512 + nf * 128], 8, nf * 128, "fu_%d" % fg, ti)
                    for ff in range(nf):
                        ft = fg * 4 + ff
                        for (gc, n, lc) in cgs:
                            pg = nps(); pu = nps()
                            for kc in range(8):
                                op("pe", lambda e, kc=kc, pg=pg, ff=ff, lc=lc, n=n, wg=wg: e.matmul(pg[:, 0:n], lhsT=wg[:, kc, ff * 128:(ff + 1) * 128], rhs=h2T[:, kc, lc:lc + n], start=(kc == 0), stop=(kc == 7)), [wg] + cgB(hB, lc, n), [pg])
                            for kc in range(8):
                                op("pe", lambda e, kc=kc, pu=pu, ff=ff, lc=lc, n=n, wu=wu: e.matmul(pu[:, 0:n], lhsT=wu[:, kc, ff * 128:(ff + 1) * 128], rhs=h2T[:, kc, lc:lc + n], start=(kc == 0), stop=(kc == 7)), [wu] + cgB(hB, lc, n), [pu])
                            sl = silu_t[ft % 2]
                            op("act", lambda e, pg=pg, sl=sl, n=n: e.activation(out=sl[:, 0:n], in_=pg[:, 0:n], func=AF.Silu), [pg], [sl])
                            op("dve", lambda e, pu=pu, sl=sl, ft=ft, lc=lc, n=n: e.tensor_tensor(out=hT[:, ft, lc:lc + n], in0=pu[:, 0:n], in1=sl[:, 0:n], op=ALU.mult), [pu, sl], [TB(None, hTB[ft])])
                chk(5.6)
                for hf_ in range(2):
                    for fgp in range(3):
                        nk = 8 if fgp < 2 else 6
                        wd = wload(w_fd[fgp * 1024:fgp * 1024 + nk * 128, hf_ * 512:(hf_ + 1) * 512], nk, 512, "fd_%d_%d" % (hf_, fgp), ti)
                        for si, (g0, rows, j, ridx) in enumerate(subs):
                            lc = j * 128
                            ps = nps()
                            for kc in range(nk):
                                ft = fgp * 8 + kc
                                op("pe", lambda e, ps=ps, kc=kc, ft=ft, lc=lc, rows=rows, wd=wd, nk=nk: e.matmul(ps[0:rows, :], lhsT=hT[:, ft, lc:lc + rows], rhs=wd[:, kc, :], start=(kc == 0), stop=(kc == nk - 1)), [TB(None, hTB[ft]), wd], [ps])
                            op("dve", lambda e, ps=ps, j=j, rows=rows, hf_=hf_: e.tensor_tensor(out=x1[0:rows, j, hf_ * 512:(hf_ + 1) * 512], in0=ps[0:rows, :], in1=x1[0:rows, j, hf_ * 512:(hf_ + 1) * 512], op=ALU.add), [ps, TB(None, x1B[j])], [TB(None, x1B[j])])
                for (g0, rows, j, ridx) in subs:
                    cnt["x"] += 1
                    ss = ssv[cnt["x"] % 4]; yo = yout[cnt["x"] % 2]
                    xt = TB(x1[:, j, :], x1B[j])
                    op("dve", lambda e, ss=ss, rows=rows: e.memset(ss[0:rows, :], 0.0), [], [ss])
                    op("act", lambda e, ss=ss, rows=rows, xt=xt: e.activation(out=junk[0:rows, :], in_=xt[0:rows, :], func=AF.Square, accum_out=ss[0:rows, :]), [xt, ss], [junk, ss])
                    op("act", lambda e, ss=ss, rows=rows: e.activation(out=ss[0:rows, :], in_=ss[0:rows, :], func=AF.Sqrt, scale=1.0 / D, bias=EPS), [ss], [ss])
                    op("dve", lambda e, ss=ss, rows=rows: e.reciprocal(out=ss[0:rows, :], in_=ss[0:rows, :]), [ss], [ss])
                    op("dve", lambda e, ss=ss, rows=rows, xt=xt, yo=yo: e.scalar_tensor_tensor(out=yo[0:rows, :], in0=xt[0:rows, :], scalar=ss[0:rows, 0:1], in1=gft[0:rows, :], op0=ALU.mult, op1=ALU.mult), [xt, ss, gft], [yo])
                    dst = y_o[g0:g0 + rows, :] if g0 < TOK else ys_o
                    op("sp", lambda e, yo=yo, rows=rows, dst=dst: e.dma_start(out=dst, in_=yo[0:rows, :]), [yo], [yo], chan="oy%d" % (cnt["x"] % 2))

            chk(5)
            def adv(g, tag):
                r = next(g, None)
                assert r == tag, (r, tag)

            def x1_load(ti):
                for j in range(4):
                    g0 = ti * 512 + j * 128
                    load(TB(x1[:, j, :], x1B[j]), xo[g0:g0 + 128, :])
                if ti == 3:
                    load(TB(x1[0:NS, 4, :], x1B[4]), xs)

            tg = [tile_main(ti) for ti in range(4)]
            adv(tg[0], "F1"); adv(tg[0], "F2")
            for ti in range(4):
                x1_load(ti)
                if ti < 3:
                    adv(tg[ti + 1], "F1")
                adv(tg[ti], "B1")
                if ti < 3:
                    adv(tg[ti + 1], "F2")
                adv(tg[ti], None)
                chk(6 + ti)
        try:
            body()
        except _Stop:
            pass
        P.emit(nc)
    return nc


_NC = {}


def make_inputs(x_prompt, x_sample, state_k_win, state_v_win, state_ssm_re, state_ssm_im,
           norm1_g, w_in, b_gate, attn_sinks, ssm_lam_re, ssm_lam_im, ssm_log_dt,
           ssm_b_re, ssm_b_im, ssm_c_re, ssm_c_im, ssm_d, w_glu, b_glu,
           w_branch_attn, w_branch_ssm, w_out, norm2_g, w_ffn_gate, w_ffn_up, w_ffn_down, norm_f_g):
    f32 = np.float32
    bf = ml_dtypes.bfloat16
    A = lambda a: np.ascontiguousarray(np.asarray(a), dtype=f32)
    x_prompt = A(x_prompt); x_sample = A(x_sample)
    perm = np.concatenate([np.r_[t * 64:(t + 1) * 64, (4 + t) * 64:(5 + t) * 64] for t in range(4)])
    w_in0 = A(w_in)[0]
    w_in_p = np.ascontiguousarray(np.concatenate([w_in0[:, :512][:, perm], w_in0[:, 512:]], axis=1))
    w_ba_p = np.ascontiguousarray(A(w_branch_attn)[0][perm, :])
    sinks = A(attn_sinks)[0]
    sink_l = np.ascontiguousarray(np.concatenate([np.tile(sinks[None, 0:4], (64, 1)), np.tile(sinks[None, 4:8], (64, 1))], axis=0))

    def st_layout(a):
        a = a.reshape((16, 2, 64) + a.shape[2:])
        return np.ascontiguousarray(np.moveaxis(a, 0, 2).reshape((128, 16) + a.shape[3:]))

    lam_re_l = st_layout(A(ssm_lam_re)[0]); lam_im_l = st_layout(A(ssm_lam_im)[0])
    logdt_l = st_layout(np.broadcast_to(A(ssm_log_dt)[0][:, None], (32, 64)))
    bre_l = st_layout(A(ssm_b_re)[0]); bim_l = st_layout(A(ssm_b_im)[0])
    cre_l = st_layout(np.ascontiguousarray(A(ssm_c_re)[0].transpose(0, 2, 1)))
    cim_l = st_layout(np.ascontiguousarray(A(ssm_c_im)[0].transpose(0, 2, 1)))
    p_idx = np.arange(128)
    mask_g2 = (p_idx[:, None] // 64 == np.arange(2)[None, :]).astype(f32)
    mask_bd = (p_idx[:, None] // 32 == p_idx[None, :] // 32).astype(f32)
    m_own = (p_idx[:, None] <= p_idx[None, :]).astype(f32)
    m_prev = (p_idx[:, None] > p_idx[None, :]).astype(f32)
    rep4 = lambda m: np.ascontiguousarray(np.tile(m, (1, 4))).astype(bf)
    inv_freq = (500000.0 ** (-(np.arange(8, dtype=f32) * 2.0 / 16))).astype(f32)
    bc128 = lambda v: np.ascontiguousarray(np.broadcast_to(A(v).reshape(1, -1), (128, 1024)))
    common = dict(
        g1b=bc128(norm1_g), g2b=bc128(norm2_g), gfb=bc128(norm_f_g), w_in=w_in_p,
        bgate_l=np.ascontiguousarray(A(b_gate)[0].reshape(16, 128).T), sink_l=sink_l,
        lam_re_l=lam_re_l, lam_im_l=lam_im_l, logdt_l=logdt_l, bre_l=bre_l, bim_l=bim_l, cre_l=cre_l, cim_l=cim_l,
        d_l=np.ascontiguousarray(A(ssm_d)[0].reshape(4, 128).T), w_glu=A(w_glu)[0],
        bglu_l=np.ascontiguousarray(A(b_glu)[0].reshape(4, 128).T),
        w_ba=w_ba_p, w_bs=A(w_branch_ssm)[0], w_out=A(w_out)[0], w_fg=A(w_ffn_gate)[0], w_fu=A(w_ffn_up)[0], w_fd=A(w_ffn_down)[0],
        ident_bf=np.eye(128, dtype=f32).astype(bf), ident_f=np.eye(128, dtype=f32), mask_g2=mask_g2, mask_bd=mask_bd,
        mk_own=rep4(m_own), mk_prev=rep4(m_prev),
    )
    skw = A(state_k_win)[0].reshape(128, 128, 128); svw = A(state_v_win)[0].reshape(128, 128, 128)
    sre = A(state_ssm_re)[0]; sim = A(state_ssm_im)[0]
    in_maps = []
    for c in range(8):
        b, hf = c // 2, c % 2
        pos = np.zeros((18, 128), dtype=f32)
        pos[:16] = hf * 2048 + np.arange(16)[:, None] * 128 + np.arange(128)[None, :]
        pos[16] = hf * 2048 - 128 + np.arange(128)
        pos[17] = 8192.0
        ang = pos[:, :, None] * inv_freq[None, None, :]
        m = dict(common)
        m.update(
            xo=np.ascontiguousarray(x_prompt[b, hf * 2048:(hf + 1) * 2048]),
            xp=np.ascontiguousarray(x_prompt[b, 0:2048]) if hf else np.zeros((2048, 1024), f32),
            xs=np.ascontiguousarray(x_sample[c * 16:(c + 1) * 16, 0]),
            kwin=np.ascontiguousarray(skw[c * 16:(c + 1) * 16]), vwin=np.ascontiguousarray(svw[c * 16:(c + 1) * 16]),
            sre_l=np.ascontiguousarray(np.moveaxis(st_layout(np.moveaxis(sre[c * 16:(c + 1) * 16], 0, 2)), 2, 2)),
            sim_l=np.ascontiguousarray(st_layout(np.moveaxis(sim[c * 16:(c + 1) * 16], 0, 2))),
            mk_prev0=rep4(m_prev * float(hf)),
            ropec=np.ascontiguousarray(np.cos(ang).astype(f32).transpose(1, 0, 2)),
            ropes=np.ascontiguousarray(np.sin(ang).astype(f32).transpose(1, 0, 2)),
        )
        in_maps.append(m)
    return in_maps


def kernel(**inputs):
    in_maps = make_inputs(**inputs)
    if "nc" not in _NC:
        _NC["nc"] = build()
    res = run_bass_kernel_spmd(_NC["nc"], in_maps, core_ids=list(range(8)))
    R = res.results
    f32 = np.float32

    def un_st(a):
        a = a.reshape((2, 64, 16) + a.shape[2:])
        return np.moveaxis(a, 2, 0).reshape((32, 64) + a.shape[3:])

    y_prompt = np.stack([np.concatenate([R[2 * b]["y"], R[2 * b + 1]["y"]], axis=0) for b in range(4)])
    y_sample = np.concatenate([R[c]["ys"] for c in range(8)], axis=0)[:, None, :]
    kwp = np.stack([R[2 * b + 1]["kwp"].reshape(128, 2, 64) for b in range(4)])[None]
    vwp = np.stack([R[2 * b + 1]["vwp"].reshape(128, 2, 64) for b in range(4)])[None]
    hre = np.stack([un_st(R[2 * b + 1]["hre"]) for b in range(4)])[None]
    him = np.stack([un_st(R[2 * b + 1]["him"]) for b in range(4)])[None]
    kws = np.concatenate([R[c]["kws"].reshape(16, 128, 2, 64) for c in range(8)], axis=0)[None]
    vws = np.concatenate([R[c]["vws"].reshape(16, 128, 2, 64) for c in range(8)], axis=0)[None]
    sres = np.concatenate([np.moveaxis(un_st(R[c]["sres"]), 2, 0) for c in range(8)], axis=0)[None]
    sims = np.concatenate([np.moveaxis(un_st(R[c]["sims"]), 2, 0) for c in range(8)], axis=0)[None]
    outs = (y_prompt, y_sample, kwp, vwp, hre, him, kws, vws, sres, sims)
    return tuple(np.ascontiguousarray(o, dtype=f32) for o in outs)
```

```python
import contextlib
import math
import numpy as np
import ml_dtypes
import concourse.bass as bass
import concourse.mybir as mybir
from concourse.bass_utils import run_bass_kernel_spmd

F32 = mybir.dt.float32
BF16 = mybir.dt.bfloat16
AF = mybir.ActivationFunctionType
ALU = mybir.AluOpType

D = 1024
TOK = 2048
NS = 16
NCOL = TOK + NS
DFF = 2816
NFT = 22
EPS = 1e-5


class Buf:
    __slots__ = ("w", "r")

    def __init__(self):
        self.w = None
        self.r = []


class _Op:
    __slots__ = ("fn", "deps", "chan", "val", "ms", "alld", "seg", "cost", "lat", "idx", "eng")

    def __init__(self, fn, deps, chan):
        self.fn = fn
        self.deps = deps
        self.chan = chan
        self.val = 0
        self.ms = 0


LOOKAHEAD = 600


class _ProbeIns:
    def then_inc(self, *a, **k):
        return self


class _ProbeEng:
    def __init__(self):
        self.kind = None
        self.kw = None
        self.args = None

    def __getattr__(self, name):
        def f(*args, **kw):
            self.kind = name
            self.kw = kw
            self.args = args
            return _ProbeIns()
        return f


def _free(ap):
    n = 1
    for d in ap.shape[1:]:
        n *= int(d)
    return n


def _estimate(eng, fn):
    p = _ProbeEng()
    try:
        fn(p)
        kw, args, kind = p.kw, p.args, p.kind
        if kind == "dma_start":
            out = kw.get("out")
            nbytes = _free(out) * int(out.shape[0]) * (2 if "bfloat16" in str(out.dtype) else 4)
            if eng == "pool":
                nbytes *= 2
            return 0.06, 2.0 + nbytes / 200e3
        if kind == "matmul":
            n = _free(kw["rhs"])
            c = 0.045 + max(n, 64) / 2400.0
            return c, c + 0.1
        if kind == "transpose":
            return 0.1, 0.2
        out = kw.get("out", args[0] if args else None)
        n = _free(out)
        if kind == "tensor_tensor_scan":
            n *= 2
        if eng == "act":
            c = 0.19 + n / 1200.0
        else:
            c = 0.16 + n / 960.0
        return c, c + 0.05
    except Exception:
        return None


class Prog:
    ENG = ("pe", "act", "dve", "pool", "sp")

    def __init__(self):
        self.ops = {e: [] for e in self.ENG}
        self.chan_cnt = {}
        self.chan_last = {}
        self.seg = 0

    def op(self, eng, fn, reads=(), writes=(), chan=None, n=512, lat=None):
        idx = len(self.ops[eng])
        me = (eng, idx)
        alld = set()
        for b in reads:
            if b.w is not None:
                alld.add(b.w)
        for b in writes:
            if b.w is not None:
                alld.add(b.w)
            for r in b.r:
                alld.add(r)
        if chan is not None and chan in self.chan_last:
            alld.add(self.chan_last[chan])
        alld.discard(me)
        raw = {b.w for b in reads if b.w is not None}
        deps = set()
        for d in alld:
            dop = self.ops[d[0]][d[1]]
            same_compute = (d[0] == eng and dop.chan is None and chan is None)
            if same_compute and eng == "pe":
                continue
            deps.add(d)
        o = _Op(fn, deps, chan)
        o.alld = alld
        o.seg = self.seg
        o.eng = eng
        o.idx = idx
        est = _estimate(eng, fn) if fn is not None else (0.0, 0.0)
        if est is None:
            est = (0.06, 2.5) if chan is not None else (0.3, 0.3)
        o.cost, o.lat = est
        if chan is not None:
            self.chan_cnt[chan] = self.chan_cnt.get(chan, 0) + 16
            o.val = self.chan_cnt[chan]
            self.chan_last[chan] = me
        self.ops[eng].append(o)
        for b in reads:
            b.r.append(me)
        for b in writes:
            b.w = me
            b.r = []
        return me

    def fence(self):
        last = []
        for e in self.ENG:
            for i in range(len(self.ops[e]) - 1, -1, -1):
                if self.ops[e][i].chan is None and self.ops[e][i].fn is not None:
                    last.append((e, i))
                    break
        last += list(self.chan_last.values())
        self.seg += 1
        for e in self.ENG:
            idx = len(self.ops[e])
            o = _Op(None, {d for d in last if d != (e, idx)}, None)
            o.alld = set(); o.seg = self.seg; o.eng = e; o.idx = idx; o.cost = 0.0; o.lat = 0.0
            self.ops[e].append(o)
        self.seg += 1

    def schedule(self):
        order = {e: [] for e in self.ENG}
        fin = {}
        nseg = self.seg + 1
        segops = [{e: [] for e in self.ENG} for _ in range(nseg)]
        for e in self.ENG:
            for o in self.ops[e]:
                segops[o.seg][e].append(o)
        tbase = 0.0
        for sg in range(nseg):
            for e in self.ENG:
                for o in segops[sg][e]:
                    if o.fn is None:
                        nd = {d for d in o.deps if self.ops[d[0]][d[1]].chan is not None}
                        for e2 in self.ENG:
                            for i2 in reversed(order[e2]):
                                o2 = self.ops[e2][i2]
                                if o2.chan is None and o2.fn is not None:
                                    if (e2, i2) != (e, o.idx):
                                        nd.add((e2, i2))
                                    break
                        o.deps = nd
            pend = {e: list(segops[sg][e]) for e in self.ENG}
            efree = {e: tbase for e in self.ENG}
            total = sum(len(v) for v in pend.values())
            done = 0
            while done < total:
                best = None
                for e in self.ENG:
                    pl = pend[e]
                    for o in pl[:LOOKAHEAD]:
                        ok = True
                        rdy = efree[e]
                        for d in o.alld:
                            f = fin.get(d)
                            if f is None:
                                ok = False
                                break
                            if f > rdy:
                                rdy = f
                        if not ok:
                            continue
                        if best is None or rdy < best[0] - 1e-9:
                            best = (rdy, e, o)
                        break_early = (rdy <= efree[e] + 1e-9)
                        if break_early:
                            break
                assert best is not None, "scheduler deadlock"
                rdy, e, o = best
                pend[e].remove(o)
                cp = None; cpt = -1.0
                for d in o.alld:
                    if fin[d] > cpt:
                        cpt = fin[d]; cp = d
                if efree[e] >= cpt and order[e]:
                    cp = (e, order[e][-1])
                o.ms = 0
                self.crit = getattr(self, "crit", {})
                self.crit[(e, o.idx)] = (cp, rdy, rdy + o.lat)
                efree[e] = rdy + o.cost
                fin[(e, o.idx)] = rdy + o.lat
                order[e].append(o.idx)
                done += 1
            tbase = max([tbase] + [fin[(e, o.idx)] for e in self.ENG for o in segops[sg][e]])
            self.seg_end = getattr(self, 'seg_end', []) + [round(tbase, 1)]
        self.est_us = tbase
        return order

    def emit(self, nc):
        order = self.schedule()
        needed = {e: set() for e in self.ENG}
        for e in self.ENG:
            for o in self.ops[e]:
                for (de, di) in o.deps:
                    if self.ops[de][di].chan is None:
                        needed[de].add(di)
        for e in self.ENG:
            c = 0
            for i in order[e]:
                o = self.ops[e][i]
                if o.chan is None and i in needed[e]:
                    c += 1
                    o.ms = c
                    o.val = c
        with contextlib.ExitStack() as st:
            esem = {e: st.enter_context(nc.semaphore("s_" + e)) for e in self.ENG}
            csem = {c: st.enter_context(nc.semaphore("c_" + str(c))) for c in self.chan_cnt}
            block = st.enter_context(nc.Block())
            prog = self

            def run(engname, eng):
                waited = {}
                for i in order[engname]:
                    o = prog.ops[engname][i]
                    for (de, di) in sorted(o.deps):
                        d = prog.ops[de][di]
                        if d.chan is not None:
                            sem, key = csem[d.chan], ("c", d.chan)
                        else:
                            sem, key = esem[de], ("e", de)
                        if waited.get(key, 0) >= d.val:
                            continue
                        eng.wait_ge(sem, d.val)
                        waited[key] = d.val
                    if o.fn is None:
                        continue
                    ins = o.fn(eng)
                    if o.chan is not None:
                        ins.then_inc(csem[o.chan], 16)
                    elif o.ms:
                        ins.then_inc(esem[engname], 1)
                if engname == "sp":
                    for c, v in prog.chan_cnt.items():
                        if waited.get(("c", c), 0) < v:
                            eng.wait_ge(csem[c], v)

            block.tensor(lambda e: run("pe", e))
            block.scalar(lambda e: run("act", e))
            block.vector(lambda e: run("dve", e))
            block.gpsimd(lambda e: run("pool", e))
            block.sync(lambda e: run("sp", e))


class _Stop(Exception):
    pass


STOP = [None]
DUMPS = []


class TB:
    def __init__(self, t, b=None):
        self.t = t
        self.b = b if b is not None else Buf()

    def __getitem__(self, k):
        return self.t[k]


def build():
    nc = bass.Bass("TRN2", target_bir_lowering=False)
    P = Prog()

    def din(name, shape, dt=F32):
        return nc.dram_tensor(name, list(shape), dt, kind="ExternalInput").ap()

    def dout(name, shape, dt=F32):
        return nc.dram_tensor(name, list(shape), dt, kind="ExternalOutput").ap()

    xo = din("xo", [TOK, D]); xp = din("xp", [TOK, D]); xs = din("xs", [NS, D])
    kwin = din("kwin", [NS, 128, 128]); vwin = din("vwin", [NS, 128, 128])
    sre_l = din("sre_l", [128, 16, NS]); sim_l = din("sim_l", [128, 16, NS])
    mk_own = din("mk_own", [128, 512], BF16); mk_prev = din("mk_prev", [128, 512], BF16)
    mk_prev0 = din("mk_prev0", [128, 512], BF16)
    ropec = din("ropec", [128, 18, 8]); ropes = din("ropes", [128, 18, 8])
    g1b = din("g1b", [128, D]); g2b = din("g2b", [128, D]); gfb = din("gfb", [128, D])
    w_in = din("w_in", [D, 3328]); bgate_l = din("bgate_l", [128, 16]); sink_l = din("sink_l", [128, 4])
    lam_re_l = din("lam_re_l", [128, 16]); lam_im_l = din("lam_im_l", [128, 16]); logdt_l = din("logdt_l", [128, 16])
    bre_l = din("bre_l", [128, 16, 16]); bim_l = din("bim_l", [128, 16, 16])
    cre_l = din("cre_l", [128, 16, 16]); cim_l = din("cim_l", [128, 16, 16])
    d_l = din("d_l", [128, 4]); w_glu = din("w_glu", [512, 512]); bglu_l = din("bglu_l", [128, 4])
    w_ba = din("w_ba", [512, D]); w_bs = din("w_bs", [512, D]); w_out = din("w_out", [D, D])
    w_fg = din("w_fg", [D, DFF]); w_fu = din("w_fu", [D, DFF]); w_fd = din("w_fd", [DFF, D])
    ident_bf = din("ident_bf", [128, 128], BF16); ident_f = din("ident_f", [128, 128])
    mask_g2 = din("mask_g2", [128, 2]); mask_bd = din("mask_bd", [128, 128])

    y_o = dout("y", [TOK, D]); ys_o = dout("ys", [NS, D])
    kwp_o = dout("kwp", [128, 128]); vwp_o = dout("vwp", [128, 128])
    hre_o = dout("hre", [128, 16]); him_o = dout("him", [128, 16])
    kws_o = dout("kws", [NS, 128, 128]); vws_o = dout("vws", [NS, 128, 128])
    sres_o = dout("sres", [128, 16, NS]); sims_o = dout("sims", [128, 16, NS])

    with contextlib.ExitStack() as st:
        def sb(name, shape, dt=F32):
            return TB(st.enter_context(nc.sbuf_tensor(name, list(shape), dt)))

        def op(eng, fn, r=(), w=(), chan=None):
            return P.op(eng, fn, [x.b for x in r], [x.b for x in w], chan)

        def chk(k):
            if STOP[0] == k:
                raise _Stop()

        def dump(name, ap, shape, dt=F32):
            if STOP[0] is None:
                return
            d = nc.dram_tensor("dbg_" + name, list(shape), dt, kind="ExternalOutput").ap()
            DUMPS.append("dbg_" + name)
            P.fence()
            P.op("sp", lambda e: e.dma_start(out=d, in_=ap), [], [], chan="dbg_" + name)

        def body():
            PS = [TB(st.enter_context(nc.psum_tensor("ps%d" % i, [128, 512], F32))) for i in range(6)]
            PTB = [TB(st.enter_context(nc.psum_tensor("pt%d" % i, [128, 1024], BF16))) for i in range(2)]
            rr = {"ps": 0, "pt": 0, "ld": 0}

            def nps():
                rr["ps"] = (rr["ps"] + 1) % len(PS)
                return PS[rr["ps"]]

            def npt():
                rr["pt"] = (rr["pt"] + 1) % 2
                return PTB[rr["pt"]]

            def load(dst, src, r=(), q="sp"):
                rr["ld"] += 1
                ch = "ld%d" % (rr["ld"] % 16)
                op(q, lambda e: e.dma_start(out=dst[:], in_=src), r, [dst], chan=ch)

            identb = sb("identb", [128, 128], BF16); identf = sb("identf", [128, 128])
            mg2 = sb("mg2", [128, 2]); mbd = sb("mbd", [128, 128])
            mko = sb("mko", [128, 512], BF16); mkp = sb("mkp", [128, 512], BF16); mkp0 = sb("mkp0", [128, 512], BF16)
            rc_t = sb("rc_t", [128, 18, 8]); rs_t = sb("rs_t", [128, 18, 8])
            g1t = sb("g1t", [128, D]); g2t = sb("g2t", [128, D]); gft = sb("gft", [128, D])
            bgt = sb("bgt", [128, 16]); esk = sb("esk", [128, 4]); dlt = sb("dlt", [128, 4]); bglt = sb("bglt", [128, 4])
            ones_bf = sb("ones_bf", [128, 64], BF16)
            for dst, src in ((identb, ident_bf), (identf, ident_f), (mg2, mask_g2), (mbd, mask_bd), (mko, mk_own),
                             (mkp, mk_prev), (mkp0, mk_prev0), (rc_t, ropec), (rs_t, ropes), (g1t, g1b), (g2t, g2b),
                             (gft, gfb), (bgt, bgate_l), (esk, sink_l), (dlt, d_l), (bglt, bglu_l)):
                load(dst, src)
            op("dve", lambda e: e.memset(ones_bf[:], 1.0), [], [ones_bf])
            op("act", lambda e: e.activation(out=esk[:], in_=esk[:], func=AF.Exp), [esk], [esk])
            chk(-1)

            ssmT = sb("ssmT", [128, 4, NCOL], BF16)
            kT_all = sb("kT_all", [128, TOK + 128], BF16)
            v_all = sb("v_all", [128, 17, 128], BF16)
            wqk = sb("wqk", [128, 8, 768], BF16)
            op("pool", lambda e: e.dma_start(out=wqk[:, :, 0:512], in_=w_in[:, 0:512].rearrange("(k p) c -> p k c", p=128)), [], [wqk], chan="wq0")
            op("pool", lambda e: e.dma_start(out=wqk[:, :, 512:768], in_=w_in[:, 512:768].rearrange("(k p) c -> p k c", p=128)), [wqk], [wqk], chan="wq1")

            ARW = 38800
            arena = st.enter_context(nc.sbuf_tensor("arena", [128, ARW], F32))
            apos = {"o": 0}

            def carve(shape, dt=F32):
                n = int(np.prod(shape[1:]))
                words = n if dt == F32 else (n + 1) // 2
                o = apos["o"]
                apos["o"] = o + words
                assert apos["o"] <= ARW, apos["o"]
                v = arena[:, o:o + words]
                if dt != F32:
                    v = v.bitcast(dt)
                if len(shape) == 3:
                    v = v.rearrange("p (a b) -> p a b", b=shape[2])
                elif len(shape) == 4:
                    v = v.rearrange("p (a b c) -> p a b c", b=shape[2], c=shape[3])
                elif len(shape) == 5:
                    v = v.rearrange("p (a b c d) -> p a b c d", b=shape[2], c=shape[3], d=shape[4])
                return TB(v)

            WB = carve([128, 4, 8, 2, 128], BF16)
            WD = carve([128, 8, 16, 2, 32], BF16)
            KT = carve([128, 4, 8, 128], BF16)
            UTR = carve([128, 16, 64]); UTI = carve([128, 16, 64]); RT = carve([128, 16, 64])
            LBR = carve([128, 16]); LBI = carve([128, 16]); R8 = carve([128, 16])
            CARR = carve([128, 16]); CARI = carve([128, 16])
            wglu_t = carve([128, 4, 512], BF16)
            wu = carve([128, 8, 512], BF16)
            op("pool", lambda e: e.dma_start(out=wu[:], in_=w_in[:, 768:1280].rearrange("(k p) c -> p k c", p=128)), [], [wu], chan="wq2")
            op("pool", lambda e: e.dma_start(out=wglu_t[:], in_=w_glu.rearrange("(k p) c -> p k c", p=128)), [wu], [wglu_t], chan="wq2")
            apos_keep = apos["o"]

            C0 = Buf()

            def cvec(shape=(128, 16)):
                t = carve(list(shape)); t.b = C0
                return t

            def dv(fn):
                P.op("dve", fn, [C0], [C0])

            def av(fn):
                P.op("act", fn, [C0], [C0])

            def TT(o, a, b, o_):
                dv(lambda e: e.tensor_tensor(out=o, in0=a, in1=b, op=o_))

            lamr = cvec(); lami = cvec(); ldt = cvec()
            brel = cvec((128, 16, 16)); biml = cvec((128, 16, 16)); crel = cvec((128, 16, 16)); ciml = cvec((128, 16, 16))
            for dst, src in ((lamr, lam_re_l), (lami, lam_im_l), (ldt, logdt_l), (brel, bre_l), (biml, bim_l), (crel, cre_l), (ciml, cim_l)):
                load(dst, src)
            chk(-0.9)
            dtv = cvec(); are = cvec(); aim = cvec(); mag = cvec(); cr = cvec(); ci = cvec(); t1 = cvec(); t2 = cvec(); t3 = cvec()
            av(lambda e: e.activation(out=dtv[:], in_=ldt[:], func=AF.Exp))
            TT(are[:], lamr[:], dtv[:], ALU.mult)
            TT(aim[:], lami[:], dtv[:], ALU.mult)
            av(lambda e: e.activation(out=mag[:], in_=are[:], func=AF.Exp))
            av(lambda e: e.activation(out=R8[:], in_=are[:], func=AF.Exp, scale=8.0))
            av(lambda e: e.activation(out=ci[:], in_=aim[:], func=AF.Sin, scale=1.0 / 16))
            av(lambda e: e.activation(out=t1[:], in_=aim[:], func=AF.Sin, scale=1.0 / 32))
            TT(t1[:], t1[:], t1[:], ALU.mult)
            dv(lambda e: e.tensor_scalar(out=cr[:], in0=t1[:], scalar1=-2.0, scalar2=1.0, op0=ALU.mult, op1=ALU.add))
            for _ in range(4):
                TT(t1[:], cr[:], cr[:], ALU.mult)
                TT(t2[:], ci[:], ci[:], ALU.mult)
                TT(t3[:], cr[:], ci[:], ALU.mult)
                TT(cr[:], t1[:], t2[:], ALU.subtract)
                dv(lambda e: e.tensor_scalar(out=ci[:], in0=t3[:], scalar1=2.0, scalar2=None, op0=ALU.mult))
            TT(LBR[:], mag[:], cr[:], ALU.mult)
            TT(LBI[:], mag[:], ci[:], ALU.mult)
            chk(-0.8)
            den = cvec(); nr = cvec(); cfr = cvec(); cfi = cvec()
            TT(t1[:], lamr[:], lamr[:], ALU.mult)
            TT(t2[:], lami[:], lami[:], ALU.mult)
            TT(den[:], t1[:], t2[:], ALU.add)
            dv(lambda e: e.reciprocal(out=den[:], in_=den[:]))
            dv(lambda e: e.tensor_scalar(out=nr[:], in0=LBR[:], scalar1=-1.0, scalar2=None, op0=ALU.add))
            TT(t1[:], nr[:], lamr[:], ALU.mult)
            TT(t2[:], LBI[:], lami[:], ALU.mult)
            TT(t1[:], t1[:], t2[:], ALU.add)
            TT(cfr[:], t1[:], den[:], ALU.mult)
            TT(t1[:], LBI[:], lamr[:], ALU.mult)
            TT(t2[:], nr[:], lami[:], ALU.mult)
            TT(t1[:], t1[:], t2[:], ALU.subtract)
            TT(cfi[:], t1[:], den[:], ALU.mult)
            bbR = cvec((128, 16, 16)); bbI = cvec((128, 16, 16)); u1 = cvec((128, 16, 16)); u2 = cvec((128, 16, 16))

            def bc_h(v):
                return v[:].unsqueeze(2).broadcast_to([128, 16, 16])

            TT(u1[:], brel[:], bc_h(cfr), ALU.mult); TT(u2[:], biml[:], bc_h(cfi), ALU.mult); TT(bbR[:], u1[:], u2[:], ALU.subtract)
            TT(u1[:], biml[:], bc_h(cfr), ALU.mult); TT(u2[:], brel[:], bc_h(cfi), ALU.mult); TT(bbI[:], u1[:], u2[:], ALU.add)
            LPR = cvec((128, 9, 16)); LPI = cvec((128, 9, 16))
            dv(lambda e: e.memset(LPR[:, 0, :], 1.0)); dv(lambda e: e.memset(LPI[:, 0, :], 0.0))
            for k in range(8):
                TT(t1[:], LPR[:, k, :], LBR[:], ALU.mult); TT(t2[:], LPI[:, k, :], LBI[:], ALU.mult)
                TT(LPR[:, k + 1, :], t1[:], t2[:], ALU.subtract)
                TT(t1[:], LPR[:, k, :], LBI[:], ALU.mult); TT(t2[:], LPI[:, k, :], LBR[:], ALU.mult)
                TT(LPI[:, k + 1, :], t1[:], t2[:], ALU.add)
            dv(lambda e: e.reciprocal(out=t3[:], in_=R8[:]))
            def mq(v):
                return v.rearrange("p (q m) -> p m q", m=4)
            TT(UTR[:, :, 0].rearrange("p (m q) -> p m q", q=4), mq(LPR[:, 8, :]), mq(t3[:]), ALU.mult)
            TT(UTI[:, :, 0].rearrange("p (m q) -> p m q", q=4), mq(LPI[:, 8, :]), mq(t3[:]), ALU.mult)
            w1 = cvec((128, 16, 32)); w2 = cvec((128, 16, 32))
            n = 1
            while n < 64:
                def bc_n(T_, n=n):
                    return T_[:, :, n - 1:n].broadcast_to([128, 16, n])
                TT(w1[:, :, 0:n], UTR[:, :, 0:n], bc_n(UTR), ALU.mult); TT(w2[:, :, 0:n], UTI[:, :, 0:n], bc_n(UTI), ALU.mult)
                TT(UTR[:, :, n:2 * n], w1[:, :, 0:n], w2[:, :, 0:n], ALU.subtract)
                TT(w1[:, :, 0:n], UTR[:, :, 0:n], bc_n(UTI), ALU.mult); TT(w2[:, :, 0:n], UTI[:, :, 0:n], bc_n(UTR), ALU.mult)
                TT(UTI[:, :, n:2 * n], w1[:, :, 0:n], w2[:, :, 0:n], ALU.add)
                n *= 2
            dv(lambda e: e.tensor_copy(out=RT[:].rearrange("p (m q) c -> p m q c", q=4), in_=mq(R8[:]).unsqueeze(3).broadcast_to([128, 4, 4, 64])))
            dv(lambda e: e.memset(RT[:, :, 0:1], 0.0))
            dv(lambda e: e.memset(CARR[:], 0.0)); dv(lambda e: e.memset(CARI[:], 0.0))
            chk(-0.7)
            BPR = cvec((128, 8, 16, 16)); BPI = cvec((128, 8, 16, 16)); X1 = cvec((128, 8, 16, 16)); X2 = cvec((128, 8, 16, 16))

            def bc_k(v):
                return v[:].unsqueeze(1).broadcast_to([128, 8, 16, 16])

            def bc_p(v, lo):
                return v[:, lo:lo + 8, :].unsqueeze(3).broadcast_to([128, 8, 16, 16])

            TT(X1[:], bc_k(bbR), bc_p(LPR, 0), ALU.mult); TT(X2[:], bc_k(bbI), bc_p(LPI, 0), ALU.mult); TT(BPR[:], X1[:], X2[:], ALU.subtract)
            TT(X1[:], bc_k(bbI), bc_p(LPR, 0), ALU.mult); TT(X2[:], bc_k(bbR), bc_p(LPI, 0), ALU.mult); TT(BPI[:], X1[:], X2[:], ALU.add)
            chk(-0.6)
            EBR = cvec((128, 128, 2, 16)); EBI = cvec((128, 128, 2, 16))
            mg2b = mg2[:].unsqueeze(1).unsqueeze(3).broadcast_to([128, 128, 2, 16])
            TT(EBR[:], BPR[:].rearrange("p k s h -> p (k s) h").unsqueeze(2).broadcast_to([128, 128, 2, 16]), mg2b, ALU.mult)
            TT(EBI[:], BPI[:].rearrange("p k s h -> p (k s) h").unsqueeze(2).broadcast_to([128, 128, 2, 16]), mg2b, ALU.mult)
            EBRv = EBR[:].rearrange("p (k q m) g h -> p k q (m g h)", k=8, q=4)
            EBIv = EBI[:].rearrange("p (k q m) g h -> p k q (m g h)", k=8, q=4)
            CER = cvec((128, 16, 2, 16)); CEIN = cvec((128, 16, 2, 16))
            mg2c = mg2[:].unsqueeze(1).unsqueeze(3).broadcast_to([128, 16, 2, 16])
            TT(CER[:], crel[:].unsqueeze(2).broadcast_to([128, 16, 2, 16]), mg2c, ALU.mult)
            TT(CEIN[:], ciml[:].unsqueeze(2).broadcast_to([128, 16, 2, 16]), mg2c, ALU.mult)
            dv(lambda e: e.tensor_scalar(out=CEIN[:], in0=CEIN[:], scalar1=-1.0, scalar2=None, op0=ALU.mult))
            CERv = CER[:].rearrange("p (q m) g h -> p q (m g h)", q=4)
            CEINv = CEIN[:].rearrange("p (q m) g h -> p q (m g h)", q=4)
            CB = TB(None, C0)
            chk(-0.5)
            for q in range(4):
                for s in range(0, 8, 2):
                    ps = nps()
                    for j in range(2):
                        for ri, EV in enumerate((EBRv, EBIv)):
                            src = EV[:, 7 - (s + j), q, :]
                            dstp = ps[:, (j * 2 + ri) * 128:(j * 2 + ri + 1) * 128]
                            op("pe", lambda e, src=src, dstp=dstp: e.transpose(out=dstp, in_=src, identity=identf[:]), [CB, identf], [ps])
                    op("act", lambda e, ps=ps, q=q, s=s: e.activation(out=WB[:, q, s:s + 2, :, :].rearrange("p a b c -> p (a b c)"), in_=ps[:], func=AF.Copy), [ps], [WB])
            chk(-0.4)
            tmpk = cvec((128, 128))
            for q in range(4):
                for dl in range(8):
                    ps = nps()
                    op("pe", lambda e, ps=ps, q=q, dl=dl: e.matmul(ps[:, 0:128], lhsT=EBRv[:, dl, q, :], rhs=CERv[:, q, :], start=True, stop=False), [CB], [ps])
                    op("pe", lambda e, ps=ps, q=q, dl=dl: e.matmul(ps[:, 0:128], lhsT=EBIv[:, dl, q, :], rhs=CEINv[:, q, :], start=False, stop=True), [CB], [ps])
                    if dl == 0:
                        op("dve", lambda e, ps=ps, tmpk=tmpk: e.tensor_tensor(out=tmpk[:], in0=ps[:, 0:128], in1=mbd[:], op=ALU.mult), [ps, mbd, CB], [CB])
                        op("dve", lambda e, q=q, tmpk=tmpk: e.scalar_tensor_tensor(out=KT[:, q, 0, :], in0=identf[:], scalar=dlt[:, q:q + 1], in1=tmpk[:], op0=ALU.mult, op1=ALU.add), [CB, identf, dlt], [KT])
                    else:
                        op("dve", lambda e, ps=ps, q=q, dl=dl: e.tensor_tensor(out=KT[:, q, dl, :], in0=ps[:, 0:128], in1=mbd[:], op=ALU.mult), [ps, mbd], [KT])
            chk(-0.3)
            TT(X1[:], bc_k(crel), bc_p(LPR, 1), ALU.mult); TT(X2[:], bc_k(ciml), bc_p(LPI, 1), ALU.mult); TT(BPR[:], X1[:], X2[:], ALU.subtract)
            TT(X1[:], bc_k(crel), bc_p(LPI, 1), ALU.mult); TT(X2[:], bc_k(ciml), bc_p(LPR, 1), ALU.mult); TT(BPI[:], X1[:], X2[:], ALU.add)
            dv(lambda e: e.tensor_scalar(out=BPI[:], in0=BPI[:], scalar1=-1.0, scalar2=None, op0=ALU.mult))
            for ri, BPx in enumerate((BPR, BPI)):
                P.op("dve", lambda e, ri=ri, BPx=BPx: e.tensor_tensor(
                    out=WD[:, :, :, ri, :].rearrange("p t s (g h) -> p (t s) g h", g=2),
                    in0=BPx[:].rearrange("p k s h -> p (k s) h").unsqueeze(2).broadcast_to([128, 128, 2, 16]),
                    in1=mg2b, op=ALU.mult), [C0, mg2.b], [WD.b])

            P.fence()
            chk(1)
            apos["o"] = apos_keep

            def rmsnorm_to_T(xt, rows, gt, dstT, col0, tmp_bf, ss, junk):
                op("dve", lambda e: e.memset(ss[0:rows, :], 0.0), [], [ss])
                op("act", lambda e: e.activation(out=junk[0:rows, :], in_=xt[0:rows, :], func=AF.Square, accum_out=ss[0:rows, :]), [xt, ss], [junk, ss])
                op("act", lambda e: e.activation(out=ss[0:rows, :], in_=ss[0:rows, :], func=AF.Sqrt, scale=1.0 / D, bias=EPS), [ss], [ss])
                op("dve", lambda e: e.reciprocal(out=ss[0:rows, :], in_=ss[0:rows, :]), [ss], [ss])
                op("dve", lambda e: e.scalar_tensor_tensor(out=tmp_bf[0:rows, :], in0=xt[0:rows, :], scalar=ss[0:rows, 0:1], in1=gt[0:rows, :], op0=ALU.mult, op1=ALU.mult), [xt, ss, gt], [tmp_bf])
                pt = npt()
                for kc in range(8):
                    op("pe", lambda e, kc=kc: e.transpose(out=pt[:, kc * 128:kc * 128 + rows], in_=tmp_bf[0:rows, kc * 128:(kc + 1) * 128], identity=identb[0:rows, 0:rows]), [tmp_bf, identb], [pt])
                op("act", lambda e: e.activation(out=dstT[:, 0:8, col0:col0 + rows], in_=pt[:].rearrange("p (k t) -> p k t", t=128)[:, :, 0:rows], func=AF.Copy), [pt], [dstT])
                return ss

            WRING = [None] * 4
            wr_i = {"i": 0}

            SCR = {}

            def scratch(key, n):
                if key not in SCR:
                    SCR[key] = (nc.dram_tensor("wscr_" + key, [128, n], BF16).ap(), TB(None))
                return SCR[key]

            def wload(src_ap, nk, ncols, key=None, ti=0):
                wr_i["i"] = (wr_i["i"] + 1) % 4
                slot = WRING[wr_i["i"]]
                flat = slot[:, 0:nk * ncols]
                view = flat.rearrange("p (k c) -> p k c", c=ncols)
                i = wr_i["i"]
                if key is None:
                    op("pool", lambda e: e.dma_start(out=view, in_=src_ap.rearrange("(k p) c -> p k c", p=128)), [], [slot], chan="wr%d" % i)
                    return TB(view, slot.b)
                scr, sb_ = scratch(key, nk * ncols)
                op("pool", lambda e: e.dma_start(out=flat, in_=scr), [sb_], [slot], chan="wr%d" % i)
                return TB(view, slot.b)

            def precast(src_ap, nk, ncols, key, off=0, total=None):
                scr, sb_ = scratch(key, total if total is not None else nk * ncols)
                dst = scr[:, off:off + nk * ncols].rearrange("p (k c) -> p k c", c=ncols)
                op("pool", lambda e: e.dma_start(out=dst, in_=src_ap.rearrange("(k p) c -> p k c", p=128)), [sb_], [sb_], chan="wcast")

            xst = [carve([128, D]) for _ in range(2)]
            xnb = [carve([128, D], BF16) for _ in range(2)]
            junk = carve([128, D], BF16); ssv = [carve([128, 1]) for _ in range(2)]
            xnT = [carve([128, 8, 528], BF16) for _ in range(2)]
            uT = [carve([128, 4, 528], BF16) for _ in range(2)]
            Hp = [carve([128, 2, 16, 65], BF16)] * 2
            mt = [carve([128, 256]) for _ in range(4)]
            bRm = [carve([128, 256]) for _ in range(4)]; bIm = [carve([128, 256]) for _ in range(4)]
            GR = carve([128, 256]); GI = carve([128, 256])
            yT = carve([128, 4, 528]); zT = yT; zTb = carve([128, 4, 528], BF16); sg = carve([128, 528])
            small = [carve([128, 16]) for _ in range(8)]
            h0r = carve([128, 16, NS]); h0i = carve([128, 16, NS]); h0b = carve([128, 2, 16, NS], BF16)
            hnr = carve([128, 16, NS]); hni = carve([128, 16, NS]); s1 = carve([128, 16, NS]); s2 = carve([128, 16, NS])
            kvf = [carve([128, 256]) for _ in range(2)]
            qk_tok = [carve([128, 640], BF16) for _ in range(2)]
            rtmp = [carve([128, 10, 8]) for _ in range(4)]
            chk(0.4)
            load(h0r, sre_l); load(h0i, sim_l)
            chk(0.5)
            op("dve", lambda e: e.tensor_copy(out=h0b[:, 0, :, :], in_=h0r[:]), [h0r], [h0b])
            op("dve", lambda e: e.tensor_copy(out=h0b[:, 1, :, :], in_=h0i[:]), [h0i], [h0b])
            cnt = {"x": 0, "t": 0}
            hpB = [Buf() for _ in range(4)]

            def rope_kv(ps_kv, rows, ridx, kv_f):
                op("act", lambda e: e.activation(out=kv_f[0:rows, :], in_=ps_kv[0:rows, 0:256], func=AF.Copy), [ps_kv], [kv_f])
                oview = kv_f[0:rows, 0:128].rearrange("p (h d) -> p h d", d=64)
                rope_apply(oview, oview, rows, ridx, 2, [kv_f], kv_f)

            def rope_apply(src, dst, rows, ridx, nh, rbufs, dstb):
                cosb = rc_t[0:rows, ridx:ridx + 1, :].broadcast_to([rows, nh, 8])
                sinb = rs_t[0:rows, ridx:ridx + 1, :].broadcast_to([rows, nh, 8])
                a, b, c, d = rtmp
                x1 = src[:, :, 0:8]; x2 = src[:, :, 8:16]
                op("dve", lambda e: e.tensor_tensor(out=a[0:rows, 0:nh, :], in0=x1, in1=cosb, op=ALU.mult), rbufs + [rc_t], [a])
                op("dve", lambda e: e.tensor_tensor(out=b[0:rows, 0:nh, :], in0=x2, in1=sinb, op=ALU.mult), rbufs + [rs_t], [b])
                op("dve", lambda e: e.tensor_tensor(out=c[0:rows, 0:nh, :], in0=x2, in1=cosb, op=ALU.mult), rbufs + [rc_t], [c])
                op("dve", lambda e: e.tensor_tensor(out=d[0:rows, 0:nh, :], in0=x1, in1=sinb, op=ALU.mult), rbufs + [rs_t], [d])
                op("dve", lambda e: e.tensor_tensor(out=dst[:, :, 0:8], in0=a[0:rows, 0:nh, :], in1=b[0:rows, 0:nh, :], op=ALU.subtract), [a, b], [dstb])
                op("dve", lambda e: e.tensor_tensor(out=dst[:, :, 8:16], in0=c[0:rows, 0:nh, :], in1=d[0:rows, 0:nh, :], op=ALU.add), [c, d], [dstb])

            def ssm_tile(src_dram, ntok_sub, is_own, tile_i, sample=False):
                cnt["t"] += 1
                xT = xnT[cnt["t"] % 2]; u_t = uT[cnt["t"] % 2]; hp = Hp[cnt["t"] % 2]
                ncols = 16 if sample else 512
                subs = [(0, 16)] if sample else [(j * 128, 128) for j in range(4)]
                for (c0, rows) in subs:
                    cnt["x"] += 1
                    xt = xst[cnt["x"] % 2]; tb = xnb[cnt["x"] % 2]; ss = ssv[cnt["x"] % 2]
                    load(TB(xt[0:rows, :], xt.b), src_dram[c0:c0 + rows, :])
                    rmsnorm_to_T(xt, rows, g1t, xT, c0, tb, ss, junk)
                chk(2.01)
                for ct in range(4):
                    ps = nps()
                    for kc in range(8):
                        op("pe", lambda e, ps=ps, kc=kc, ct=ct: e.matmul(ps[:, 0:ncols], lhsT=wu[:, kc, ct * 128:(ct + 1) * 128], rhs=xT[:, kc, 0:ncols], start=(kc == 0), stop=(kc == 7)), [wu, xT], [ps])
                    op("act", lambda e, ps=ps, ct=ct: e.activation(out=u_t[:, ct, 0:ncols], in_=ps[:, 0:ncols], func=AF.Copy), [ps], [u_t])
                chk(2.02)
                if sample:
                    bps = []
                    for m in range(4):
                        ps = nps()
                        bps.append(ps)
                        for q in range(4):
                            for ri in range(2):
                                reg = q * 2 + ri
                                op("pe", lambda e, ps=ps, q=q, m=m, ri=ri, reg=reg: e.matmul(
                                    ps[:, reg * 16:(reg + 1) * 16], lhsT=WB[m * 32:(m + 1) * 32, q, 7, ri, :],
                                    rhs=u_t[m * 32:(m + 1) * 32, q, 0:16], start=True, stop=True, tile_position=(m * 32, 0)), [WB, u_t], [ps])
                    lbr_b = LBR[:].unsqueeze(2).broadcast_to([128, 16, 16]); lbi_b = LBI[:].unsqueeze(2).broadcast_to([128, 16, 16])
                    op("dve", lambda e: e.tensor_tensor(out=s1[:], in0=h0r[:], in1=lbr_b, op=ALU.mult), [h0r, LBR], [s1])
                    op("dve", lambda e: e.tensor_tensor(out=s2[:], in0=h0i[:], in1=lbi_b, op=ALU.mult), [h0i, LBI], [s2])
                    op("dve", lambda e: e.tensor_tensor(out=s1[:], in0=s1[:], in1=s2[:], op=ALU.subtract), [s1, s2], [s1])
                    for m in range(4):
                        bv = bps[m][:, 0:128].rearrange("p (q r b) -> p q r b", r=2, b=16)
                        op("dve", lambda e, m=m, bv=bv: e.tensor_tensor(out=hnr[:].rearrange("p (q m) b -> p m q b", m=4)[:, m], in0=s1[:].rearrange("p (q m) b -> p m q b", m=4)[:, m], in1=bv[:, :, 0, :], op=ALU.add), [s1, bps[m]], [hnr])
                    op("dve", lambda e: e.tensor_tensor(out=s1[:], in0=h0i[:], in1=lbr_b, op=ALU.mult), [h0i, LBR, hnr], [s1])
                    op("dve", lambda e: e.tensor_tensor(out=s2[:], in0=h0r[:], in1=lbi_b, op=ALU.mult), [h0r, LBI], [s2])
                    op("dve", lambda e: e.tensor_tensor(out=s1[:], in0=s1[:], in1=s2[:], op=ALU.add), [s1, s2], [s1])
                    for m in range(4):
                        bv = bps[m][:, 0:128].rearrange("p (q r b) -> p q r b", r=2, b=16)
                        op("dve", lambda e, m=m, bv=bv: e.tensor_tensor(out=hni[:].rearrange("p (q m) b -> p m q b", m=4)[:, m], in0=s1[:].rearrange("p (q m) b -> p m q b", m=4)[:, m], in1=bv[:, :, 1, :], op=ALU.add), [s1, bps[m]], [hni])
                    op("sp", lambda e: e.dma_start(out=sres_o, in_=hnr[:]), [hnr], [], chan="o_sr")
                    op("sp", lambda e: e.dma_start(out=sims_o, in_=hni[:]), [hni], [], chan="o_si")
                    for q in range(4):
                        ps = nps()
                        op("pe", lambda e, ps=ps, q=q: e.matmul(ps[:, 0:16], lhsT=KT[:, q, 0, :], rhs=u_t[:, q, 0:16], start=True, stop=False), [KT, u_t], [ps])
                        for m in range(4):
                            for ri in range(2):
                                last = (ri == 1)
                                op("pe", lambda e, ps=ps, q=q, m=m, ri=ri, last=last: e.matmul(
                                    ps[m * 32:(m + 1) * 32, 0:16], lhsT=WD[:, 0, q * 4 + m, ri, :], rhs=h0b[:, ri, q * 4 + m, :],
                                    start=False, stop=last, tile_position=(0, m * 32)), [WD, h0b], [ps])
                        op("act", lambda e, ps=ps, q=q: e.activation(out=yT[:, q, 0:16], in_=ps[:, 0:16], func=AF.Copy), [ps], [yT])
                    glu(16, TOK)
                    return
                yield xT
                vps = [nps() for _ in range(4)]
                for q in range(4):
                    for ri in range(2):
                        reg = q * 2 + ri
                        for s in range(8):
                            for m in range(4):
                                ps = vps[m]
                                op("pe", lambda e, ps=ps, reg=reg, q=q, m=m, s=s, ri=ri: e.matmul(
                                    ps[:, reg * 64:(reg + 1) * 64], lhsT=WB[m * 32:(m + 1) * 32, q, s, ri, :],
                                    rhs=u_t[m * 32:(m + 1) * 32, q, 0:512].rearrange("p (c s) -> p s c", s=8)[:, s, :], start=(s == 0), stop=(s == 7), tile_position=(m * 32, 0)), [WB, u_t], [ps])
                chk(2.03)
                op("dve", lambda e: e.tensor_copy(out=hp[:, 0, :, 0:1], in_=CARR[:].unsqueeze(2)), [CARR], [hp] + [TB(None, b_) for b_ in hpB])
                op("dve", lambda e: e.tensor_copy(out=hp[:, 1, :, 0:1], in_=CARI[:].unsqueeze(2)), [CARI], [hp] + [TB(None, b_) for b_ in hpB])

                def a3(t):
                    return t[:, 0:256].rearrange("p (q c) -> p q c", c=64)

                def stm(X, m):
                    return X[:].rearrange("p (q m) -> p m q", m=4)[:, m, :]

                a, b, c, d = mt[0], mt[1], mt[2], mt[3]
                k1, k2, k3, k4 = small[0], small[1], small[2], small[3]
                for m in range(4):
                    V = vps[m]
                    Vv = V[:].rearrange("p (q r c) -> p q r c", r=2, c=64)
                    VR = Vv[:, :, 0, :]; VI = Vv[:, :, 1, :]
                    ur = UTR[:, m * 4:(m + 1) * 4, :]; ui = UTI[:, m * 4:(m + 1) * 4, :]
                    bR = bRm[m]; bI = bIm[m]
                    op("dve", lambda e, VR=VR, ur=ur: e.tensor_tensor(out=a3(a), in0=VR, in1=ur, op=ALU.mult), [V, UTR], [a])
                    op("dve", lambda e, VI=VI, ui=ui: e.tensor_tensor(out=a3(b), in0=VI, in1=ui, op=ALU.mult), [V, UTI], [b])
                    op("dve", lambda e, VI=VI, ur=ur: e.tensor_tensor(out=a3(c), in0=VI, in1=ur, op=ALU.mult), [V, UTR], [c])
                    op("dve", lambda e, VR=VR, ui=ui: e.tensor_tensor(out=a3(d), in0=VR, in1=ui, op=ALU.mult), [V, UTI], [d])
                    op("dve", lambda e, bR=bR: e.tensor_tensor(out=bR[:, 0:256], in0=a[:, 0:256], in1=b[:, 0:256], op=ALU.add), [a, b], [bR])
                    op("dve", lambda e, bI=bI: e.tensor_tensor(out=bI[:, 0:256], in0=c[:, 0:256], in1=d[:, 0:256], op=ALU.subtract), [c, d], [bI])
                for m in range(4):
                    ur = UTR[:, m * 4:(m + 1) * 4, :]; ui = UTI[:, m * 4:(m + 1) * 4, :]
                    rt = RT[:, m * 4:(m + 1) * 4, :].rearrange("p q c -> p (q c)")
                    bR = bRm[m]; bI = bIm[m]
                    op("dve", lambda e, m=m: e.tensor_tensor(out=k1[:, 0:4], in0=stm(CARR, m), in1=stm(R8, m), op=ALU.mult), [CARR, R8], [k1])
                    op("dve", lambda e, m=m: e.tensor_tensor(out=k2[:, 0:4], in0=stm(CARI, m), in1=stm(R8, m), op=ALU.mult), [CARI, R8], [k2])
                    op("dve", lambda e, bR=bR: e.tensor_tensor(out=a3(bR)[:, :, 0:1], in0=a3(bR)[:, :, 0:1], in1=k1[:, 0:4].unsqueeze(2), op=ALU.add), [bR, k1], [bR])
                    op("dve", lambda e, bI=bI: e.tensor_tensor(out=a3(bI)[:, :, 0:1], in0=a3(bI)[:, :, 0:1], in1=k2[:, 0:4].unsqueeze(2), op=ALU.add), [bI, k2], [bI])
                    op("dve", lambda e, rt=rt, bR=bR: e.tensor_tensor_scan(out=GR[:, 0:256], data0=rt, data1=bR[:, 0:256], initial=0.0, op0=ALU.mult, op1=ALU.add), [bR, RT], [GR])
                    op("dve", lambda e, rt=rt, bI=bI: e.tensor_tensor_scan(out=GI[:, 0:256], data0=rt, data1=bI[:, 0:256], initial=0.0, op0=ALU.mult, op1=ALU.add), [bI, RT], [GI])
                    if is_own:
                        hpr = hp[:, 0, :, :].rearrange("p (q m) c -> p m q c", m=4)[:, m, :, 1:65]
                        hpi = hp[:, 1, :, :].rearrange("p (q m) c -> p m q c", m=4)[:, m, :, 1:65]
                        op("dve", lambda e, ur=ur: e.tensor_tensor(out=a3(a), in0=a3(GR), in1=ur, op=ALU.mult), [GR, UTR], [a])
                        op("dve", lambda e, ui=ui: e.tensor_tensor(out=a3(b), in0=a3(GI), in1=ui, op=ALU.mult), [GI, UTI], [b])
                        op("dve", lambda e, hpr=hpr: e.tensor_tensor(out=hpr, in0=a3(a), in1=a3(b), op=ALU.subtract), [a, b], [TB(None, hpB[m])])
                        op("dve", lambda e, ur=ur: e.tensor_tensor(out=a3(c), in0=a3(GI), in1=ur, op=ALU.mult), [GI, UTR], [c])
                        op("dve", lambda e, ui=ui: e.tensor_tensor(out=a3(d), in0=a3(GR), in1=ui, op=ALU.mult), [GR, UTI], [d])
                        op("dve", lambda e, hpi=hpi: e.tensor_tensor(out=hpi, in0=a3(c), in1=a3(d), op=ALU.add), [c, d], [TB(None, hpB[m])])
                    g63r = a3(GR)[:, :, 63]; g63i = a3(GI)[:, :, 63]
                    u63r = UTR[:, m * 4:(m + 1) * 4, 63]; u63i = UTI[:, m * 4:(m + 1) * 4, 63]
                    op("dve", lambda e, g63r=g63r, u63r=u63r: e.tensor_tensor(out=k1[:, 0:4], in0=g63r, in1=u63r, op=ALU.mult), [GR, UTR], [k1])
                    op("dve", lambda e, g63i=g63i, u63i=u63i: e.tensor_tensor(out=k2[:, 0:4], in0=g63i, in1=u63i, op=ALU.mult), [GI, UTI], [k2])
                    op("dve", lambda e, g63i=g63i, u63r=u63r: e.tensor_tensor(out=k3[:, 0:4], in0=g63i, in1=u63r, op=ALU.mult), [GI, UTR], [k3])
                    op("dve", lambda e, g63r=g63r, u63i=u63i: e.tensor_tensor(out=k4[:, 0:4], in0=g63r, in1=u63i, op=ALU.mult), [GR, UTI], [k4])
                    op("dve", lambda e, m=m: e.tensor_tensor(out=stm(CARR, m), in0=k1[:, 0:4], in1=k2[:, 0:4], op=ALU.subtract), [k1, k2, hp], [CARR])
                    op("dve", lambda e, m=m: e.tensor_tensor(out=stm(CARI, m), in0=k3[:, 0:4], in1=k4[:, 0:4], op=ALU.add), [k3, k4, hp], [CARI])
                chk(2.04)
                if not is_own:
                    return xT
                for q in range(4):
                    ps = nps()
                    for tau in range(8):
                        for dl in range(tau + 1):
                            op("pe", lambda e, ps=ps, q=q, tau=tau, dl=dl: e.matmul(
                                ps[:, tau * 64:(tau + 1) * 64], lhsT=KT[:, q, dl, :], rhs=u_t[:, q, 0:512].rearrange("p (c s) -> p s c", s=8)[:, tau - dl, :], start=(dl == 0), stop=(dl == tau)), [KT, u_t], [ps])
                    op("act", lambda e, ps=ps, q=q: e.activation(out=yT[:, q, 0:512].rearrange("p (c t) -> p t c", t=8), in_=ps[:].rearrange("p (t c) -> p t c", c=64), func=AF.Copy), [ps], [yT])
                pds = [nps() for _ in range(4)]
                for m in range(4):
                    for q in range(4):
                        ps = pds[q]
                        for tau in range(8):
                            for ri in range(2):
                                op("pe", lambda e, ps=ps, q=q, tau=tau, m=m, ri=ri: e.matmul(
                                    ps[m * 32:(m + 1) * 32, tau * 64:(tau + 1) * 64], lhsT=WD[:, tau, q * 4 + m, ri, :], rhs=hp[:, ri, q * 4 + m, 0:64],
                                    start=(ri == 0), stop=(ri == 1), tile_position=(0, m * 32)), [WD, TB(None, hpB[m])], [ps])
                for q in range(4):
                    ps = pds[q]
                    op("dve", lambda e, ps=ps, q=q: e.tensor_tensor(out=yT[:, q, 0:512].rearrange("p (c t) -> p t c", t=8), in0=ps[:].rearrange("p (t c) -> p t c", c=64), in1=yT[:, q, 0:512].rearrange("p (c t) -> p t c", t=8), op=ALU.add), [ps, yT], [yT])
                glu(512, tile_i * 512)
                return xT

            def glu(ncols, col0):
                op("act", lambda e: e.activation(out=zT[:, :, 0:ncols], in_=yT[:, :, 0:ncols], func=AF.Gelu), [yT], [zT])
                op("dve", lambda e: e.tensor_copy(out=zTb[:, :, 0:ncols], in_=zT[:, :, 0:ncols]), [zT], [zTb])
                for ct in range(4):
                    ps = nps()
                    for kc in range(4):
                        op("pe", lambda e, ps=ps, kc=kc, ct=ct: e.matmul(ps[:, 0:ncols], lhsT=wglu_t[:, kc, ct * 128:(ct + 1) * 128], rhs=zTb[:, kc, 0:ncols], start=(kc == 0), stop=(kc == 3)), [wglu_t, zTb], [ps])
                    op("act", lambda e, ps=ps, ct=ct: e.activation(out=sg[:, 0:ncols], in_=ps[:, 0:ncols], func=AF.Sigmoid, bias=bglt[:, ct:ct + 1]), [ps, bglt], [sg])
                    op("dve", lambda e, ct=ct: e.tensor_tensor(out=ssmT[:, ct, col0:col0 + ncols], in0=zT[:, ct, 0:ncols], in1=sg[:, 0:ncols], op=ALU.mult), [zT, sg], [ssmT])

            chk(0)
            for dg in range(2):
                precast(w_in[:, 1280 + dg * 512:1280 + (dg + 1) * 512], 8, 512, "g1_%d" % dg)
                precast(w_in[:, 2304 + dg * 512:2304 + (dg + 1) * 512], 8, 512, "g2_%d" % dg)
                precast(w_ba[:, dg * 512:(dg + 1) * 512], 4, 512, "br_%d" % dg, 0, 4096)
                precast(w_bs[:, dg * 512:(dg + 1) * 512], 4, 512, "br_%d" % dg, 2048, 4096)
            for hf_ in range(2):
                precast(w_out[:, hf_ * 512:(hf_ + 1) * 512], 8, 512, "wo_%d" % hf_)
            for fg in range(6):
                nf = 4 if fg < 5 else 2
                precast(w_fg[:, fg * 512:fg * 512 + nf * 128], 8, nf * 128, "fg_%d" % fg)
                precast(w_fu[:, fg * 512:fg * 512 + nf * 128], 8, nf * 128, "fu_%d" % fg)
            for hf_ in range(2):
                for fgp in range(3):
                    nk = 8 if fgp < 2 else 6
                    precast(w_fd[fgp * 1024:fgp * 1024 + nk * 128, hf_ * 512:(hf_ + 1) * 512], nk, 512, "fd_%d_%d" % (hf_, fgp))
            def finish(g):
                for _ in g:
                    pass

            tiles_ = [(xp[ti * 512:(ti + 1) * 512, :], False, ti) for ti in range(4)] + [(xo[ti * 512:(ti + 1) * 512, :], True, ti) for ti in range(4)]
            gens = []
            for k_ in range(4):
                g_ = ssm_tile(tiles_[k_][0], 4, tiles_[k_][1], tiles_[k_][2])
                xT_last = next(g_)
                if gens:
                    finish(gens[-1])
                gens.append(g_)
            def kv_block(xT, c0, rows, ridx, blk, kout=None, vout=None, kcol=None):
                psk = nps()
                for kc in range(8):
                    op("pe", lambda e, kc=kc: e.matmul(psk[0:rows, 0:256], lhsT=xT[:, kc, c0:c0 + rows], rhs=wqk[:, kc, 512:768], start=(kc == 0), stop=(kc == 7)), [xT, wqk], [psk])
                kf = kvf[blk % 2]
                rope_kv(psk, rows, ridx, kf)
                return kf

            kf = kv_block(xT_last, 384, 128, 16, 0)
            qkt = qk_tok[0]
            op("act", lambda e: e.activation(out=qkt[:, 512:640], in_=kf[:, 0:128], func=AF.Copy), [kf], [qkt])
            op("act", lambda e: e.activation(out=v_all[:, 0, :], in_=kf[:, 128:256], func=AF.Copy), [kf], [v_all])
            pt = npt()
            op("pe", lambda e: e.transpose(out=pt[:, 0:128], in_=qkt[:, 512:640], identity=identb[:]), [qkt, identb], [pt])
            op("act", lambda e: e.activation(out=kT_all[:, 0:128], in_=pt[:, 0:128], func=AF.Copy), [pt], [kT_all])
            chk(3)
            for k_ in range(4, 8):
                g_ = ssm_tile(tiles_[k_][0], 4, tiles_[k_][1], tiles_[k_][2])
                next(g_)
                finish(gens[-1])
                gens.append(g_)
            finish(gens[-1])
            op("sp", lambda e: e.dma_start(out=hre_o, in_=CARR[:]), [CARR], [], chan="o_hr")
            op("sp", lambda e: e.dma_start(out=him_o, in_=CARI[:]), [CARI], [], chan="o_hi")
            finish(ssm_tile(xs, 1, True, 0, sample=True))

            P.fence()
            apos["o"] = 0

            for i_ in range(4):
                WRING[i_] = carve([128, 4096], BF16)
            x1 = carve([128, 5, D])
            xn_tok = [carve([128, D], BF16) for _ in range(2)]
            junk = carve([128, D], BF16); ssv = [carve([128, 1]) for _ in range(4)]
            actT = carve([128, 8, 528], BF16)
            h2T = carve([128, 8, 528], BF16)
            xs_f = [carve([128, D]) for _ in range(2)]
            hB = [Buf() for _ in range(5)]
            qT = carve([128, 4, 528], BF16)
            attnT = carve([128, 4, 528], BF16)
            mergedT = carve([128, 8, 528], BF16)
            alias_o = apos["o"]
            hT = carve([128, NFT, 528], BF16)
            alias_end = apos["o"]
            qk_tok = [carve([128, 640], BF16) for _ in range(2)]
            kvf = [carve([128, 256]) for _ in range(2)]
            rtmp = [carve([128, 10, 8]) for _ in range(4)]
            pexp = [carve([128, 512], BF16) for _ in range(4)]
            qf = carve([128, 512])
            denr = carve([128, 512])
            sg1 = carve([128, 528]); sg2 = carve([128, 528]); mtmp = carve([128, 528]); mtmp2 = carve([128, 528])
            silu_t = [carve([128, 528])] * 2
            yout = [carve([128, D])] * 2
            keep_o = apos["o"]
            apos["o"] = alias_o
            KSb = carve([128, NS, 128], BF16); VSb = carve([128, NS, 128], BF16); KST = carve([128, NS, 128], BF16)
            assert apos["o"] <= alias_end
            apos["o"] = keep_o
            pes = carve([128, 2, NS, 4], BF16)
            cnt = {"x": 0}
            x1B = [Buf() for _ in range(5)]; aTB = [Buf() for _ in range(5)]; qTB = [Buf() for _ in range(5)]
            atB = [Buf() for _ in range(5)]; mgB = [Buf() for _ in range(8)]; hTB = [Buf() for _ in range(NFT)]
            kTB = [Buf() for _ in range(18)]; vB = [Buf() for _ in range(18)]
            bkw = TB(None); bvw = TB(None)

            def Dp(bufs):
                return [TB(None, b) for b in bufs]

            def cgB(lst, lc, n):
                return Dp([lst[k] for k in range(5) if k * 128 < lc + n and k * 128 + (128 if k < 4 else 16) > lc])

            HTALL = Dp(hTB)

            def tile_main(ti):
                has_s = (ti == 3)
                subs = [(ti * 512 + j * 128, 128, j, ti * 4 + j) for j in range(4)]
                if has_s:
                    subs.append((TOK, 16, 4, 17))
                cgs = [(ti * 512, 512, 0)] + ([(TOK, 16, 512)] if has_s else [])
                for (g0, rows, j, ridx) in subs:
                    cnt["x"] += 1
                    src = xo[g0:g0 + rows, :] if g0 < TOK else xs
                    xf = xs_f[cnt["x"] % 2]
                    load(TB(xf[0:rows, :], xf.b), src)
                    rmsnorm_to_T(xf, rows, g1t, TB(actT.t, aTB[j]), j * 128, xn_tok[cnt["x"] % 2], ssv[cnt["x"] % 4], junk)
                chk(5.1)
                for (g0, rows, j, ridx) in subs:
                    cnt["x"] += 1
                    lc = j * 128
                    psq = nps(); psk = nps()
                    for kc in range(8):
                        op("pe", lambda e, kc=kc, psq=psq, lc=lc, rows=rows: e.matmul(psq[0:rows, :], lhsT=actT[:, kc, lc:lc + rows], rhs=wqk[:, kc, 0:512], start=(kc == 0), stop=(kc == 7)), [TB(None, aTB[j]), wqk], [psq])
                    for kc in range(8):
                        op("pe", lambda e, kc=kc, psk=psk, lc=lc, rows=rows: e.matmul(psk[0:rows, 0:256], lhsT=actT[:, kc, lc:lc + rows], rhs=wqk[:, kc, 512:768], start=(kc == 0), stop=(kc == 7)), [TB(None, aTB[j]), wqk], [psk])
                    chk(5.11)
                    qkt = qk_tok[cnt["x"] % 2]; kf = kvf[cnt["x"] % 2]
                    op("act", lambda e, psq=psq, rows=rows: e.activation(out=qf[0:rows, :], in_=psq[0:rows, :], func=AF.Copy), [psq], [qf])
                    op("dve", lambda e, qkt=qkt, rows=rows: e.tensor_copy(out=qkt[0:rows, 0:512], in_=qf[0:rows, :]), [qf], [qkt])
                    chk(5.115)
                    rope_apply(qf[0:rows, :].rearrange("p (h d) -> p h d", d=64), qkt[0:rows, 0:512].rearrange("p (h d) -> p h d", d=64), rows, ridx, 8, [qf], qkt)
                    chk(5.12)
                    rope_kv(psk, rows, ridx, kf)
                    chk(5.13)
                    op("act", lambda e, qkt=qkt, kf=kf, rows=rows: e.activation(out=qkt[0:rows, 512:640], in_=kf[0:rows, 0:128], func=AF.Copy), [kf], [qkt])
                    if g0 < TOK:
                        blk = g0 // 128 + 1
                        op("act", lambda e, kf=kf, blk=blk: e.activation(out=v_all[:, blk, :], in_=kf[:, 128:256], func=AF.Copy), [kf], [TB(None, vB[blk])])
                        if g0 == TOK - 128:
                            op("sp", lambda e, kf=kf: e.dma_start(out=kwp_o, in_=kf[:, 0:128]), [kf], [], chan="o_kw")
                            op("sp", lambda e, kf=kf: e.dma_start(out=vwp_o, in_=kf[:, 128:256]), [kf], [], chan="o_vw")
                    else:
                        op("sp", lambda e, kf=kf: e.dma_start(out=kws_o[:, 127, :], in_=kf[0:NS, 0:128]), [kf], [bkw], chan="o_ks")
                        op("sp", lambda e, kf=kf: e.dma_start(out=vws_o[:, 127, :], in_=kf[0:NS, 128:256]), [kf], [bvw], chan="o_vs")
                        op("sp", lambda e: e.dma_start(out=kws_o[:, 0:127, :], in_=kwin[:, 1:128, :]), [], [], chan="o_ks2")
                        op("sp", lambda e: e.dma_start(out=vws_o[:, 0:127, :], in_=vwin[:, 1:128, :]), [], [], chan="o_vs2")
                        op("pool", lambda e: e.dma_start(out=KSb[0:112, :, :], in_=kwin[:, 1:113, :].rearrange("b j e -> j b e")), [], [KSb] + HTALL, chan="l_ks")
                        op("pool", lambda e: e.dma_start(out=VSb[0:112, :, :], in_=vwin[:, 1:113, :].rearrange("b j e -> j b e")), [], [VSb] + HTALL, chan="l_vs")
                        op("pool", lambda e: e.dma_start(out=KSb[112:127, :, :], in_=kwin[:, 113:128, :].rearrange("b j e -> j b e")), [KSb], [KSb], chan="l_ks")
                        op("pool", lambda e: e.dma_start(out=VSb[112:127, :, :], in_=vwin[:, 113:128, :].rearrange("b j e -> j b e")), [VSb], [VSb], chan="l_vs")
                        op("pool", lambda e: e.dma_start(out=KSb[127:128, :, :], in_=kws_o[:, 127:128, :].rearrange("b j e -> j b e")), [KSb, bkw], [KSb], chan="l_ks")
                        op("pool", lambda e: e.dma_start(out=VSb[127:128, :, :], in_=vws_o[:, 127:128, :].rearrange("b j e -> j b e")), [VSb, bvw], [VSb], chan="l_vs")
                    chk(5.14)
                    pt = npt()
                    for t in range(5):
                        op("pe", lambda e, t=t, pt=pt, qkt=qkt, rows=rows: e.transpose(out=pt[:, t * 128:t * 128 + rows], in_=qkt[0:rows, t * 128:(t + 1) * 128], identity=identb[0:rows, 0:rows]), [qkt, identb], [pt])
                    op("act", lambda e, pt=pt, lc=lc, rows=rows: e.activation(out=qT[:, 0:4, lc:lc + rows], in_=pt[:, 0:512].rearrange("p (k t) -> p k t", t=128)[:, :, 0:rows], func=AF.Copy), [pt], [TB(None, qTB[j])])
                    if g0 < TOK:
                        op("act", lambda e, pt=pt, g0=g0: e.activation(out=kT_all[:, 128 + g0:256 + g0], in_=pt[:, 512:640], func=AF.Copy), [pt], [TB(None, kTB[g0 // 128 + 1])])
                chk(5.2)
                for (g0, rows, j, ridx) in subs:
                    if g0 >= TOK:
                        continue
                    lc = j * 128
                    blk = g0 // 128 + 1
                    pe_t = []
                    for kv in range(2):
                        for which in range(2):
                            kb = blk - which
                            ps = nps()
                            op("pe", lambda e, ps=ps, kv=kv, kb=kb, lc=lc: e.matmul(
                                ps[:].rearrange("p (h q) -> p h q", q=128), lhsT=kT_all[kv * 64:(kv + 1) * 64, kb * 128:(kb + 1) * 128],
                                rhs=qT[kv * 64:(kv + 1) * 64, 0:4, lc:lc + 128], start=True, stop=True, tile_position=(kv * 64, 0)), [TB(None, kTB[kb]), TB(None, qTB[j])], [ps])
                            pe_ = pexp[kv * 2 + which]
                            op("act", lambda e, ps=ps, pe_=pe_: e.activation(out=pe_[:], in_=ps[:], func=AF.Exp, scale=0.125), [ps], [pe_])
                            mk = mko if which == 0 else (mkp0 if blk == 1 else mkp)
                            op("dve", lambda e, pe_=pe_, mk=mk: e.tensor_tensor(out=pe_[:], in0=pe_[:], in1=mk[:], op=ALU.mult), [pe_, mk], [pe_])
                            pe_t.append((kv, kb, pe_))
                    psn = nps(); psd = nps()
                    for idx, (kv, kb, pe_) in enumerate(pe_t):
                        first = (idx % 2 == 0); lastk = (idx % 2 == 1)
                        op("pe", lambda e, kv=kv, kb=kb, pe_=pe_, first=first, lastk=lastk, psn=psn: e.matmul(
                            psn[kv * 64:(kv + 1) * 64, :], lhsT=v_all[:, kb, kv * 64:(kv + 1) * 64], rhs=pe_[:], start=first, stop=lastk, tile_position=(0, kv * 64)), [TB(None, vB[kb]), pe_], [psn])
                        op("pe", lambda e, kv=kv, pe_=pe_, first=first, lastk=lastk, psd=psd: e.matmul(
                            psd[kv * 64:(kv + 1) * 64, :], lhsT=ones_bf[:, 0:64], rhs=pe_[:], start=first, stop=lastk, tile_position=(0, kv * 64)), [ones_bf, pe_], [psd])
                    op("dve", lambda e, psd=psd: e.tensor_tensor(out=denr[:].rearrange("p (h q) -> p h q", q=128), in0=psd[:].rearrange("p (h q) -> p h q", q=128), in1=esk[:].unsqueeze(2).broadcast_to([128, 4, 128]), op=ALU.add), [psd, esk], [denr])
                    op("dve", lambda e: e.reciprocal(out=denr[:], in_=denr[:]), [denr], [denr])
                    op("dve", lambda e, psn=psn, lc=lc: e.tensor_tensor(out=attnT[:, :, lc:lc + 128], in0=psn[:].rearrange("p (h q) -> p h q", q=128), in1=denr[:].rearrange("p (h q) -> p h q", q=128), op=ALU.mult), [psn, denr], [TB(None, atB[j])])
                chk(5.3)
                if has_s:
                    for hb in range(2):
                        pt = npt()
                        for bb in range(8):
                            b_ = hb * 8 + bb
                            op("pe", lambda e, pt=pt, bb=bb, b_=b_: e.transpose(out=pt[:, bb * 128:(bb + 1) * 128], in_=KSb[:, b_, :], identity=identb[:]), [KSb, identb], [pt])
                        op("act", lambda e, pt=pt, hb=hb: e.activation(out=KST[:, hb * 8:(hb + 1) * 8, :].rearrange("p b k -> p (b k)"), in_=pt[:], func=AF.Copy), [pt], [KST] + HTALL)
                    for kv in range(2):
                        ps = nps()
                        for b_ in range(NS):
                            op("pe", lambda e, ps=ps, kv=kv, b_=b_: e.matmul(
                                ps[:, b_ * 4:(b_ + 1) * 4], lhsT=KST[kv * 64:(kv + 1) * 64, b_, :],
                                rhs=qT[kv * 64:(kv + 1) * 64, 0:4, 512 + b_], start=True, stop=True, tile_position=(kv * 64, 0)), [KST, TB(None, qTB[4])] + HTALL, [ps])
                        op("act", lambda e, ps=ps, kv=kv: e.activation(out=pes[:, kv, :, :].rearrange("p b c -> p (b c)"), in_=ps[:, 0:64], func=AF.Exp, scale=0.125), [ps], [pes])
                    psn = nps(); psd = nps()
                    for kv in range(2):
                        for b_ in range(NS):
                            op("pe", lambda e, kv=kv, b_=b_, psn=psn: e.matmul(psn[kv * 64:(kv + 1) * 64, b_ * 4:(b_ + 1) * 4], lhsT=VSb[:, b_, kv * 64:(kv + 1) * 64], rhs=pes[:, kv, b_, :], start=True, stop=True, tile_position=(0, kv * 64)), [VSb, pes] + HTALL, [psn])
                            op("pe", lambda e, kv=kv, b_=b_, psd=psd: e.matmul(psd[kv * 64:(kv + 1) * 64, b_ * 4:(b_ + 1) * 4], lhsT=ones_bf[:, 0:64], rhs=pes[:, kv, b_, :], start=True, stop=True, tile_position=(0, kv * 64)), [ones_bf, pes], [psd])
                    dv_ = denr[:, 0:64].rearrange("p (b h) -> p b h", h=4)
                    op("dve", lambda e, psd=psd: e.tensor_tensor(out=dv_, in0=psd[:, 0:64].rearrange("p (b h) -> p b h", h=4), in1=esk[:].unsqueeze(1).broadcast_to([128, NS, 4]), op=ALU.add), [psd, esk], [denr])
                    op("dve", lambda e: e.reciprocal(out=denr[:, 0:64], in_=denr[:, 0:64]), [denr], [denr])
                    op("dve", lambda e, psn=psn: e.tensor_tensor(out=attnT[:, :, 512:528].rearrange("p h b -> p b h"), in0=psn[:, 0:64].rearrange("p (b h) -> p b h", h=4), in1=dv_, op=ALU.mult), [psn, denr], [TB(None, atB[4])])
                yield "F1"
                for dg in range(2):
                    wg1 = wload(w_in[:, 1280 + dg * 512:1280 + (dg + 1) * 512], 8, 512, "g1_%d" % dg, ti)
                    wg2 = wload(w_in[:, 2304 + dg * 512:2304 + (dg + 1) * 512], 8, 512, "g2_%d" % dg, ti)
                    wr_i["i"] = (wr_i["i"] + 1) % 4
                    slot = WRING[wr_i["i"]]
                    vba = slot[:, 0:2048].rearrange("p (k c) -> p k c", c=512); vbs = slot[:, 2048:4096].rearrange("p (k c) -> p k c", c=512)
                    i_ = wr_i["i"]
                    scr_b, sbb_ = scratch("br_%d" % dg, 4096)
                    op("pool", lambda e, slot=slot, scr_b=scr_b: e.dma_start(out=slot[:, 0:4096], in_=scr_b), [sbb_], [slot], chan="wr%d" % i_)
                    wbr = TB(None, slot.b)
                    for dd in range(4):
                        dt_ = dg * 4 + dd
                        for (gc, n, lc) in cgs:
                            p1 = nps(); p2 = nps(); pa = nps(); pb = nps()
                            for kc in range(8):
                                op("pe", lambda e, kc=kc, p1=p1, dd=dd, lc=lc, n=n, wg1=wg1: e.matmul(p1[:, 0:n], lhsT=wg1[:, kc, dd * 128:(dd + 1) * 128], rhs=actT[:, kc, lc:lc + n], start=(kc == 0), stop=(kc == 7)), [wg1] + cgB(aTB, lc, n), [p1])
                            for kc in range(8):
                                op("pe", lambda e, kc=kc, p2=p2, dd=dd, lc=lc, n=n, wg2=wg2: e.matmul(p2[:, 0:n], lhsT=wg2[:, kc, dd * 128:(dd + 1) * 128], rhs=actT[:, kc, lc:lc + n], start=(kc == 0), stop=(kc == 7)), [wg2] + cgB(aTB, lc, n), [p2])
                            for kc in range(4):
                                op("pe", lambda e, kc=kc, pa=pa, dd=dd, lc=lc, n=n, vba=vba: e.matmul(pa[:, 0:n], lhsT=vba[:, kc, dd * 128:(dd + 1) * 128], rhs=attnT[:, kc, lc:lc + n], start=(kc == 0), stop=(kc == 3)), [wbr] + cgB(atB, lc, n), [pa])
                            for kc in range(4):
                                op("pe", lambda e, kc=kc, pb=pb, dd=dd, gc=gc, n=n, vbs=vbs: e.matmul(pb[:, 0:n], lhsT=vbs[:, kc, dd * 128:(dd + 1) * 128], rhs=ssmT[:, kc, gc:gc + n], start=(kc == 0), stop=(kc == 3)), [wbr, ssmT], [pb])
                            op("act", lambda e, p1=p1, dt_=dt_, n=n: e.activation(out=sg1[:, 0:n], in_=p1[:, 0:n], func=AF.Sigmoid, bias=bgt[:, dt_:dt_ + 1]), [p1, bgt], [sg1])
                            op("act", lambda e, p2=p2, dt_=dt_, n=n: e.activation(out=sg2[:, 0:n], in_=p2[:, 0:n], func=AF.Sigmoid, bias=bgt[:, 8 + dt_:9 + dt_]), [p2, bgt], [sg2])
                            op("dve", lambda e, pa=pa, n=n: e.tensor_tensor(out=mtmp[:, 0:n], in0=pa[:, 0:n], in1=sg1[:, 0:n], op=ALU.mult), [pa, sg1], [mtmp])
                            op("dve", lambda e, pb=pb, n=n: e.tensor_tensor(out=mtmp2[:, 0:n], in0=pb[:, 0:n], in1=sg2[:, 0:n], op=ALU.mult), [pb, sg2], [mtmp2])
                            op("dve", lambda e, dt_=dt_, lc=lc, n=n: e.tensor_tensor(out=mergedT[:, dt_, lc:lc + n], in0=mtmp[:, 0:n], in1=mtmp2[:, 0:n], op=ALU.add), [mtmp, mtmp2], [TB(None, mgB[dt_])])
                chk(5.4)
                yield "F2"
                wo = [wload(w_out[:, hf_ * 512:(hf_ + 1) * 512], 8, 512, "wo_%d" % hf_, ti) for hf_ in range(2)]
                for (g0, rows, j, ridx) in subs:
                    lc = j * 128
                    for hf_ in range(2):
                        ps = nps()
                        for kc in range(8):
                            op("pe", lambda e, kc=kc, ps=ps, lc=lc, rows=rows, hf_=hf_: e.matmul(ps[0:rows, :], lhsT=mergedT[:, kc, lc:lc + rows], rhs=wo[hf_][:, kc, :], start=(kc == 0), stop=(kc == 7)), [TB(None, mgB[kc]), wo[hf_]], [ps])
                        op("dve", lambda e, ps=ps, j=j, rows=rows, hf_=hf_: e.tensor_tensor(out=x1[0:rows, j, hf_ * 512:(hf_ + 1) * 512], in0=ps[0:rows, :], in1=x1[0:rows, j, hf_ * 512:(hf_ + 1) * 512], op=ALU.add), [ps, TB(None, x1B[j])], [TB(None, x1B[j])])
                for (g0, rows, j, ridx) in subs:
                    cnt["x"] += 1
                    rmsnorm_to_T(TB(x1[:, j, :], x1B[j]), rows, g2t, TB(h2T.t, hB[j]), j * 128, xn_tok[cnt["x"] % 2], ssv[cnt["x"] % 4], junk)
                chk(5.5)
                yield "B1"
                for fg in range(6):
                    nf = 4 if fg < 5 else 2
                    wg = wload(w_fg[:, fg * 512:fg * 512 + nf * 128], 8, nf * 128, "fg_%d" % fg, ti)
                    wu = wload(w_fu[:, fg * 512:fg * 512 + nf * 128], 8, nf * 128, "fu_%d" % fg, ti)
                    for ff in range(nf):
                        ft = fg * 4 + ff
                        for (gc, n, lc) in cgs:
                            pg = nps(); pu = nps()
                            for kc in range(8):
                                op("pe", lambda e, kc=kc, pg=pg, ff=ff, lc=lc, n=n, wg=wg: e.matmul(pg[:, 0:n], lhsT=wg[:, kc, ff * 128:(ff + 1) * 128], rhs=h2T[:, kc, lc:lc + n], start=(kc == 0), stop=(kc == 7)), [wg] + cgB(hB, lc, n), [pg])
                            for kc in range(8):
                                op("pe", lambda e, kc=kc, pu=pu, ff=ff, lc=lc, n=n, wu=wu: e.matmul(pu[:, 0:n], lhsT=wu[:, kc, ff * 128:(ff + 1) * 128], rhs=h2T[:, kc, lc:lc + n], start=(kc == 0), stop=(kc == 7)), [wu] + cgB(hB, lc, n), [pu])
                            sl = silu_t[ft % 2]
                            op("act", lambda e, pg=pg, sl=sl, n=n: e.activation(out=sl[:, 0:n], in_=pg[:, 0:n], func=AF.Silu), [pg], [sl])
                            op("dve", lambda e, pu=pu, sl=sl, ft=ft, lc=lc, n=n: e.tensor_tensor(out=hT[:, ft, lc:lc + n], in0=pu[:, 0:n], in1=sl[:, 0:n], op=ALU.mult), [pu, sl], [TB(None, hTB[ft])])
                chk(5.6)
                for hf_ in range(2):
                    for fgp in range(3):
                        nk = 8 if fgp < 2 else 6
                        wd = wload(w_fd[fgp * 1024:fgp * 1024 + nk * 128, hf_ * 512:(hf_ + 1) * 512], nk, 512, "fd_%d_%d" % (hf_, fgp), ti)
                        for si, (g0, rows, j, ridx) in enumerate(subs):
                            lc = j * 128
                            ps = nps()
                            for kc in range(nk):
                                ft = fgp * 8 + kc
                                op("pe", lambda e, ps=ps, kc=kc, ft=ft, lc=lc, rows=rows, wd=wd, nk=nk: e.matmul(ps[0:rows, :], lhsT=hT[:, ft, lc:lc + rows], rhs=wd[:, kc, :], start=(kc == 0), stop=(kc == nk - 1)), [TB(None, hTB[ft]), wd], [ps])
                            op("dve", lambda e, ps=ps, j=j, rows=rows, hf_=hf_: e.tensor_tensor(out=x1[0:rows, j, hf_ * 512:(hf_ + 1) * 512], in0=ps[0:rows, :], in1=x1[0:rows, j, hf_ * 512:(hf_ + 1) * 512], op=ALU.add), [ps, TB(None, x1B[j])], [TB(None, x1B[j])])
                for (g0, rows, j, ridx) in subs:
                    cnt["x"] += 1
                    ss = ssv[cnt["x"] % 4]; yo = yout[cnt["x"] % 2]
                    xt = TB(x1[:, j, :], x1B[j])
                    op("dve", lambda e, ss=ss, rows=rows: e.memset(ss[0:rows, :], 0.0), [], [ss])
                    op("act", lambda e, ss=ss, rows=rows, xt=xt: e.activation(out=junk[0:rows, :], in_=xt[0:rows, :], func=AF.Square, accum_out=ss[0:rows, :]), [xt, ss], [junk, ss])
                    op("act", lambda e, ss=ss, rows=rows: e.activation(out=ss[0:rows, :], in_=ss[0:rows, :], func=AF.Sqrt, scale=1.0 / D, bias=EPS), [ss], [ss])
                    op("dve", lambda e, ss=ss, rows=rows: e.reciprocal(out=ss[0:rows, :], in_=ss[0:rows, :]), [ss], [ss])
                    op("dve", lambda e, ss=ss, rows=rows, xt=xt, yo=yo: e.scalar_tensor_tensor(out=yo[0:rows, :], in0=xt[0:rows, :], scalar=ss[0:rows, 0:1], in1=gft[0:rows, :], op0=ALU.mult, op1=ALU.mult), [xt, ss, gft], [yo])
                    dst = y_o[g0:g0 + rows, :] if g0 < TOK else ys_o
                    op("sp", lambda e, yo=yo, rows=rows, dst=dst: e.dma_start(out=dst, in_=yo[0:rows, :]), [yo], [yo], chan="oy%d" % (cnt["x"] % 2))

            chk(5)
            def adv(g, tag):
                r = next(g, None)
                assert r == tag, (r, tag)

            def x1_load(ti):
                for j in range(4):
                    g0 = ti * 512 + j * 128
                    load(TB(x1[:, j, :], x1B[j]), xo[g0:g0 + 128, :])
                if ti == 3:
                    load(TB(x1[0:NS, 4, :], x1B[4]), xs)

            tg = [tile_main(ti) for ti in range(4)]
            adv(tg[0], "F1"); adv(tg[0], "F2")
            for ti in range(4):
                x1_load(ti)
                if ti < 3:
                    adv(tg[ti + 1], "F1")
                adv(tg[ti], "B1")
                if ti < 3:
                    adv(tg[ti + 1], "F2")
                adv(tg[ti], None)
                chk(6 + ti)
        try:
            body()
        except _Stop:
            pass
        P.emit(nc)
    return nc


_NC = {}


def make_inputs(x_prompt, x_sample, state_k_win, state_v_win, state_ssm_re, state_ssm_im,
           norm1_g, w_in, b_gate, attn_sinks, ssm_lam_re, ssm_lam_im, ssm_log_dt,
           ssm_b_re, ssm_b_im, ssm_c_re, ssm_c_im, ssm_d, w_glu, b_glu,
           w_branch_attn, w_branch_ssm, w_out, norm2_g, w_ffn_gate, w_ffn_up, w_ffn_down, norm_f_g):
    f32 = np.float32
    bf = ml_dtypes.bfloat16
    A = lambda a: np.ascontiguousarray(np.asarray(a), dtype=f32)
    x_prompt = A(x_prompt); x_sample = A(x_sample)
    perm = np.concatenate([np.r_[t * 64:(t + 1) * 64, (4 + t) * 64:(5 + t) * 64] for t in range(4)])
    w_in0 = A(w_in)[0]
    w_in_p = np.ascontiguousarray(np.concatenate([w_in0[:, :512][:, perm], w_in0[:, 512:]], axis=1))
    w_ba_p = np.ascontiguousarray(A(w_branch_attn)[0][perm, :])
    sinks = A(attn_sinks)[0]
    sink_l = np.ascontiguousarray(np.concatenate([np.tile(sinks[None, 0:4], (64, 1)), np.tile(sinks[None, 4:8], (64, 1))], axis=0))

    def st_layout(a):
        a = a.reshape((16, 2, 64) + a.shape[2:])
        return np.ascontiguousarray(np.moveaxis(a, 0, 2).reshape((128, 16) + a.shape[3:]))

    lam_re_l = st_layout(A(ssm_lam_re)[0]); lam_im_l = st_layout(A(ssm_lam_im)[0])
    logdt_l = st_layout(np.broadcast_to(A(ssm_log_dt)[0][:, None], (32, 64)))
    bre_l = st_layout(A(ssm_b_re)[0]); bim_l = st_layout(A(ssm_b_im)[0])
    cre_l = st_layout(np.ascontiguousarray(A(ssm_c_re)[0].transpose(0, 2, 1)))
    cim_l = st_layout(np.ascontiguousarray(A(ssm_c_im)[0].transpose(0, 2, 1)))
    p_idx = np.arange(128)
    mask_g2 = (p_idx[:, None] // 64 == np.arange(2)[None, :]).astype(f32)
    mask_bd = (p_idx[:, None] // 32 == p_idx[None, :] // 32).astype(f32)
    m_own = (p_idx[:, None] <= p_idx[None, :]).astype(f32)
    m_prev = (p_idx[:, None] > p_idx[None, :]).astype(f32)
    rep4 = lambda m: np.ascontiguousarray(np.tile(m, (1, 4))).astype(bf)
    inv_freq = (500000.0 ** (-(np.arange(8, dtype=f32) * 2.0 / 16))).astype(f32)
    bc128 = lambda v: np.ascontiguousarray(np.broadcast_to(A(v).reshape(1, -1), (128, 1024)))
    common = dict(
        g1b=bc128(norm1_g), g2b=bc128(norm2_g), gfb=bc128(norm_f_g), w_in=w_in_p,
        bgate_l=np.ascontiguousarray(A(b_gate)[0].reshape(16, 128).T), sink_l=sink_l,
        lam_re_l=lam_re_l, lam_im_l=lam_im_l, logdt_l=logdt_l, bre_l=bre_l, bim_l=bim_l, cre_l=cre_l, cim_l=cim_l,
        d_l=np.ascontiguousarray(A(ssm_d)[0].reshape(4, 128).T), w_glu=A(w_glu)[0],
        bglu_l=np.ascontiguousarray(A(b_glu)[0].reshape(4, 128).T),
        w_ba=w_ba_p, w_bs=A(w_branch_ssm)[0], w_out=A(w_out)[0], w_fg=A(w_ffn_gate)[0], w_fu=A(w_ffn_up)[0], w_fd=A(w_ffn_down)[0],
        ident_bf=np.eye(128, dtype=f32).astype(bf), ident_f=np.eye(128, dtype=f32), mask_g2=mask_g2, mask_bd=mask_bd,
        mk_own=rep4(m_own), mk_prev=rep4(m_prev),
    )
    skw = A(state_k_win)[0].reshape(128, 128, 128); svw = A(state_v_win)[0].reshape(128, 128, 128)
    sre = A(state_ssm_re)[0]; sim = A(state_ssm_im)[0]
    in_maps = []
    for c in range(8):
        b, hf = c // 2, c % 2
        pos = np.zeros((18, 128), dtype=f32)
        pos[:16] = hf * 2048 + np.arange(16)[:, None] * 128 + np.arange(128)[None, :]
        pos[16] = hf * 2048 - 128 + np.arange(128)
        pos[17] = 8192.0
        ang = pos[:, :, None] * inv_freq[None, None, :]
        m = dict(common)
        m.update(
            xo=np.ascontiguousarray(x_prompt[b, hf * 2048:(hf + 1) * 2048]),
            xp=np.ascontiguousarray(x_prompt[b, 0:2048]) if hf else np.zeros((2048, 1024), f32),
            xs=np.ascontiguousarray(x_sample[c * 16:(c + 1) * 16, 0]),
            kwin=np.ascontiguousarray(skw[c * 16:(c + 1) * 16]), vwin=np.ascontiguousarray(svw[c * 16:(c + 1) * 16]),
            sre_l=np.ascontiguousarray(np.moveaxis(st_layout(np.moveaxis(sre[c * 16:(c + 1) * 16], 0, 2)), 2, 2)),
            sim_l=np.ascontiguousarray(st_layout(np.moveaxis(sim[c * 16:(c + 1) * 16], 0, 2))),
            mk_prev0=rep4(m_prev * float(hf)),
            ropec=np.ascontiguousarray(np.cos(ang).astype(f32).transpose(1, 0, 2)),
            ropes=np.ascontiguousarray(np.sin(ang).astype(f32).transpose(1, 0, 2)),
        )
        in_maps.append(m)
    return in_maps


def kernel(**inputs):
    in_maps = make_inputs(**inputs)
    if "nc" not in _NC:
        _NC["nc"] = build()
    res = run_bass_kernel_spmd(_NC["nc"], in_maps, core_ids=list(range(8)))
    R = res.results
    f32 = np.float32

    def un_st(a):
        a = a.reshape((2, 64, 16) + a.shape[2:])
        return np.moveaxis(a, 2, 0).reshape((32, 64) + a.shape[3:])

    y_prompt = np.stack([np.concatenate([R[2 * b]["y"], R[2 * b + 1]["y"]], axis=0) for b in range(4)])
    y_sample = np.concatenate([R[c]["ys"] for c in range(8)], axis=0)[:, None, :]
    kwp = np.stack([R[2 * b + 1]["kwp"].reshape(128, 2, 64) for b in range(4)])[None]
    vwp = np.stack([R[2 * b + 1]["vwp"].reshape(128, 2, 64) for b in range(4)])[None]
    hre = np.stack([un_st(R[2 * b + 1]["hre"]) for b in range(4)])[None]
    him = np.stack([un_st(R[2 * b + 1]["him"]) for b in range(4)])[None]
    kws = np.concatenate([R[c]["kws"].reshape(16, 128, 2, 64) for c in range(8)], axis=0)[None]
    vws = np.concatenate([R[c]["vws"].reshape(16, 128, 2, 64) for c in range(8)], axis=0)[None]
    sres = np.concatenate([np.moveaxis(un_st(R[c]["sres"]), 2, 0) for c in range(8)], axis=0)[None]
    sims = np.concatenate([np.moveaxis(un_st(R[c]["sims"]), 2, 0) for c in range(8)], axis=0)[None]
    outs = (y_prompt, y_sample, kwp, vwp, hre, him, kws, vws, sres, sims)
    return tuple(np.ascontiguousarray(o, dtype=f32) for o in outs)
```

```python
import contextlib
import math
import numpy as np
import ml_dtypes
import concourse.bass as bass
import concourse.mybir as mybir
from concourse.bass_utils import run_bass_kernel_spmd

F32 = mybir.dt.float32
BF16 = mybir.dt.bfloat16
AF = mybir.ActivationFunctionType
ALU = mybir.AluOpType

D = 1024
TOK = 2048
NS = 16
NCOL = TOK + NS
DFF = 2816
NFT = 22
EPS = 1e-5


class Buf:
    __slots__ = ("w", "r")

    def __init__(self):
        self.w = None
        self.r = []


class _Op:
    __slots__ = ("fn", "deps", "chan", "val", "ms", "alld", "seg", "cost", "lat", "idx", "eng")

    def __init__(self, fn, deps, chan):
        self.fn = fn
        self.deps = deps
        self.chan = chan
        self.val = 0
        self.ms = 0


LOOKAHEAD = 600


class _ProbeIns:
    def then_inc(self, *a, **k):
        return self


class _ProbeEng:
    def __init__(self):
        self.kind = None
        self.kw = None
        self.args = None

    def __getattr__(self, name):
        def f(*args, **kw):
            self.kind = name
            self.kw = kw
            self.args = args
            return _ProbeIns()
        return f


def _free(ap):
    n = 1
    for d in ap.shape[1:]:
        n *= int(d)
    return n


def _estimate(eng, fn):
    p = _ProbeEng()
    try:
        fn(p)
        kw, args, kind = p.kw, p.args, p.kind
        if kind == "dma_start":
            out = kw.get("out")
            nbytes = _free(out) * int(out.shape[0]) * (2 if "bfloat16" in str(out.dtype) else 4)
            if eng == "pool":
                nbytes *= 2
            return 0.06, 2.0 + nbytes / 200e3
        if kind == "matmul":
            n = _free(kw["rhs"])
            c = 0.045 + max(n, 64) / 2400.0
            return c, c + 0.1
        if kind == "transpose":
            return 0.1, 0.2
        out = kw.get("out", args[0] if args else None)
        n = _free(out)
        if kind == "tensor_tensor_scan":
            n *= 2
        if eng == "act":
            c = 0.19 + n / 1200.0
        else:
            c = 0.16 + n / 960.0
        return c, c + 0.05
    except Exception:
        return None


class Prog:
    ENG = ("pe", "act", "dve", "pool", "sp")

    def __init__(self):
        self.ops = {e: [] for e in self.ENG}
        self.chan_cnt = {}
        self.chan_last = {}
        self.seg = 0

    def op(self, eng, fn, reads=(), writes=(), chan=None, n=512, lat=None):
        idx = len(self.ops[eng])
        me = (eng, idx)
        alld = set()
        for b in reads:
            if b.w is not None:
                alld.add(b.w)
        for b in writes:
            if b.w is not None:
                alld.add(b.w)
            for r in b.r:
                alld.add(r)
        if chan is not None and chan in self.chan_last:
            alld.add(self.chan_last[chan])
        alld.discard(me)
        raw = {b.w for b in reads if b.w is not None}
        deps = set()
        for d in alld:
            dop = self.ops[d[0]][d[1]]
            same_compute = (d[0] == eng and dop.chan is None and chan is None)
            if same_compute and eng == "pe":
                continue
            deps.add(d)
        o = _Op(fn, deps, chan)
        o.alld = alld
        o.seg = self.seg
        o.eng = eng
        o.idx = idx
        est = _estimate(eng, fn) if fn is not None else (0.0, 0.0)
        if est is None:
            est = (0.06, 2.5) if chan is not None else (0.3, 0.3)
        o.cost, o.lat = est
        if chan is not None:
            self.chan_cnt[chan] = self.chan_cnt.get(chan, 0) + 16
            o.val = self.chan_cnt[chan]
            self.chan_last[chan] = me
        self.ops[eng].append(o)
        for b in reads:
            b.r.append(me)
        for b in writes:
            b.w = me
            b.r = []
        return me

    def fence(self):
        last = []
        for e in self.ENG:
            for i in range(len(self.ops[e]) - 1, -1, -1):
                if self.ops[e][i].chan is None and self.ops[e][i].fn is not None:
                    last.append((e, i))
                    break
        last += list(self.chan_last.values())
        self.seg += 1
        for e in self.ENG:
            idx = len(self.ops[e])
            o = _Op(None, {d for d in last if d != (e, idx)}, None)
            o.alld = set(); o.seg = self.seg; o.eng = e; o.idx = idx; o.cost = 0.0; o.lat = 0.0
            self.ops[e].append(o)
        self.seg += 1

    def schedule(self):
        order = {e: [] for e in self.ENG}
        fin = {}
        nseg = self.seg + 1
        segops = [{e: [] for e in self.ENG} for _ in range(nseg)]
        for e in self.ENG:
            for o in self.ops[e]:
                segops[o.seg][e].append(o)
        tbase = 0.0
        for sg in range(nseg):
            for e in self.ENG:
                for o in segops[sg][e]:
                    if o.fn is None:
                        nd = {d for d in o.deps if self.ops[d[0]][d[1]].chan is not None}
                        for e2 in self.ENG:
                            for i2 in reversed(order[e2]):
                                o2 = self.ops[e2][i2]
                                if o2.chan is None and o2.fn is not None:
                                    if (e2, i2) != (e, o.idx):
                                        nd.add((e2, i2))
                                    break
                        o.deps = nd
            pend = {e: list(segops[sg][e]) for e in self.ENG}
            efree = {e: tbase for e in self.ENG}
            total = sum(len(v) for v in pend.values())
            done = 0
            while done < total:
                best = None
                for e in self.ENG:
                    pl = pend[e]
                    for o in pl[:LOOKAHEAD]:
                        ok = True
                        rdy = efree[e]
                        for d in o.alld:
                            f = fin.get(d)
                            if f is None:
                                ok = False
                                break
                            if f > rdy:
                                rdy = f
                        if not ok:
                            continue
                        if best is None or rdy < best[0] - 1e-9:
                            best = (rdy, e, o)
                        break_early = (rdy <= efree[e] + 1e-9)
                        if break_early:
                            break
                assert best is not None, "scheduler deadlock"
                rdy, e, o = best
                pend[e].remove(o)
                cp = None; cpt = -1.0
                for d in o.alld:
                    if fin[d] > cpt:
                        cpt = fin[d]; cp = d
                if efree[e] >= cpt and order[e]:
                    cp = (e, order[e][-1])
                o.ms = 0
                self.crit = getattr(self, "crit", {})
                self.crit[(e, o.idx)] = (cp, rdy, rdy + o.lat)
                efree[e] = rdy + o.cost
                fin[(e, o.idx)] = rdy + o.lat
                order[e].append(o.idx)
                done += 1
            tbase = max([tbase] + [fin[(e, o.idx)] for e in self.ENG for o in segops[sg][e]])
            self.seg_end = getattr(self, 'seg_end', []) + [round(tbase, 1)]
        self.est_us = tbase
        return order

    def emit(self, nc):
        order = self.schedule()
        needed = {e: set() for e in self.ENG}
        for e in self.ENG:
            for o in self.ops[e]:
                for (de, di) in o.deps:
                    if self.ops[de][di].chan is None:
                        needed[de].add(di)
        for e in self.ENG:
            c = 0
            for i in order[e]:
                o = self.ops[e][i]
                if o.chan is None and i in needed[e]:
                    c += 1
                    o.ms = c
                    o.val = c
        with contextlib.ExitStack() as st:
            esem = {e: st.enter_context(nc.semaphore("s_" + e)) for e in self.ENG}
            csem = {c: st.enter_context(nc.semaphore("c_" + str(c))) for c in self.chan_cnt}
            block = st.enter_context(nc.Block())
            prog = self

            def run(engname, eng):
                waited = {}
                for i in order[engname]:
                    o = prog.ops[engname][i]
                    for (de, di) in sorted(o.deps):
                        d = prog.ops[de][di]
                        if d.chan is not None:
                            sem, key = csem[d.chan], ("c", d.chan)
                        else:
                            sem, key = esem[de], ("e", de)
                        if waited.get(key, 0) >= d.val:
                            continue
                        eng.wait_ge(sem, d.val)
                        waited[key] = d.val
                    if o.fn is None:
                        continue
                    ins = o.fn(eng)
                    if o.chan is not None:
                        ins.then_inc(csem[o.chan], 16)
                    elif o.ms:
                        ins.then_inc(esem[engname], 1)
                if engname == "sp":
                    for c, v in prog.chan_cnt.items():
                        if waited.get(("c", c), 0) < v:
                            eng.wait_ge(csem[c], v)

            block.tensor(lambda e: run("pe", e))
            block.scalar(lambda e: run("act", e))
            block.vector(lambda e: run("dve", e))
            block.gpsimd(lambda e: run("pool", e))
            block.sync(lambda e: run("sp", e))


class _Stop(Exception):
    pass


STOP = [None]
DUMPS = []


class TB:
    def __init__(self, t, b=None):
        self.t = t
        self.b = b if b is not None else Buf()

    def __getitem__(self, k):
        return self.t[k]


def build():
    nc = bass.Bass("TRN2", target_bir_lowering=False)
    P = Prog()

    def din(name, shape, dt=F32):
        return nc.dram_tensor(name, list(shape), dt, kind="ExternalInput").ap()

    def dout(name, shape, dt=F32):
        return nc.dram_tensor(name, list(shape), dt, kind="ExternalOutput").ap()

    xo = din("xo", [TOK, D]); xp = din("xp", [TOK, D]); xs = din("xs", [NS, D])
    kwin = din("kwin", [NS, 128, 128]); vwin = din("vwin", [NS, 128, 128])
    sre_l = din("sre_l", [128, 16, NS]); sim_l = din("sim_l", [128, 16, NS])
    mk_own = din("mk_own", [128, 512], BF16); mk_prev = din("mk_prev", [128, 512], BF16)
    mk_prev0 = din("mk_prev0", [128, 512], BF16)
    ropec = din("ropec", [128, 18, 8]); ropes = din("ropes", [128, 18, 8])
    g1b = din("g1b", [128, D]); g2b = din("g2b", [128, D]); gfb = din("gfb", [128, D])
    w_in = din("w_in", [D, 3328]); bgate_l = din("bgate_l", [128, 16]); sink_l = din("sink_l", [128, 4])
    lam_re_l = din("lam_re_l", [128, 16]); lam_im_l = din("lam_im_l", [128, 16]); logdt_l = din("logdt_l", [128, 16])
    bre_l = din("bre_l", [128, 16, 16]); bim_l = din("bim_l", [128, 16, 16])
    cre_l = din("cre_l", [128, 16, 16]); cim_l = din("cim_l", [128, 16, 16])
    d_l = din("d_l", [128, 4]); w_glu = din("w_glu", [512, 512]); bglu_l = din("bglu_l", [128, 4])
    w_ba = din("w_ba", [512, D]); w_bs = din("w_bs", [512, D]); w_out = din("w_out", [D, D])
    w_fg = din("w_fg", [D, DFF]); w_fu = din("w_fu", [D, DFF]); w_fd = din("w_fd", [DFF, D])
    ident_bf = din("ident_bf", [128, 128], BF16); ident_f = din("ident_f", [128, 128])
    mask_g2 = din("mask_g2", [128, 2]); mask_bd = din("mask_bd", [128, 128])

    y_o = dout("y", [TOK, D]); ys_o = dout("ys", [NS, D])
    kwp_o = dout("kwp", [128, 128]); vwp_o = dout("vwp", [128, 128])
    hre_o = dout("hre", [128, 16]); him_o = dout("him", [128, 16])
    kws_o = dout("kws", [NS, 128, 128]); vws_o = dout("vws", [NS, 128, 128])
    sres_o = dout("sres", [128, 16, NS]); sims_o = dout("sims", [128, 16, NS])

    with contextlib.ExitStack() as st:
        def sb(name, shape, dt=F32):
            return TB(st.enter_context(nc.sbuf_tensor(name, list(shape), dt)))

        def op(eng, fn, r=(), w=(), chan=None):
            return P.op(eng, fn, [x.b for x in r], [x.b for x in w], chan)

        def chk(k):
            if STOP[0] == k:
                raise _Stop()

        def dump(name, ap, shape, dt=F32):
            if STOP[0] is None:
                return
            d = nc.dram_tensor("dbg_" + name, list(shape), dt, kind="ExternalOutput").ap()
            DUMPS.append("dbg_" + name)
            P.fence()
            P.op("sp", lambda e: e.dma_start(out=d, in_=ap), [], [], chan="dbg_" + name)

        def body():
            PS = [TB(st.enter_context(nc.psum_tensor("ps%d" % i, [128, 512], F32))) for i in range(6)]
            PTB = [TB(st.enter_context(nc.psum_tensor("pt%d" % i, [128, 1024], BF16))) for i in range(2)]
            rr = {"ps": 0, "pt": 0, "ld": 0}

            def nps():
                rr["ps"] = (rr["ps"] + 1) % len(PS)
                return PS[rr["ps"]]

            def npt():
                rr["pt"] = (rr["pt"] + 1) % 2
                return PTB[rr["pt"]]

            def load(dst, src, r=(), q="sp"):
                rr["ld"] += 1
                ch = "ld%d" % (rr["ld"] % 8)
                op(q, lambda e: e.dma_start(out=dst[:], in_=src), r, [dst], chan=ch)

            identb = sb("identb", [128, 128], BF16); identf = sb("identf", [128, 128])
            mg2 = sb("mg2", [128, 2]); mbd = sb("mbd", [128, 128])
            mko = sb("mko", [128, 512], BF16); mkp = sb("mkp", [128, 512], BF16); mkp0 = sb("mkp0", [128, 512], BF16)
            rc_t = sb("rc_t", [128, 18, 8]); rs_t = sb("rs_t", [128, 18, 8])
            g1t = sb("g1t", [128, D]); g2t = sb("g2t", [128, D]); gft = sb("gft", [128, D])
            bgt = sb("bgt", [128, 16]); esk = sb("esk", [128, 4]); dlt = sb("dlt", [128, 4]); bglt = sb("bglt", [128, 4])
            ones_bf = sb("ones_bf", [128, 64], BF16)
            for dst, src in ((identb, ident_bf), (identf, ident_f), (mg2, mask_g2), (mbd, mask_bd), (mko, mk_own),
                             (mkp, mk_prev), (mkp0, mk_prev0), (rc_t, ropec), (rs_t, ropes), (g1t, g1b), (g2t, g2b),
                             (gft, gfb), (bgt, bgate_l), (esk, sink_l), (dlt, d_l), (bglt, bglu_l)):
                load(dst, src)
            op("dve", lambda e: e.memset(ones_bf[:], 1.0), [], [ones_bf])
            op("act", lambda e: e.activation(out=esk[:], in_=esk[:], func=AF.Exp), [esk], [esk])
            chk(-1)

            ssmT = sb("ssmT", [128, 4, NCOL], BF16)
            kT_all = sb("kT_all", [128, TOK + 128], BF16)
            v_all = sb("v_all", [128, 17, 128], BF16)
            wqk = sb("wqk", [128, 8, 768], BF16)
            op("pool", lambda e: e.dma_start(out=wqk[:, :, 0:512], in_=w_in[:, 0:512].rearrange("(k p) c -> p k c", p=128)), [], [wqk], chan="wq0")
            op("pool", lambda e: e.dma_start(out=wqk[:, :, 512:768], in_=w_in[:, 512:768].rearrange("(k p) c -> p k c", p=128)), [wqk], [wqk], chan="wq1")

            ARW = 38800
            arena = st.enter_context(nc.sbuf_tensor("arena", [128, ARW], F32))
            apos = {"o": 0}

            def carve(shape, dt=F32):
                n = int(np.prod(shape[1:]))
                words = n if dt == F32 else (n + 1) // 2
                o = apos["o"]
                apos["o"] = o + words
                assert apos["o"] <= ARW, apos["o"]
                v = arena[:, o:o + words]
                if dt != F32:
                    v = v.bitcast(dt)
                if len(shape) == 3:
                    v = v.rearrange("p (a b) -> p a b", b=shape[2])
                elif len(shape) == 4:
                    v = v.rearrange("p (a b c) -> p a b c", b=shape[2], c=shape[3])
                elif len(shape) == 5:
                    v = v.rearrange("p (a b c d) -> p a b c d", b=shape[2], c=shape[3], d=shape[4])
                return TB(v)

            WB = carve([128, 4, 8, 2, 128], BF16)
            WD = carve([128, 8, 16, 2, 32], BF16)
            KT = carve([128, 4, 8, 128], BF16)
            UTR = carve([128, 16, 64]); UTI = carve([128, 16, 64]); RT = carve([128, 16, 64])
            LBR = carve([128, 16]); LBI = carve([128, 16]); R8 = carve([128, 16])
            CARR = carve([128, 16]); CARI = carve([128, 16])
            wglu_t = carve([128, 4, 512], BF16)
            wu = carve([128, 8, 512], BF16)
            op("pool", lambda e: e.dma_start(out=wu[:], in_=w_in[:, 768:1280].rearrange("(k p) c -> p k c", p=128)), [], [wu], chan="wq2")
            op("pool", lambda e: e.dma_start(out=wglu_t[:], in_=w_glu.rearrange("(k p) c -> p k c", p=128)), [wu], [wglu_t], chan="wq2")
            apos_keep = apos["o"]

            C0 = Buf()

            def cvec(shape=(128, 16)):
                t = carve(list(shape)); t.b = C0
                return t

            def dv(fn):
                P.op("dve", fn, [C0], [C0])

            def av(fn):
                P.op("act", fn, [C0], [C0])

            def TT(o, a, b, o_):
                dv(lambda e: e.tensor_tensor(out=o, in0=a, in1=b, op=o_))

            lamr = cvec(); lami = cvec(); ldt = cvec()
            brel = cvec((128, 16, 16)); biml = cvec((128, 16, 16)); crel = cvec((128, 16, 16)); ciml = cvec((128, 16, 16))
            for dst, src in ((lamr, lam_re_l), (lami, lam_im_l), (ldt, logdt_l), (brel, bre_l), (biml, bim_l), (crel, cre_l), (ciml, cim_l)):
                load(dst, src)
            chk(-0.9)
            dtv = cvec(); are = cvec(); aim = cvec(); mag = cvec(); cr = cvec(); ci = cvec(); t1 = cvec(); t2 = cvec(); t3 = cvec()
            av(lambda e: e.activation(out=dtv[:], in_=ldt[:], func=AF.Exp))
            TT(are[:], lamr[:], dtv[:], ALU.mult)
            TT(aim[:], lami[:], dtv[:], ALU.mult)
            av(lambda e: e.activation(out=mag[:], in_=are[:], func=AF.Exp))
            av(lambda e: e.activation(out=R8[:], in_=are[:], func=AF.Exp, scale=8.0))
            av(lambda e: e.activation(out=ci[:], in_=aim[:], func=AF.Sin, scale=1.0 / 16))
            av(lambda e: e.activation(out=t1[:], in_=aim[:], func=AF.Sin, scale=1.0 / 32))
            TT(t1[:], t1[:], t1[:], ALU.mult)
            dv(lambda e: e.tensor_scalar(out=cr[:], in0=t1[:], scalar1=-2.0, scalar2=1.0, op0=ALU.mult, op1=ALU.add))
            for _ in range(4):
                TT(t1[:], cr[:], cr[:], ALU.mult)
                TT(t2[:], ci[:], ci[:], ALU.mult)
                TT(t3[:], cr[:], ci[:], ALU.mult)
                TT(cr[:], t1[:], t2[:], ALU.subtract)
                dv(lambda e: e.tensor_scalar(out=ci[:], in0=t3[:], scalar1=2.0, scalar2=None, op0=ALU.mult))
            TT(LBR[:], mag[:], cr[:], ALU.mult)
            TT(LBI[:], mag[:], ci[:], ALU.mult)
            chk(-0.8)
            den = cvec(); nr = cvec(); cfr = cvec(); cfi = cvec()
            TT(t1[:], lamr[:], lamr[:], ALU.mult)
            TT(t2[:], lami[:], lami[:], ALU.mult)
            TT(den[:], t1[:], t2[:], ALU.add)
            dv(lambda e: e.reciprocal(out=den[:], in_=den[:]))
            dv(lambda e: e.tensor_scalar(out=nr[:], in0=LBR[:], scalar1=-1.0, scalar2=None, op0=ALU.add))
            TT(t1[:], nr[:], lamr[:], ALU.mult)
            TT(t2[:], LBI[:], lami[:], ALU.mult)
            TT(t1[:], t1[:], t2[:], ALU.add)
            TT(cfr[:], t1[:], den[:], ALU.mult)
            TT(t1[:], LBI[:], lamr[:], ALU.mult)
            TT(t2[:], nr[:], lami[:], ALU.mult)
            TT(t1[:], t1[:], t2[:], ALU.subtract)
            TT(cfi[:], t1[:], den[:], ALU.mult)
            bbR = cvec((128, 16, 16)); bbI = cvec((128, 16, 16)); u1 = cvec((128, 16, 16)); u2 = cvec((128, 16, 16))

            def bc_h(v):
                return v[:].unsqueeze(2).broadcast_to([128, 16, 16])

            TT(u1[:], brel[:], bc_h(cfr), ALU.mult); TT(u2[:], biml[:], bc_h(cfi), ALU.mult); TT(bbR[:], u1[:], u2[:], ALU.subtract)
            TT(u1[:], biml[:], bc_h(cfr), ALU.mult); TT(u2[:], brel[:], bc_h(cfi), ALU.mult); TT(bbI[:], u1[:], u2[:], ALU.add)
            LPR = cvec((128, 9, 16)); LPI = cvec((128, 9, 16))
            dv(lambda e: e.memset(LPR[:, 0, :], 1.0)); dv(lambda e: e.memset(LPI[:, 0, :], 0.0))
            for k in range(8):
                TT(t1[:], LPR[:, k, :], LBR[:], ALU.mult); TT(t2[:], LPI[:, k, :], LBI[:], ALU.mult)
                TT(LPR[:, k + 1, :], t1[:], t2[:], ALU.subtract)
                TT(t1[:], LPR[:, k, :], LBI[:], ALU.mult); TT(t2[:], LPI[:, k, :], LBR[:], ALU.mult)
                TT(LPI[:, k + 1, :], t1[:], t2[:], ALU.add)
            dv(lambda e: e.reciprocal(out=t3[:], in_=R8[:]))
            def mq(v):
                return v.rearrange("p (q m) -> p m q", m=4)
            TT(UTR[:, :, 0].rearrange("p (m q) -> p m q", q=4), mq(LPR[:, 8, :]), mq(t3[:]), ALU.mult)
            TT(UTI[:, :, 0].rearrange("p (m q) -> p m q", q=4), mq(LPI[:, 8, :]), mq(t3[:]), ALU.mult)
            w1 = cvec((128, 16, 32)); w2 = cvec((128, 16, 32))
            n = 1
            while n < 64:
                def bc_n(T_, n=n):
                    return T_[:, :, n - 1:n].broadcast_to([128, 16, n])
                TT(w1[:, :, 0:n], UTR[:, :, 0:n], bc_n(UTR), ALU.mult); TT(w2[:, :, 0:n], UTI[:, :, 0:n], bc_n(UTI), ALU.mult)
                TT(UTR[:, :, n:2 * n], w1[:, :, 0:n], w2[:, :, 0:n], ALU.subtract)
                TT(w1[:, :, 0:n], UTR[:, :, 0:n], bc_n(UTI), ALU.mult); TT(w2[:, :, 0:n], UTI[:, :, 0:n], bc_n(UTR), ALU.mult)
                TT(UTI[:, :, n:2 * n], w1[:, :, 0:n], w2[:, :, 0:n], ALU.add)
                n *= 2
            dv(lambda e: e.tensor_copy(out=RT[:].rearrange("p (m q) c -> p m q c", q=4), in_=mq(R8[:]).unsqueeze(3).broadcast_to([128, 4, 4, 64])))
            dv(lambda e: e.memset(RT[:, :, 0:1], 0.0))
            dv(lambda e: e.memset(CARR[:], 0.0)); dv(lambda e: e.memset(CARI[:], 0.0))
            chk(-0.7)
            BPR = cvec((128, 8, 16, 16)); BPI = cvec((128, 8, 16, 16)); X1 = cvec((128, 8, 16, 16)); X2 = cvec((128, 8, 16, 16))

            def bc_k(v):
                return v[:].unsqueeze(1).broadcast_to([128, 8, 16, 16])

            def bc_p(v, lo):
                return v[:, lo:lo + 8, :].unsqueeze(3).broadcast_to([128, 8, 16, 16])

            TT(X1[:], bc_k(bbR), bc_p(LPR, 0), ALU.mult); TT(X2[:], bc_k(bbI), bc_p(LPI, 0), ALU.mult); TT(BPR[:], X1[:], X2[:], ALU.subtract)
            TT(X1[:], bc_k(bbI), bc_p(LPR, 0), ALU.mult); TT(X2[:], bc_k(bbR), bc_p(LPI, 0), ALU.mult); TT(BPI[:], X1[:], X2[:], ALU.add)
            chk(-0.6)
            EBR = cvec((128, 128, 2, 16)); EBI = cvec((128, 128, 2, 16))
            mg2b = mg2[:].unsqueeze(1).unsqueeze(3).broadcast_to([128, 128, 2, 16])
            TT(EBR[:], BPR[:].rearrange("p k s h -> p (k s) h").unsqueeze(2).broadcast_to([128, 128, 2, 16]), mg2b, ALU.mult)
            TT(EBI[:], BPI[:].rearrange("p k s h -> p (k s) h").unsqueeze(2).broadcast_to([128, 128, 2, 16]), mg2b, ALU.mult)
            EBRv = EBR[:].rearrange("p (k q m) g h -> p k q (m g h)", k=8, q=4)
            EBIv = EBI[:].rearrange("p (k q m) g h -> p k q (m g h)", k=8, q=4)
            CER = cvec((128, 16, 2, 16)); CEIN = cvec((128, 16, 2, 16))
            mg2c = mg2[:].unsqueeze(1).unsqueeze(3).broadcast_to([128, 16, 2, 16])
            TT(CER[:], crel[:].unsqueeze(2).broadcast_to([128, 16, 2, 16]), mg2c, ALU.mult)
            TT(CEIN[:], ciml[:].unsqueeze(2).broadcast_to([128, 16, 2, 16]), mg2c, ALU.mult)
            dv(lambda e: e.tensor_scalar(out=CEIN[:], in0=CEIN[:], scalar1=-1.0, scalar2=None, op0=ALU.mult))
            CERv = CER[:].rearrange("p (q m) g h -> p q (m g h)", q=4)
            CEINv = CEIN[:].rearrange("p (q m) g h -> p q (m g h)", q=4)
            CB = TB(None, C0)
            chk(-0.5)
            for q in range(4):
                for s in range(0, 8, 2):
                    ps = nps()
                    for j in range(2):
                        for ri, EV in enumerate((EBRv, EBIv)):
                            src = EV[:, 7 - (s + j), q, :]
                            dstp = ps[:, (j * 2 + ri) * 128:(j * 2 + ri + 1) * 128]
                            op("pe", lambda e, src=src, dstp=dstp: e.transpose(out=dstp, in_=src, identity=identf[:]), [CB, identf], [ps])
                    op("act", lambda e, ps=ps, q=q, s=s: e.activation(out=WB[:, q, s:s + 2, :, :].rearrange("p a b c -> p (a b c)"), in_=ps[:], func=AF.Copy), [ps], [WB])
            chk(-0.4)
            tmpk = cvec((128, 128))
            for q in range(4):
                for dl in range(8):
                    ps = nps()
                    op("pe", lambda e, ps=ps, q=q, dl=dl: e.matmul(ps[:, 0:128], lhsT=EBRv[:, dl, q, :], rhs=CERv[:, q, :], start=True, stop=False), [CB], [ps])
                    op("pe", lambda e, ps=ps, q=q, dl=dl: e.matmul(ps[:, 0:128], lhsT=EBIv[:, dl, q, :], rhs=CEINv[:, q, :], start=False, stop=True), [CB], [ps])
                    if dl == 0:
                        op("dve", lambda e, ps=ps, tmpk=tmpk: e.tensor_tensor(out=tmpk[:], in0=ps[:, 0:128], in1=mbd[:], op=ALU.mult), [ps, mbd, CB], [CB])
                        op("dve", lambda e, q=q, tmpk=tmpk: e.scalar_tensor_tensor(out=KT[:, q, 0, :], in0=identf[:], scalar=dlt[:, q:q + 1], in1=tmpk[:], op0=ALU.mult, op1=ALU.add), [CB, identf, dlt], [KT])
                    else:
                        op("dve", lambda e, ps=ps, q=q, dl=dl: e.tensor_tensor(out=KT[:, q, dl, :], in0=ps[:, 0:128], in1=mbd[:], op=ALU.mult), [ps, mbd], [KT])
            chk(-0.3)
            TT(X1[:], bc_k(crel), bc_p(LPR, 1), ALU.mult); TT(X2[:], bc_k(ciml), bc_p(LPI, 1), ALU.mult); TT(BPR[:], X1[:], X2[:], ALU.subtract)
            TT(X1[:], bc_k(crel), bc_p(LPI, 1), ALU.mult); TT(X2[:], bc_k(ciml), bc_p(LPR, 1), ALU.mult); TT(BPI[:], X1[:], X2[:], ALU.add)
            dv(lambda e: e.tensor_scalar(out=BPI[:], in0=BPI[:], scalar1=-1.0, scalar2=None, op0=ALU.mult))
            for ri, BPx in enumerate((BPR, BPI)):
                P.op("dve", lambda e, ri=ri, BPx=BPx: e.tensor_tensor(
                    out=WD[:, :, :, ri, :].rearrange("p t s (g h) -> p (t s) g h", g=2),
                    in0=BPx[:].rearrange("p k s h -> p (k s) h").unsqueeze(2).broadcast_to([128, 128, 2, 16]),
                    in1=mg2b, op=ALU.mult), [C0, mg2.b], [WD.b])

            P.fence()
            chk(1)
            apos["o"] = apos_keep

            def rmsnorm_to_T(xt, rows, gt, dstT, col0, tmp_bf, ss, junk):
                op("dve", lambda e: e.memset(ss[0:rows, :], 0.0), [], [ss])
                op("act", lambda e: e.activation(out=junk[0:rows, :], in_=xt[0:rows, :], func=AF.Square, accum_out=ss[0:rows, :]), [xt, ss], [junk, ss])
                op("act", lambda e: e.activation(out=ss[0:rows, :], in_=ss[0:rows, :], func=AF.Sqrt, scale=1.0 / D, bias=EPS), [ss], [ss])
                op("dve", lambda e: e.reciprocal(out=ss[0:rows, :], in_=ss[0:rows, :]), [ss], [ss])
                op("dve", lambda e: e.scalar_tensor_tensor(out=tmp_bf[0:rows, :], in0=xt[0:rows, :], scalar=ss[0:rows, 0:1], in1=gt[0:rows, :], op0=ALU.mult, op1=ALU.mult), [xt, ss, gt], [tmp_bf])
                pt = npt()
                for kc in range(8):
                    op("pe", lambda e, kc=kc: e.transpose(out=pt[:, kc * 128:kc * 128 + rows], in_=tmp_bf[0:rows, kc * 128:(kc + 1) * 128], identity=identb[0:rows, 0:rows]), [tmp_bf, identb], [pt])
                op("act", lambda e: e.activation(out=dstT[:, 0:8, col0:col0 + rows], in_=pt[:].rearrange("p (k t) -> p k t", t=128)[:, :, 0:rows], func=AF.Copy), [pt], [dstT])
                return ss

            WRING = [None] * 4
            wr_i = {"i": 0}

            SCR = {}

            def scratch(key, n):
                if key not in SCR:
                    SCR[key] = (nc.dram_tensor("wscr_" + key, [128, n], BF16).ap(), TB(None))
                return SCR[key]

            def wload(src_ap, nk, ncols, key=None, ti=0):
                wr_i["i"] = (wr_i["i"] + 1) % 4
                slot = WRING[wr_i["i"]]
                flat = slot[:, 0:nk * ncols]
                view = flat.rearrange("p (k c) -> p k c", c=ncols)
                i = wr_i["i"]
                if key is None:
                    op("pool", lambda e: e.dma_start(out=view, in_=src_ap.rearrange("(k p) c -> p k c", p=128)), [], [slot], chan="wr%d" % i)
                    return TB(view, slot.b)
                scr, sb_ = scratch(key, nk * ncols)
                op("pool", lambda e: e.dma_start(out=flat, in_=scr), [sb_], [slot], chan="wr%d" % i)
                return TB(view, slot.b)

            def precast(src_ap, nk, ncols, key, off=0, total=None):
                scr, sb_ = scratch(key, total if total is not None else nk * ncols)
                dst = scr[:, off:off + nk * ncols].rearrange("p (k c) -> p k c", c=ncols)
                op("pool", lambda e: e.dma_start(out=dst, in_=src_ap.rearrange("(k p) c -> p k c", p=128)), [sb_], [sb_], chan="wcast")

            xst = [carve([128, D]) for _ in range(2)]
            xnb = [carve([128, D], BF16) for _ in range(2)]
            junk = carve([128, D], BF16); ssv = [carve([128, 1]) for _ in range(2)]
            xnT = [carve([128, 8, 528], BF16) for _ in range(2)]
            uT = [carve([128, 4, 528], BF16) for _ in range(2)]
            Hp = [carve([128, 2, 16, 65], BF16)] * 2
            mt = [carve([128, 256]) for _ in range(4)]
            bRm = [carve([128, 256]) for _ in range(4)]; bIm = [carve([128, 256]) for _ in range(4)]
            GR = carve([128, 256]); GI = carve([128, 256])
            yT = carve([128, 4, 528]); zT = yT; zTb = carve([128, 4, 528], BF16); sg = carve([128, 528])
            small = [carve([128, 16]) for _ in range(8)]
            h0r = carve([128, 16, NS]); h0i = carve([128, 16, NS]); h0b = carve([128, 2, 16, NS], BF16)
            hnr = carve([128, 16, NS]); hni = carve([128, 16, NS]); s1 = carve([128, 16, NS]); s2 = carve([128, 16, NS])
            kvf = [carve([128, 256]) for _ in range(2)]
            qk_tok = [carve([128, 640], BF16) for _ in range(2)]
            rtmp = [carve([128, 10, 8]) for _ in range(4)]
            chk(0.4)
            load(h0r, sre_l); load(h0i, sim_l)
            chk(0.5)
            op("dve", lambda e: e.tensor_copy(out=h0b[:, 0, :, :], in_=h0r[:]), [h0r], [h0b])
            op("dve", lambda e: e.tensor_copy(out=h0b[:, 1, :, :], in_=h0i[:]), [h0i], [h0b])
            cnt = {"x": 0, "t": 0}
            hpB = [Buf() for _ in range(4)]

            def rope_kv(ps_kv, rows, ridx, kv_f):
                op("act", lambda e: e.activation(out=kv_f[0:rows, :], in_=ps_kv[0:rows, 0:256], func=AF.Copy), [ps_kv], [kv_f])
                oview = kv_f[0:rows, 0:128].rearrange("p (h d) -> p h d", d=64)
                rope_apply(oview, oview, rows, ridx, 2, [kv_f], kv_f)

            def rope_apply(src, dst, rows, ridx, nh, rbufs, dstb):
                cosb = rc_t[0:rows, ridx:ridx + 1, :].broadcast_to([rows, nh, 8])
                sinb = rs_t[0:rows, ridx:ridx + 1, :].broadcast_to([rows, nh, 8])
                a, b, c, d = rtmp
                x1 = src[:, :, 0:8]; x2 = src[:, :, 8:16]
                op("dve", lambda e: e.tensor_tensor(out=a[0:rows, 0:nh, :], in0=x1, in1=cosb, op=ALU.mult), rbufs + [rc_t], [a])
                op("dve", lambda e: e.tensor_tensor(out=b[0:rows, 0:nh, :], in0=x2, in1=sinb, op=ALU.mult), rbufs + [rs_t], [b])
                op("dve", lambda e: e.tensor_tensor(out=c[0:rows, 0:nh, :], in0=x2, in1=cosb, op=ALU.mult), rbufs + [rc_t], [c])
                op("dve", lambda e: e.tensor_tensor(out=d[0:rows, 0:nh, :], in0=x1, in1=sinb, op=ALU.mult), rbufs + [rs_t], [d])
                op("dve", lambda e: e.tensor_tensor(out=dst[:, :, 0:8], in0=a[0:rows, 0:nh, :], in1=b[0:rows, 0:nh, :], op=ALU.subtract), [a, b], [dstb])
                op("dve", lambda e: e.tensor_tensor(out=dst[:, :, 8:16], in0=c[0:rows, 0:nh, :], in1=d[0:rows, 0:nh, :], op=ALU.add), [c, d], [dstb])

            def ssm_tile(src_dram, ntok_sub, is_own, tile_i, sample=False):
                cnt["t"] += 1
                xT = xnT[cnt["t"] % 2]; u_t = uT[cnt["t"] % 2]; hp = Hp[cnt["t"] % 2]
                ncols = 16 if sample else 512
                subs = [(0, 16)] if sample else [(j * 128, 128) for j in range(4)]
                for (c0, rows) in subs:
                    cnt["x"] += 1
                    xt = xst[cnt["x"] % 2]; tb = xnb[cnt["x"] % 2]; ss = ssv[cnt["x"] % 2]
                    load(TB(xt[0:rows, :], xt.b), src_dram[c0:c0 + rows, :])
                    rmsnorm_to_T(xt, rows, g1t, xT, c0, tb, ss, junk)
                chk(2.01)
                for ct in range(4):
                    ps = nps()
                    for kc in range(8):
                        op("pe", lambda e, ps=ps, kc=kc, ct=ct: e.matmul(ps[:, 0:ncols], lhsT=wu[:, kc, ct * 128:(ct + 1) * 128], rhs=xT[:, kc, 0:ncols], start=(kc == 0), stop=(kc == 7)), [wu, xT], [ps])
                    op("act", lambda e, ps=ps, ct=ct: e.activation(out=u_t[:, ct, 0:ncols], in_=ps[:, 0:ncols], func=AF.Copy), [ps], [u_t])
                chk(2.02)
                if sample:
                    bps = []
                    for m in range(4):
                        ps = nps()
                        bps.append(ps)
                        for q in range(4):
                            for ri in range(2):
                                reg = q * 2 + ri
                                op("pe", lambda e, ps=ps, q=q, m=m, ri=ri, reg=reg: e.matmul(
                                    ps[:, reg * 16:(reg + 1) * 16], lhsT=WB[m * 32:(m + 1) * 32, q, 7, ri, :],
                                    rhs=u_t[m * 32:(m + 1) * 32, q, 0:16], start=True, stop=True, tile_position=(m * 32, 0)), [WB, u_t], [ps])
                    lbr_b = LBR[:].unsqueeze(2).broadcast_to([128, 16, 16]); lbi_b = LBI[:].unsqueeze(2).broadcast_to([128, 16, 16])
                    op("dve", lambda e: e.tensor_tensor(out=s1[:], in0=h0r[:], in1=lbr_b, op=ALU.mult), [h0r, LBR], [s1])
                    op("dve", lambda e: e.tensor_tensor(out=s2[:], in0=h0i[:], in1=lbi_b, op=ALU.mult), [h0i, LBI], [s2])
                    op("dve", lambda e: e.tensor_tensor(out=s1[:], in0=s1[:], in1=s2[:], op=ALU.subtract), [s1, s2], [s1])
                    for m in range(4):
                        bv = bps[m][:, 0:128].rearrange("p (q r b) -> p q r b", r=2, b=16)
                        op("dve", lambda e, m=m, bv=bv: e.tensor_tensor(out=hnr[:].rearrange("p (q m) b -> p m q b", m=4)[:, m], in0=s1[:].rearrange("p (q m) b -> p m q b", m=4)[:, m], in1=bv[:, :, 0, :], op=ALU.add), [s1, bps[m]], [hnr])
                    op("dve", lambda e: e.tensor_tensor(out=s1[:], in0=h0i[:], in1=lbr_b, op=ALU.mult), [h0i, LBR, hnr], [s1])
                    op("dve", lambda e: e.tensor_tensor(out=s2[:], in0=h0r[:], in1=lbi_b, op=ALU.mult), [h0r, LBI], [s2])
                    op("dve", lambda e: e.tensor_tensor(out=s1[:], in0=s1[:], in1=s2[:], op=ALU.add), [s1, s2], [s1])
                    for m in range(4):
                        bv = bps[m][:, 0:128].rearrange("p (q r b) -> p q r b", r=2, b=16)
                        op("dve", lambda e, m=m, bv=bv: e.tensor_tensor(out=hni[:].rearrange("p (q m) b -> p m q b", m=4)[:, m], in0=s1[:].rearrange("p (q m) b -> p m q b", m=4)[:, m], in1=bv[:, :, 1, :], op=ALU.add), [s1, bps[m]], [hni])
                    op("sp", lambda e: e.dma_start(out=sres_o, in_=hnr[:]), [hnr], [], chan="o_sr")
                    op("sp", lambda e: e.dma_start(out=sims_o, in_=hni[:]), [hni], [], chan="o_si")
                    for q in range(4):
                        ps = nps()
                        op("pe", lambda e, ps=ps, q=q: e.matmul(ps[:, 0:16], lhsT=KT[:, q, 0, :], rhs=u_t[:, q, 0:16], start=True, stop=False), [KT, u_t], [ps])
                        for m in range(4):
                            for ri in range(2):
                                last = (ri == 1)
                                op("pe", lambda e, ps=ps, q=q, m=m, ri=ri, last=last: e.matmul(
                                    ps[m * 32:(m + 1) * 32, 0:16], lhsT=WD[:, 0, q * 4 + m, ri, :], rhs=h0b[:, ri, q * 4 + m, :],
                                    start=False, stop=last, tile_position=(0, m * 32)), [WD, h0b], [ps])
                        op("act", lambda e, ps=ps, q=q: e.activation(out=yT[:, q, 0:16], in_=ps[:, 0:16], func=AF.Copy), [ps], [yT])
                    glu(16, TOK)
                    return
                yield xT
                vps = [nps() for _ in range(4)]
                for q in range(4):
                    for ri in range(2):
                        reg = q * 2 + ri
                        for s in range(8):
                            for m in range(4):
                                ps = vps[m]
                                op("pe", lambda e, ps=ps, reg=reg, q=q, m=m, s=s, ri=ri: e.matmul(
                                    ps[:, reg * 64:(reg + 1) * 64], lhsT=WB[m * 32:(m + 1) * 32, q, s, ri, :],
                                    rhs=u_t[m * 32:(m + 1) * 32, q, 0:512].rearrange("p (c s) -> p s c", s=8)[:, s, :], start=(s == 0), stop=(s == 7), tile_position=(m * 32, 0)), [WB, u_t], [ps])
                chk(2.03)
                op("dve", lambda e: e.tensor_copy(out=hp[:, 0, :, 0:1], in_=CARR[:].unsqueeze(2)), [CARR], [hp] + [TB(None, b_) for b_ in hpB])
                op("dve", lambda e: e.tensor_copy(out=hp[:, 1, :, 0:1], in_=CARI[:].unsqueeze(2)), [CARI], [hp] + [TB(None, b_) for b_ in hpB])

                def a3(t):
                    return t[:, 0:256].rearrange("p (q c) -> p q c", c=64)

                def stm(X, m):
                    return X[:].rearrange("p (q m) -> p m q", m=4)[:, m, :]

                a, b, c, d = mt[0], mt[1], mt[2], mt[3]
                k1, k2, k3, k4 = small[0], small[1], small[2], small[3]
                for m in range(4):
                    V = vps[m]
                    Vv = V[:].rearrange("p (q r c) -> p q r c", r=2, c=64)
                    VR = Vv[:, :, 0, :]; VI = Vv[:, :, 1, :]
                    ur = UTR[:, m * 4:(m + 1) * 4, :]; ui = UTI[:, m * 4:(m + 1) * 4, :]
                    bR = bRm[m]; bI = bIm[m]
                    op("dve", lambda e, VR=VR, ur=ur: e.tensor_tensor(out=a3(a), in0=VR, in1=ur, op=ALU.mult), [V, UTR], [a])
                    op("dve", lambda e, VI=VI, ui=ui: e.tensor_tensor(out=a3(b), in0=VI, in1=ui, op=ALU.mult), [V, UTI], [b])
                    op("dve", lambda e, VI=VI, ur=ur: e.tensor_tensor(out=a3(c), in0=VI, in1=ur, op=ALU.mult), [V, UTR], [c])
                    op("dve", lambda e, VR=VR, ui=ui: e.tensor_tensor(out=a3(d), in0=VR, in1=ui, op=ALU.mult), [V, UTI], [d])
                    op("dve", lambda e, bR=bR: e.tensor_tensor(out=bR[:, 0:256], in0=a[:, 0:256], in1=b[:, 0:256], op=ALU.add), [a, b], [bR])
                    op("dve", lambda e, bI=bI: e.tensor_tensor(out=bI[:, 0:256], in0=c[:, 0:256], in1=d[:, 0:256], op=ALU.subtract), [c, d], [bI])
                for m in range(4):
                    ur = UTR[:, m * 4:(m + 1) * 4, :]; ui = UTI[:, m * 4:(m + 1) * 4, :]
                    rt = RT[:, m * 4:(m + 1) * 4, :].rearrange("p q c -> p (q c)")
                    bR = bRm[m]; bI = bIm[m]
                    op("dve", lambda e, m=m: e.tensor_tensor(out=k1[:, 0:4], in0=stm(CARR, m), in1=stm(R8, m), op=ALU.mult), [CARR, R8], [k1])
                    op("dve", lambda e, m=m: e.tensor_tensor(out=k2[:, 0:4], in0=stm(CARI, m), in1=stm(R8, m), op=ALU.mult), [CARI, R8], [k2])
                    op("dve", lambda e, bR=bR: e.tensor_tensor(out=a3(bR)[:, :, 0:1], in0=a3(bR)[:, :, 0:1], in1=k1[:, 0:4].unsqueeze(2), op=ALU.add), [bR, k1], [bR])
                    op("dve", lambda e, bI=bI: e.tensor_tensor(out=a3(bI)[:, :, 0:1], in0=a3(bI)[:, :, 0:1], in1=k2[:, 0:4].unsqueeze(2), op=ALU.add), [bI, k2], [bI])
                    op("dve", lambda e, rt=rt, bR=bR: e.tensor_tensor_scan(out=GR[:, 0:256], data0=rt, data1=bR[:, 0:256], initial=0.0, op0=ALU.mult, op1=ALU.add), [bR, RT], [GR])
                    op("dve", lambda e, rt=rt, bI=bI: e.tensor_tensor_scan(out=GI[:, 0:256], data0=rt, data1=bI[:, 0:256], initial=0.0, op0=ALU.mult, op1=ALU.add), [bI, RT], [GI])
                    if is_own:
                        hpr = hp[:, 0, :, :].rearrange("p (q m) c -> p m q c", m=4)[:, m, :, 1:65]
                        hpi = hp[:, 1, :, :].rearrange("p (q m) c -> p m q c", m=4)[:, m, :, 1:65]
                        op("dve", lambda e, ur=ur: e.tensor_tensor(out=a3(a), in0=a3(GR), in1=ur, op=ALU.mult), [GR, UTR], [a])
                        op("dve", lambda e, ui=ui: e.tensor_tensor(out=a3(b), in0=a3(GI), in1=ui, op=ALU.mult), [GI, UTI], [b])
                        op("dve", lambda e, hpr=hpr: e.tensor_tensor(out=hpr, in0=a3(a), in1=a3(b), op=ALU.subtract), [a, b], [TB(None, hpB[m])])
                        op("dve", lambda e, ur=ur: e.tensor_tensor(out=a3(c), in0=a3(GI), in1=ur, op=ALU.mult), [GI, UTR], [c])
                        op("dve", lambda e, ui=ui: e.tensor_tensor(out=a3(d), in0=a3(GR), in1=ui, op=ALU.mult), [GR, UTI], [d])
                        op("dve", lambda e, hpi=hpi: e.tensor_tensor(out=hpi, in0=a3(c), in1=a3(d), op=ALU.add), [c, d], [TB(None, hpB[m])])
                    g63r = a3(GR)[:, :, 63]; g63i = a3(GI)[:, :, 63]
                    u63r = UTR[:, m * 4:(m + 1) * 4, 63]; u63i = UTI[:, m * 4:(m + 1) * 4, 63]
                    op("dve", lambda e, g63r=g63r, u63r=u63r: e.tensor_tensor(out=k1[:, 0:4], in0=g63r, in1=u63r, op=ALU.mult), [GR, UTR], [k1])
                    op("dve", lambda e, g63i=g63i, u63i=u63i: e.tensor_tensor(out=k2[:, 0:4], in0=g63i, in1=u63i, op=ALU.mult), [GI, UTI], [k2])
                    op("dve", lambda e, g63i=g63i, u63r=u63r: e.tensor_tensor(out=k3[:, 0:4], in0=g63i, in1=u63r, op=ALU.mult), [GI, UTR], [k3])
                    op("dve", lambda e, g63r=g63r, u63i=u63i: e.tensor_tensor(out=k4[:, 0:4], in0=g63r, in1=u63i, op=ALU.mult), [GR, UTI], [k4])
                    op("dve", lambda e, m=m: e.tensor_tensor(out=stm(CARR, m), in0=k1[:, 0:4], in1=k2[:, 0:4], op=ALU.subtract), [k1, k2, hp], [CARR])
                    op("dve", lambda e, m=m: e.tensor_tensor(out=stm(CARI, m), in0=k3[:, 0:4], in1=k4[:, 0:4], op=ALU.add), [k3, k4, hp], [CARI])
                chk(2.04)
                if not is_own:
                    return xT
                for q in range(4):
                    ps = nps()
                    for tau in range(8):
                        for dl in range(tau + 1):
                            op("pe", lambda e, ps=ps, q=q, tau=tau, dl=dl: e.matmul(
                                ps[:, tau * 64:(tau + 1) * 64], lhsT=KT[:, q, dl, :], rhs=u_t[:, q, 0:512].rearrange("p (c s) -> p s c", s=8)[:, tau - dl, :], start=(dl == 0), stop=(dl == tau)), [KT, u_t], [ps])
                    op("act", lambda e, ps=ps, q=q: e.activation(out=yT[:, q, 0:512].rearrange("p (c t) -> p t c", t=8), in_=ps[:].rearrange("p (t c) -> p t c", c=64), func=AF.Copy), [ps], [yT])
                pds = [nps() for _ in range(4)]
                for m in range(4):
                    for q in range(4):
                        ps = pds[q]
                        for tau in range(8):
                            for ri in range(2):
                                op("pe", lambda e, ps=ps, q=q, tau=tau, m=m, ri=ri: e.matmul(
                                    ps[m * 32:(m + 1) * 32, tau * 64:(tau + 1) * 64], lhsT=WD[:, tau, q * 4 + m, ri, :], rhs=hp[:, ri, q * 4 + m, 0:64],
                                    start=(ri == 0), stop=(ri == 1), tile_position=(0, m * 32)), [WD, TB(None, hpB[m])], [ps])
                for q in range(4):
                    ps = pds[q]
                    op("dve", lambda e, ps=ps, q=q: e.tensor_tensor(out=yT[:, q, 0:512].rearrange("p (c t) -> p t c", t=8), in0=ps[:].rearrange("p (t c) -> p t c", c=64), in1=yT[:, q, 0:512].rearrange("p (c t) -> p t c", t=8), op=ALU.add), [ps, yT], [yT])
                glu(512, tile_i * 512)
                return xT

            def glu(ncols, col0):
                op("act", lambda e: e.activation(out=zT[:, :, 0:ncols], in_=yT[:, :, 0:ncols], func=AF.Gelu), [yT], [zT])
                op("dve", lambda e: e.tensor_copy(out=zTb[:, :, 0:ncols], in_=zT[:, :, 0:ncols]), [zT], [zTb])
                for ct in range(4):
                    ps = nps()
                    for kc in range(4):
                        op("pe", lambda e, ps=ps, kc=kc, ct=ct: e.matmul(ps[:, 0:ncols], lhsT=wglu_t[:, kc, ct * 128:(ct + 1) * 128], rhs=zTb[:, kc, 0:ncols], start=(kc == 0), stop=(kc == 3)), [wglu_t, zTb], [ps])
                    op("act", lambda e, ps=ps, ct=ct: e.activation(out=sg[:, 0:ncols], in_=ps[:, 0:ncols], func=AF.Sigmoid, bias=bglt[:, ct:ct + 1]), [ps, bglt], [sg])
                    op("dve", lambda e, ct=ct: e.tensor_tensor(out=ssmT[:, ct, col0:col0 + ncols], in0=zT[:, ct, 0:ncols], in1=sg[:, 0:ncols], op=ALU.mult), [zT, sg], [ssmT])

            chk(0)
            for dg in range(2):
                precast(w_in[:, 1280 + dg * 512:1280 + (dg + 1) * 512], 8, 512, "g1_%d" % dg)
                precast(w_in[:, 2304 + dg * 512:2304 + (dg + 1) * 512], 8, 512, "g2_%d" % dg)
                precast(w_ba[:, dg * 512:(dg + 1) * 512], 4, 512, "br_%d" % dg, 0, 4096)
                precast(w_bs[:, dg * 512:(dg + 1) * 512], 4, 512, "br_%d" % dg, 2048, 4096)
            for hf_ in range(2):
                precast(w_out[:, hf_ * 512:(hf_ + 1) * 512], 8, 512, "wo_%d" % hf_)
            for fg in range(6):
                nf = 4 if fg < 5 else 2
                precast(w_fg[:, fg * 512:fg * 512 + nf * 128], 8, nf * 128, "fg_%d" % fg)
                precast(w_fu[:, fg * 512:fg * 512 + nf * 128], 8, nf * 128, "fu_%d" % fg)
            for hf_ in range(2):
                for fgp in range(3):
                    nk = 8 if fgp < 2 else 6
                    precast(w_fd[fgp * 1024:fgp * 1024 + nk * 128, hf_ * 512:(hf_ + 1) * 512], nk, 512, "fd_%d_%d" % (hf_, fgp))
            def finish(g):
                for _ in g:
                    pass

            tiles_ = [(xp[ti * 512:(ti + 1) * 512, :], False, ti) for ti in range(4)] + [(xo[ti * 512:(ti + 1) * 512, :], True, ti) for ti in range(4)]
            gens = []
            for k_ in range(4):
                g_ = ssm_tile(tiles_[k_][0], 4, tiles_[k_][1], tiles_[k_][2])
                xT_last = next(g_)
                if gens:
                    finish(gens[-1])
                gens.append(g_)
            def kv_block(xT, c0, rows, ridx, blk, kout=None, vout=None, kcol=None):
                psk = nps()
                for kc in range(8):
                    op("pe", lambda e, kc=kc: e.matmul(psk[0:rows, 0:256], lhsT=xT[:, kc, c0:c0 + rows], rhs=wqk[:, kc, 512:768], start=(kc == 0), stop=(kc == 7)), [xT, wqk], [psk])
                kf = kvf[blk % 2]
                rope_kv(psk, rows, ridx, kf)
                return kf

            kf = kv_block(xT_last, 384, 128, 16, 0)
            qkt = qk_tok[0]
            op("act", lambda e: e.activation(out=qkt[:, 512:640], in_=kf[:, 0:128], func=AF.Copy), [kf], [qkt])
            op("act", lambda e: e.activation(out=v_all[:, 0, :], in_=kf[:, 128:256], func=AF.Copy), [kf], [v_all])
            pt = npt()
            op("pe", lambda e: e.transpose(out=pt[:, 0:128], in_=qkt[:, 512:640], identity=identb[:]), [qkt, identb], [pt])
            op("act", lambda e: e.activation(out=kT_all[:, 0:128], in_=pt[:, 0:128], func=AF.Copy), [pt], [kT_all])
            chk(3)
            for k_ in range(4, 8):
                g_ = ssm_tile(tiles_[k_][0], 4, tiles_[k_][1], tiles_[k_][2])
                next(g_)
                finish(gens[-1])
                gens.append(g_)
            finish(gens[-1])
            op("sp", lambda e: e.dma_start(out=hre_o, in_=CARR[:]), [CARR], [], chan="o_hr")
            op("sp", lambda e: e.dma_start(out=him_o, in_=CARI[:]), [CARI], [], chan="o_hi")
            finish(ssm_tile(xs, 1, True, 0, sample=True))

            P.fence()
            apos["o"] = 0

            for i_ in range(4):
                WRING[i_] = carve([128, 4096], BF16)
            x1 = carve([128, 5, D])
            xn_tok = [carve([128, D], BF16) for _ in range(2)]
            junk = carve([128, D], BF16); ssv = [carve([128, 1]) for _ in range(4)]
            actT = carve([128, 8, 528], BF16)
            h2T = carve([128, 8, 528], BF16)
            xs_f = [carve([128, D]) for _ in range(2)]
            hB = [Buf() for _ in range(5)]
            qT = carve([128, 4, 528], BF16)
            attnT = carve([128, 4, 528], BF16)
            mergedT = carve([128, 8, 528], BF16)
            alias_o = apos["o"]
            hT = carve([128, NFT, 528], BF16)
            alias_end = apos["o"]
            qk_tok = [carve([128, 640], BF16) for _ in range(2)]
            kvf = [carve([128, 256]) for _ in range(2)]
            rtmp = [carve([128, 10, 8]) for _ in range(4)]
            pexp = [carve([128, 512], BF16) for _ in range(4)]
            qf = carve([128, 512])
            denr = carve([128, 512])
            sg1 = carve([128, 528]); sg2 = carve([128, 528]); mtmp = carve([128, 528]); mtmp2 = carve([128, 528])
            silu_t = [carve([128, 528])] * 2
            yout = [carve([128, D])] * 2
            keep_o = apos["o"]
            apos["o"] = alias_o
            KSb = carve([128, NS, 128], BF16); VSb = carve([128, NS, 128], BF16); KST = carve([128, NS, 128], BF16)
            assert apos["o"] <= alias_end
            apos["o"] = keep_o
            pes = carve([128, 2, NS, 4], BF16)
            cnt = {"x": 0}
            x1B = [Buf() for _ in range(5)]; aTB = [Buf() for _ in range(5)]; qTB = [Buf() for _ in range(5)]
            atB = [Buf() for _ in range(5)]; mgB = [Buf() for _ in range(8)]; hTB = [Buf() for _ in range(NFT)]
            kTB = [Buf() for _ in range(18)]; vB = [Buf() for _ in range(18)]
            bkw = TB(None); bvw = TB(None)

            def Dp(bufs):
                return [TB(None, b) for b in bufs]

            def cgB(lst, lc, n):
                return Dp([lst[k] for k in range(5) if k * 128 < lc + n and k * 128 + (128 if k < 4 else 16) > lc])

            HTALL = Dp(hTB)

            def tile_main(ti):
                has_s = (ti == 3)
                subs = [(ti * 512 + j * 128, 128, j, ti * 4 + j) for j in range(4)]
                if has_s:
                    subs.append((TOK, 16, 4, 17))
                cgs = [(ti * 512, 512, 0)] + ([(TOK, 16, 512)] if has_s else [])
                for (g0, rows, j, ridx) in subs:
                    cnt["x"] += 1
                    src = xo[g0:g0 + rows, :] if g0 < TOK else xs
                    xf = xs_f[cnt["x"] % 2]
                    load(TB(xf[0:rows, :], xf.b), src)
                    rmsnorm_to_T(xf, rows, g1t, TB(actT.t, aTB[j]), j * 128, xn_tok[cnt["x"] % 2], ssv[cnt["x"] % 4], junk)
                chk(5.1)
                for (g0, rows, j, ridx) in subs:
                    cnt["x"] += 1
                    lc = j * 128
                    psq = nps(); psk = nps()
                    for kc in range(8):
                        op("pe", lambda e, kc=kc, psq=psq, lc=lc, rows=rows: e.matmul(psq[0:rows, :], lhsT=actT[:, kc, lc:lc + rows], rhs=wqk[:, kc, 0:512], start=(kc == 0), stop=(kc == 7)), [TB(None, aTB[j]), wqk], [psq])
                    for kc in range(8):
                        op("pe", lambda e, kc=kc, psk=psk, lc=lc, rows=rows: e.matmul(psk[0:rows, 0:256], lhsT=actT[:, kc, lc:lc + rows], rhs=wqk[:, kc, 512:768], start=(kc == 0), stop=(kc == 7)), [TB(None, aTB[j]), wqk], [psk])
                    chk(5.11)
                    qkt = qk_tok[cnt["x"] % 2]; kf = kvf[cnt["x"] % 2]
                    op("act", lambda e, psq=psq, rows=rows: e.activation(out=qf[0:rows, :], in_=psq[0:rows, :], func=AF.Copy), [psq], [qf])
                    op("dve", lambda e, qkt=qkt, rows=rows: e.tensor_copy(out=qkt[0:rows, 0:512], in_=qf[0:rows, :]), [qf], [qkt])
                    chk(5.115)
                    rope_apply(qf[0:rows, :].rearrange("p (h d) -> p h d", d=64), qkt[0:rows, 0:512].rearrange("p (h d) -> p h d", d=64), rows, ridx, 8, [qf], qkt)
                    chk(5.12)
                    rope_kv(psk, rows, ridx, kf)
                    chk(5.13)
                    op("act", lambda e, qkt=qkt, kf=kf, rows=rows: e.activation(out=qkt[0:rows, 512:640], in_=kf[0:rows, 0:128], func=AF.Copy), [kf], [qkt])
                    if g0 < TOK:
                        blk = g0 // 128 + 1
                        op("act", lambda e, kf=kf, blk=blk: e.activation(out=v_all[:, blk, :], in_=kf[:, 128:256], func=AF.Copy), [kf], [TB(None, vB[blk])])
                        if g0 == TOK - 128:
                            op("sp", lambda e, kf=kf: e.dma_start(out=kwp_o, in_=kf[:, 0:128]), [kf], [], chan="o_kw")
                            op("sp", lambda e, kf=kf: e.dma_start(out=vwp_o, in_=kf[:, 128:256]), [kf], [], chan="o_vw")
                    else:
                        op("sp", lambda e, kf=kf: e.dma_start(out=kws_o[:, 127, :], in_=kf[0:NS, 0:128]), [kf], [bkw], chan="o_ks")
                        op("sp", lambda e, kf=kf: e.dma_start(out=vws_o[:, 127, :], in_=kf[0:NS, 128:256]), [kf], [bvw], chan="o_vs")
                        op("sp", lambda e: e.dma_start(out=kws_o[:, 0:127, :], in_=kwin[:, 1:128, :]), [], [], chan="o_ks2")
                        op("sp", lambda e: e.dma_start(out=vws_o[:, 0:127, :], in_=vwin[:, 1:128, :]), [], [], chan="o_vs2")
                        op("pool", lambda e: e.dma_start(out=KSb[0:112, :, :], in_=kwin[:, 1:113, :].rearrange("b j e -> j b e")), [], [KSb] + HTALL, chan="l_ks")
                        op("pool", lambda e: e.dma_start(out=VSb[0:112, :, :], in_=vwin[:, 1:113, :].rearrange("b j e -> j b e")), [], [VSb] + HTALL, chan="l_vs")
                        op("pool", lambda e: e.dma_start(out=KSb[112:127, :, :], in_=kwin[:, 113:128, :].rearrange("b j e -> j b e")), [KSb], [KSb], chan="l_ks")
                        op("pool", lambda e: e.dma_start(out=VSb[112:127, :, :], in_=vwin[:, 113:128, :].rearrange("b j e -> j b e")), [VSb], [VSb], chan="l_vs")
                        op("pool", lambda e: e.dma_start(out=KSb[127:128, :, :], in_=kws_o[:, 127:128, :].rearrange("b j e -> j b e")), [KSb, bkw], [KSb], chan="l_ks")
                        op("pool", lambda e: e.dma_start(out=VSb[127:128, :, :], in_=vws_o[:, 127:128, :].rearrange("b j e -> j b e")), [VSb, bvw], [VSb], chan="l_vs")
                    chk(5.14)
                    pt = npt()
                    for t in range(5):
                        op("pe", lambda e, t=t, pt=pt, qkt=qkt, rows=rows: e.transpose(out=pt[:, t * 128:t * 128 + rows], in_=qkt[0:rows, t * 128:(t + 1) * 128], identity=identb[0:rows, 0:rows]), [qkt, identb], [pt])
                    op("act", lambda e, pt=pt, lc=lc, rows=rows: e.activation(out=qT[:, 0:4, lc:lc + rows], in_=pt[:, 0:512].rearrange("p (k t) -> p k t", t=128)[:, :, 0:rows], func=AF.Copy), [pt], [TB(None, qTB[j])])
                    if g0 < TOK:
                        op("act", lambda e, pt=pt, g0=g0: e.activation(out=kT_all[:, 128 + g0:256 + g0], in_=pt[:, 512:640], func=AF.Copy), [pt], [TB(None, kTB[g0 // 128 + 1])])
                chk(5.2)
                for (g0, rows, j, ridx) in subs:
                    if g0 >= TOK:
                        continue
                    lc = j * 128
                    blk = g0 // 128 + 1
                    pe_t = []
                    for which in range(2):
                        for kv in range(2):
                            kb = blk - which
                            ps = nps()
                            op("pe", lambda e, ps=ps, kv=kv, kb=kb, lc=lc: e.matmul(
                                ps[:].rearrange("p (h q) -> p h q", q=128), lhsT=kT_all[kv * 64:(kv + 1) * 64, kb * 128:(kb + 1) * 128],
                                rhs=qT[kv * 64:(kv + 1) * 64, 0:4, lc:lc + 128], start=True, stop=True, tile_position=(kv * 64, 0)), [TB(None, kTB[kb]), TB(None, qTB[j])], [ps])
                            pe_ = pexp[kv * 2 + which]
                            op("act", lambda e, ps=ps, pe_=pe_: e.activation(out=pe_[:], in_=ps[:], func=AF.Exp, scale=0.125), [ps], [pe_])
                            mk = mko if which == 0 else (mkp0 if blk == 1 else mkp)
                            op("dve", lambda e, pe_=pe_, mk=mk: e.tensor_tensor(out=pe_[:], in0=pe_[:], in1=mk[:], op=ALU.mult), [pe_, mk], [pe_])
                            pe_t.append((kv, kb, pe_))
                    psn = nps(); psd = nps()
                    pe_t.sort(key=lambda t: (t[0], -t[1]))
                    for idx, (kv, kb, pe_) in enumerate(pe_t):
                        first = (idx % 2 == 0); lastk = (idx % 2 == 1)
                        op("pe", lambda e, kv=kv, kb=kb, pe_=pe_, first=first, lastk=lastk, psn=psn: e.matmul(
                            psn[kv * 64:(kv + 1) * 64, :], lhsT=v_all[:, kb, kv * 64:(kv + 1) * 64], rhs=pe_[:], start=first, stop=lastk, tile_position=(0, kv * 64)), [TB(None, vB[kb]), pe_], [psn])
                        op("pe", lambda e, kv=kv, pe_=pe_, first=first, lastk=lastk, psd=psd: e.matmul(
                            psd[kv * 64:(kv + 1) * 64, :], lhsT=ones_bf[:, 0:64], rhs=pe_[:], start=first, stop=lastk, tile_position=(0, kv * 64)), [ones_bf, pe_], [psd])
                    op("dve", lambda e, psd=psd: e.tensor_tensor(out=denr[:].rearrange("p (h q) -> p h q", q=128), in0=psd[:].rearrange("p (h q) -> p h q", q=128), in1=esk[:].unsqueeze(2).broadcast_to([128, 4, 128]), op=ALU.add), [psd, esk], [denr])
                    op("dve", lambda e: e.reciprocal(out=denr[:], in_=denr[:]), [denr], [denr])
                    op("dve", lambda e, psn=psn, lc=lc: e.tensor_tensor(out=attnT[:, :, lc:lc + 128], in0=psn[:].rearrange("p (h q) -> p h q", q=128), in1=denr[:].rearrange("p (h q) -> p h q", q=128), op=ALU.mult), [psn, denr], [TB(None, atB[j])])
                chk(5.3)
                if has_s:
                    for hb in range(2):
                        pt = npt()
                        for bb in range(8):
                            b_ = hb * 8 + bb
                            op("pe", lambda e, pt=pt, bb=bb, b_=b_: e.transpose(out=pt[:, bb * 128:(bb + 1) * 128], in_=KSb[:, b_, :], identity=identb[:]), [KSb, identb], [pt])
                        op("act", lambda e, pt=pt, hb=hb: e.activation(out=KST[:, hb * 8:(hb + 1) * 8, :].rearrange("p b k -> p (b k)"), in_=pt[:], func=AF.Copy), [pt], [KST] + HTALL)
                    for kv in range(2):
                        ps = nps()
                        for b_ in range(NS):
                            op("pe", lambda e, ps=ps, kv=kv, b_=b_: e.matmul(
                                ps[:, b_ * 4:(b_ + 1) * 4], lhsT=KST[kv * 64:(kv + 1) * 64, b_, :],
                                rhs=qT[kv * 64:(kv + 1) * 64, 0:4, 512 + b_], start=True, stop=True, tile_position=(kv * 64, 0)), [KST, TB(None, qTB[4])] + HTALL, [ps])
                        op("act", lambda e, ps=ps, kv=kv: e.activation(out=pes[:, kv, :, :].rearrange("p b c -> p (b c)"), in_=ps[:, 0:64], func=AF.Exp, scale=0.125), [ps], [pes])
                    psn = nps(); psd = nps()
                    for kv in range(2):
                        for b_ in range(NS):
                            op("pe", lambda e, kv=kv, b_=b_, psn=psn: e.matmul(psn[kv * 64:(kv + 1) * 64, b_ * 4:(b_ + 1) * 4], lhsT=VSb[:, b_, kv * 64:(kv + 1) * 64], rhs=pes[:, kv, b_, :], start=True, stop=True, tile_position=(0, kv * 64)), [VSb, pes] + HTALL, [psn])
                            op("pe", lambda e, kv=kv, b_=b_, psd=psd: e.matmul(psd[kv * 64:(kv + 1) * 64, b_ * 4:(b_ + 1) * 4], lhsT=ones_bf[:, 0:64], rhs=pes[:, kv, b_, :], start=True, stop=True, tile_position=(0, kv * 64)), [ones_bf, pes], [psd])
                    dv_ = denr[:, 0:64].rearrange("p (b h) -> p b h", h=4)
                    op("dve", lambda e, psd=psd: e.tensor_tensor(out=dv_, in0=psd[:, 0:64].rearrange("p (b h) -> p b h", h=4), in1=esk[:].unsqueeze(1).broadcast_to([128, NS, 4]), op=ALU.add), [psd, esk], [denr])
                    op("dve", lambda e: e.reciprocal(out=denr[:, 0:64], in_=denr[:, 0:64]), [denr], [denr])
                    op("dve", lambda e, psn=psn: e.tensor_tensor(out=attnT[:, :, 512:528].rearrange("p h b -> p b h"), in0=psn[:, 0:64].rearrange("p (b h) -> p b h", h=4), in1=dv_, op=ALU.mult), [psn, denr], [TB(None, atB[4])])
                yield "F1"
                for dg in range(2):
                    wg1 = wload(w_in[:, 1280 + dg * 512:1280 + (dg + 1) * 512], 8, 512, "g1_%d" % dg, ti)
                    wg2 = wload(w_in[:, 2304 + dg * 512:2304 + (dg + 1) * 512], 8, 512, "g2_%d" % dg, ti)
                    wr_i["i"] = (wr_i["i"] + 1) % 4
                    slot = WRING[wr_i["i"]]
                    vba = slot[:, 0:2048].rearrange("p (k c) -> p k c", c=512); vbs = slot[:, 2048:4096].rearrange("p (k c) -> p k c", c=512)
                    i_ = wr_i["i"]
                    scr_b, sbb_ = scratch("br_%d" % dg, 4096)
                    op("pool", lambda e, slot=slot, scr_b=scr_b: e.dma_start(out=slot[:, 0:4096], in_=scr_b), [sbb_], [slot], chan="wr%d" % i_)
                    wbr = TB(None, slot.b)
                    for dd in range(4):
                        dt_ = dg * 4 + dd
                        for (gc, n, lc) in cgs:
                            p1 = nps(); p2 = nps(); pa = nps(); pb = nps()
                            for kc in range(8):
                                op("pe", lambda e, kc=kc, p1=p1, dd=dd, lc=lc, n=n, wg1=wg1: e.matmul(p1[:, 0:n], lhsT=wg1[:, kc, dd * 128:(dd + 1) * 128], rhs=actT[:, kc, lc:lc + n], start=(kc == 0), stop=(kc == 7)), [wg1] + cgB(aTB, lc, n), [p1])
                            for kc in range(8):
                                op("pe", lambda e, kc=kc, p2=p2, dd=dd, lc=lc, n=n, wg2=wg2: e.matmul(p2[:, 0:n], lhsT=wg2[:, kc, dd * 128:(dd + 1) * 128], rhs=actT[:, kc, lc:lc + n], start=(kc == 0), stop=(kc == 7)), [wg2] + cgB(aTB, lc, n), [p2])
                            for kc in range(4):
                                op("pe", lambda e, kc=kc, pa=pa, dd=dd, lc=lc, n=n, vba=vba: e.matmul(pa[:, 0:n], lhsT=vba[:, kc, dd * 128:(dd + 1) * 128], rhs=attnT[:, kc, lc:lc + n], start=(kc == 0), stop=(kc == 3)), [wbr] + cgB(atB, lc, n), [pa])
                            for kc in range(4):
                                op("pe", lambda e, kc=kc, pb=pb, dd=dd, gc=gc, n=n, vbs=vbs: e.matmul(pb[:, 0:n], lhsT=vbs[:, kc, dd * 128:(dd + 1) * 128], rhs=ssmT[:, kc, gc:gc + n], start=(kc == 0), stop=(kc == 3)), [wbr, ssmT], [pb])
                            op("act", lambda e, p1=p1, dt_=dt_, n=n: e.activation(out=sg1[:, 0:n], in_=p1[:, 0:n], func=AF.Sigmoid, bias=bgt[:, dt_:dt_ + 1]), [p1, bgt], [sg1])
                            op("act", lambda e, p2=p2, dt_=dt_, n=n: e.activation(out=sg2[:, 0:n], in_=p2[:, 0:n], func=AF.Sigmoid, bias=bgt[:, 8 + dt_:9 + dt_]), [p2, bgt], [sg2])
                            op("dve", lambda e, pa=pa, n=n: e.tensor_tensor(out=mtmp[:, 0:n], in0=pa[:, 0:n], in1=sg1[:, 0:n], op=ALU.mult), [pa, sg1], [mtmp])
                            op("dve", lambda e, pb=pb, n=n: e.tensor_tensor(out=mtmp2[:, 0:n], in0=pb[:, 0:n], in1=sg2[:, 0:n], op=ALU.mult), [pb, sg2], [mtmp2])
                            op("dve", lambda e, dt_=dt_, lc=lc, n=n: e.tensor_tensor(out=mergedT[:, dt_, lc:lc + n], in0=mtmp[:, 0:n], in1=mtmp2[:, 0:n], op=ALU.add), [mtmp, mtmp2], [TB(None, mgB[dt_])])
                chk(5.4)
                yield "F2"
                wo = [wload(w_out[:, hf_ * 512:(hf_ + 1) * 512], 8, 512, "wo_%d" % hf_, ti) for hf_ in range(2)]
                for (g0, rows, j, ridx) in subs:
                    lc = j * 128
                    for hf_ in range(2):
                        ps = nps()
                        for kc in range(8):
                            op("pe", lambda e, kc=kc, ps=ps, lc=lc, rows=rows, hf_=hf_: e.matmul(ps[0:rows, :], lhsT=mergedT[:, kc, lc:lc + rows], rhs=wo[hf_][:, kc, :], start=(kc == 0), stop=(kc == 7)), [TB(None, mgB[kc]), wo[hf_]], [ps])
                        op("dve", lambda e, ps=ps, j=j, rows=rows, hf_=hf_: e.tensor_tensor(out=x1[0:rows, j, hf_ * 512:(hf_ + 1) * 512], in0=ps[0:rows, :], in1=x1[0:rows, j, hf_ * 512:(hf_ + 1) * 512], op=ALU.add), [ps, TB(None, x1B[j])], [TB(None, x1B[j])])
                for (g0, rows, j, ridx) in subs:
                    cnt["x"] += 1
                    rmsnorm_to_T(TB(x1[:, j, :], x1B[j]), rows, g2t, TB(h2T.t, hB[j]), j * 128, xn_tok[cnt["x"] % 2], ssv[cnt["x"] % 4], junk)
                chk(5.5)
                yield "B1"
                for fg in range(6):
                    nf = 4 if fg < 5 else 2
                    wg = wload(w_fg[:, fg * 512:fg * 512 + nf * 128], 8, nf * 128, "fg_%d" % fg, ti)
                    wu = wload(w_fu[:, fg * 512:fg * 512 + nf * 128], 8, nf * 128, "fu_%d" % fg, ti)
                    for ff in range(nf):
                        ft = fg * 4 + ff
                        for (gc, n, lc) in cgs:
                            pg = nps(); pu = nps()
                            for kc in range(8):
                                op("pe", lambda e, kc=kc, pg=pg, ff=ff, lc=lc, n=n, wg=wg: e.matmul(pg[:, 0:n], lhsT=wg[:, kc, ff * 128:(ff + 1) * 128], rhs=h2T[:, kc, lc:lc + n], start=(kc == 0), stop=(kc == 7)), [wg] + cgB(hB, lc, n), [pg])
                            for kc in range(8):
                                op("pe", lambda e, kc=kc, pu=pu, ff=ff, lc=lc, n=n, wu=wu: e.matmul(pu[:, 0:n], lhsT=wu[:, kc, ff * 128:(ff + 1) * 128], rhs=h2T[:, kc, lc:lc + n], start=(kc == 0), stop=(kc == 7)), [wu] + cgB(hB, lc, n), [pu])
                            sl = silu_t[ft % 2]
                            op("act", lambda e, pg=pg, sl=sl, n=n: e.activation(out=sl[:, 0:n], in_=pg[:, 0:n], func=AF.Silu), [pg], [sl])
                            op("dve", lambda e, pu=pu, sl=sl, ft=ft, lc=lc, n=n: e.tensor_tensor(out=hT[:, ft, lc:lc + n], in0=pu[:, 0:n], in1=sl[:, 0:n], op=ALU.mult), [pu, sl], [TB(None, hTB[ft])])
                chk(5.6)
                for hf_ in range(2):
                    for fgp in range(3):
                        nk = 8 if fgp < 2 else 6
                        wd = wload(w_fd[fgp * 1024:fgp * 1024 + nk * 128, hf_ * 512:(hf_ + 1) * 512], nk, 512, "fd_%d_%d" % (hf_, fgp), ti)
                        for si, (g0, rows, j, ridx) in enumerate(subs):
                            lc = j * 128
                            ps = nps()
                            for kc in range(nk):
                                ft = fgp * 8 + kc
                                op("pe", lambda e, ps=ps, kc=kc, ft=ft, lc=lc, rows=rows, wd=wd, nk=nk: e.matmul(ps[0:rows, :], lhsT=hT[:, ft, lc:lc + rows], rhs=wd[:, kc, :], start=(kc == 0), stop=(kc == nk - 1)), [TB(None, hTB[ft]), wd], [ps])
                            op("dve", lambda e, ps=ps, j=j, rows=rows, hf_=hf_: e.tensor_tensor(out=x1[0:rows, j, hf_ * 512:(hf_ + 1) * 512], in0=ps[0:rows, :], in1=x1[0:rows, j, hf_ * 512:(hf_ + 1) * 512], op=ALU.add), [ps, TB(None, x1B[j])], [TB(None, x1B[j])])
                for (g0, rows, j, ridx) in subs:
                    cnt["x"] += 1
                    ss = ssv[cnt["x"] % 4]; yo = yout[cnt["x"] % 2]
                    xt = TB(x1[:, j, :], x1B[j])
                    op("dve", lambda e, ss=ss, rows=rows: e.memset(ss[0:rows, :], 0.0), [], [ss])
                    op("act", lambda e, ss=ss, rows=rows, xt=xt: e.activation(out=junk[0:rows, :], in_=xt[0:rows, :], func=AF.Square, accum_out=ss[0:rows, :]), [xt, ss], [junk, ss])
                    op("act", lambda e, ss=ss, rows=rows: e.activation(out=ss[0:rows, :], in_=ss[0:rows, :], func=AF.Sqrt, scale=1.0 / D, bias=EPS), [ss], [ss])
                    op("dve", lambda e, ss=ss, rows=rows: e.reciprocal(out=ss[0:rows, :], in_=ss[0:rows, :]), [ss], [ss])
                    op("dve", lambda e, ss=ss, rows=rows, xt=xt, yo=yo: e.scalar_tensor_tensor(out=yo[0:rows, :], in0=xt[0:rows, :], scalar=ss[0:rows, 0:1], in1=gft[0:rows, :], op0=ALU.mult, op1=ALU.mult), [xt, ss, gft], [yo])
                    dst = y_o[g0:g0 + rows, :] if g0 < TOK else ys_o
                    op("sp", lambda e, yo=yo, rows=rows, dst=dst: e.dma_start(out=dst, in_=yo[0:rows, :]), [yo], [yo], chan="oy%d" % (cnt["x"] % 2))

            chk(5)
            def adv(g, tag):
                r = next(g, None)
                assert r == tag, (r, tag)

            def x1_load(ti):
                for j in range(4):
                    g0 = ti * 512 + j * 128
                    load(TB(x1[:, j, :], x1B[j]), xo[g0:g0 + 128, :])
                if ti == 3:
                    load(TB(x1[0:NS, 4, :], x1B[4]), xs)

            tg = [tile_main(ti) for ti in range(4)]
            adv(tg[0], "F1"); adv(tg[0], "F2")
            for ti in range(4):
                x1_load(ti)
                if ti < 3:
                    adv(tg[ti + 1], "F1")
                adv(tg[ti], "B1")
                if ti < 3:
                    adv(tg[ti + 1], "F2")
                adv(tg[ti], None)
                chk(6 + ti)
        try:
            body()
        except _Stop:
            pass
        P.emit(nc)
    return nc


_NC = {}


def make_inputs(x_prompt, x_sample, state_k_win, state_v_win, state_ssm_re, state_ssm_im,
           norm1_g, w_in, b_gate, attn_sinks, ssm_lam_re, ssm_lam_im, ssm_log_dt,
           ssm_b_re, ssm_b_im, ssm_c_re, ssm_c_im, ssm_d, w_glu, b_glu,
           w_branch_attn, w_branch_ssm, w_out, norm2_g, w_ffn_gate, w_ffn_up, w_ffn_down, norm_f_g):
    f32 = np.float32
    bf = ml_dtypes.bfloat16
    A = lambda a: np.ascontiguousarray(np.asarray(a), dtype=f32)
    x_prompt = A(x_prompt); x_sample = A(x_sample)
    perm = np.concatenate([np.r_[t * 64:(t + 1) * 64, (4 + t) * 64:(5 + t) * 64] for t in range(4)])
    w_in0 = A(w_in)[0]
    w_in_p = np.ascontiguousarray(np.concatenate([w_in0[:, :512][:, perm], w_in0[:, 512:]], axis=1))
    w_ba_p = np.ascontiguousarray(A(w_branch_attn)[0][perm, :])
    sinks = A(attn_sinks)[0]
    sink_l = np.ascontiguousarray(np.concatenate([np.tile(sinks[None, 0:4], (64, 1)), np.tile(sinks[None, 4:8], (64, 1))], axis=0))

    def st_layout(a):
        a = a.reshape((16, 2, 64) + a.shape[2:])
        return np.ascontiguousarray(np.moveaxis(a, 0, 2).reshape((128, 16) + a.shape[3:]))

    lam_re_l = st_layout(A(ssm_lam_re)[0]); lam_im_l = st_layout(A(ssm_lam_im)[0])
    logdt_l = st_layout(np.broadcast_to(A(ssm_log_dt)[0][:, None], (32, 64)))
    bre_l = st_layout(A(ssm_b_re)[0]); bim_l = st_layout(A(ssm_b_im)[0])
    cre_l = st_layout(np.ascontiguousarray(A(ssm_c_re)[0].transpose(0, 2, 1)))
    cim_l = st_layout(np.ascontiguousarray(A(ssm_c_im)[0].transpose(0, 2, 1)))
    p_idx = np.arange(128)
    mask_g2 = (p_idx[:, None] // 64 == np.arange(2)[None, :]).astype(f32)
    mask_bd = (p_idx[:, None] // 32 == p_idx[None, :] // 32).astype(f32)
    m_own = (p_idx[:, None] <= p_idx[None, :]).astype(f32)
    m_prev = (p_idx[:, None] > p_idx[None, :]).astype(f32)
    rep4 = lambda m: np.ascontiguousarray(np.tile(m, (1, 4))).astype(bf)
    inv_freq = (500000.0 ** (-(np.arange(8, dtype=f32) * 2.0 / 16))).astype(f32)
    bc128 = lambda v: np.ascontiguousarray(np.broadcast_to(A(v).reshape(1, -1), (128, 1024)))
    common = dict(
        g1b=bc128(norm1_g), g2b=bc128(norm2_g), gfb=bc128(norm_f_g), w_in=w_in_p,
        bgate_l=np.ascontiguousarray(A(b_gate)[0].reshape(16, 128).T), sink_l=sink_l,
        lam_re_l=lam_re_l, lam_im_l=lam_im_l, logdt_l=logdt_l, bre_l=bre_l, bim_l=bim_l, cre_l=cre_l, cim_l=cim_l,
        d_l=np.ascontiguousarray(A(ssm_d)[0].reshape(4, 128).T), w_glu=A(w_glu)[0],
        bglu_l=np.ascontiguousarray(A(b_glu)[0].reshape(4, 128).T),
        w_ba=w_ba_p, w_bs=A(w_branch_ssm)[0], w_out=A(w_out)[0], w_fg=A(w_ffn_gate)[0], w_fu=A(w_ffn_up)[0], w_fd=A(w_ffn_down)[0],
        ident_bf=np.eye(128, dtype=f32).astype(bf), ident_f=np.eye(128, dtype=f32), mask_g2=mask_g2, mask_bd=mask_bd,
        mk_own=rep4(m_own), mk_prev=rep4(m_prev),
    )
    skw = A(state_k_win)[0].reshape(128, 128, 128); svw = A(state_v_win)[0].reshape(128, 128, 128)
    sre = A(state_ssm_re)[0]; sim = A(state_ssm_im)[0]
    in_maps = []
    for c in range(8):
        b, hf = c // 2, c % 2
        pos = np.zeros((18, 128), dtype=f32)
        pos[:16] = hf * 2048 + np.arange(16)[:, None] * 128 + np.arange(128)[None, :]
        pos[16] = hf * 2048 - 128 + np.arange(128)
        pos[17] = 8192.0
        ang = pos[:, :, None] * inv_freq[None, None, :]
        m = dict(common)
        m.update(
            xo=np.ascontiguousarray(x_prompt[b, hf * 2048:(hf + 1) * 2048]),
            xp=np.ascontiguousarray(x_prompt[b, 0:2048]) if hf else np.zeros((2048, 1024), f32),
            xs=np.ascontiguousarray(x_sample[c * 16:(c + 1) * 16, 0]),
            kwin=np.ascontiguousarray(skw[c * 16:(c + 1) * 16]), vwin=np.ascontiguousarray(svw[c * 16:(c + 1) * 16]),
            sre_l=np.ascontiguousarray(np.moveaxis(st_layout(np.moveaxis(sre[c * 16:(c + 1) * 16], 0, 2)), 2, 2)),
            sim_l=np.ascontiguousarray(st_layout(np.moveaxis(sim[c * 16:(c + 1) * 16], 0, 2))),
            mk_prev0=rep4(m_prev * float(hf)),
            ropec=np.ascontiguousarray(np.cos(ang).astype(f32).transpose(1, 0, 2)),
            ropes=np.ascontiguousarray(np.sin(ang).astype(f32).transpose(1, 0, 2)),
        )
        in_maps.append(m)
    return in_maps


def kernel(**inputs):
    in_maps = make_inputs(**inputs)
    if "nc" not in _NC:
        _NC["nc"] = build()
    res = run_bass_kernel_spmd(_NC["nc"], in_maps, core_ids=list(range(8)))
    R = res.results
    f32 = np.float32

    def un_st(a):
        a = a.reshape((2, 64, 16) + a.shape[2:])
        return np.moveaxis(a, 2, 0).reshape((32, 64) + a.shape[3:])

    y_prompt = np.stack([np.concatenate([R[2 * b]["y"], R[2 * b + 1]["y"]], axis=0) for b in range(4)])
    y_sample = np.concatenate([R[c]["ys"] for c in range(8)], axis=0)[:, None, :]
    kwp = np.stack([R[2 * b + 1]["kwp"].reshape(128, 2, 64) for b in range(4)])[None]
    vwp = np.stack([R[2 * b + 1]["vwp"].reshape(128, 2, 64) for b in range(4)])[None]
    hre = np.stack([un_st(R[2 * b + 1]["hre"]) for b in range(4)])[None]
    him = np.stack([un_st(R[2 * b + 1]["him"]) for b in range(4)])[None]
    kws = np.concatenate([R[c]["kws"].reshape(16, 128, 2, 64) for c in range(8)], axis=0)[None]
    vws = np.concatenate([R[c]["vws"].reshape(16, 128, 2, 64) for c in range(8)], axis=0)[None]
    sres = np.concatenate([np.moveaxis(un_st(R[c]["sres"]), 2, 0) for c in range(8)], axis=0)[None]
    sims = np.concatenate([np.moveaxis(un_st(R[c]["sims"]), 2, 0) for c in range(8)], axis=0)[None]
    outs = (y_prompt, y_sample, kwp, vwp, hre, him, kws, vws, sres, sims)
    return tuple(np.ascontiguousarray(o, dtype=f32) for o in outs)
```
